# Optimizing a Trainium2 kernel written in Bass

```python
import jax, jax.numpy as jnp
from jax import lax
import numpy as np

D_MODEL = 1024
BATCH = 2
SEQ = 8192
DEPTH = 1

MEM_LEN = 256
FOX_HEADS = 8
FOX_HEAD_DIM = 64
DSA_HEADS = 8
DSA_HEAD_DIM = 64
IDX_HEADS = 8
IDX_DIM = 64
MEM_HEADS = 4
MEM_HEAD_DIM = 128
BRANCH_WIDTH = 512
N_BRANCHES = 3
D_FF = 2816
ROPE_THETA = 10000.0
Q_BLOCK = 128
TOPK_MAX = 256
EPS = 1e-6

SPLIT_SIZES = (
    BRANCH_WIDTH, BRANCH_WIDTH, BRANCH_WIDTH,
    FOX_HEADS,
    BRANCH_WIDTH, DSA_HEAD_DIM, DSA_HEAD_DIM,
    IDX_HEADS * IDX_DIM, IDX_DIM, IDX_HEADS,
    BRANCH_WIDTH,
    N_BRANCHES * D_MODEL,
)
SPLIT_POINTS = tuple(int(v) for v in np.cumsum(SPLIT_SIZES)[:-1])
D_IN = int(sum(SPLIT_SIZES))

kernel_name = "hybrid_fox_dsa_mem_macaron"


def rmsnorm(x, g):
    xf = x.astype(jnp.float32)
    y = xf * lax.rsqrt(jnp.mean(xf * xf, axis=-1, keepdims=True) + EPS)
    return (y * g.astype(jnp.float32)).astype(x.dtype)


def swiglu(x, w_gate, w_up, w_down):
    return (jax.nn.silu(x @ w_gate) * (x @ w_up)) @ w_down


def rope_tables(positions, dim, dtype):
    inv_freq = ROPE_THETA ** (-jnp.arange(0, dim, 2, dtype=jnp.float32) / dim)
    ang = positions.astype(jnp.float32)[..., None] * inv_freq
    return jnp.cos(ang)[:, :, None, :].astype(dtype), jnp.sin(ang)[:, :, None, :].astype(dtype)


def rope(x, cos, sin):
    x1, x2 = jnp.split(x, 2, axis=-1)
    return jnp.concatenate([x1 * cos - x2 * sin, x2 * cos + x1 * sin], axis=-1)


def to_blocks(a):
    B, S = a.shape[:2]
    return a.reshape((B, S // Q_BLOCK, Q_BLOCK) + a.shape[2:]).swapaxes(0, 1)


def from_blocks(a):
    nb, B, Q = a.shape[:3]
    return a.swapaxes(0, 1).reshape((B, nb * Q) + a.shape[3:])


def fox_attention(q, k, v, logf):
    B, S, H, hd = q.shape
    c = jnp.cumsum(logf, axis=1)
    c_keys = c.transpose(0, 2, 1)
    key_pos = jnp.arange(S)
    scale = hd ** -0.5

    def block(args):
        qi, ci, bi = args
        qpos = bi * Q_BLOCK + jnp.arange(Q_BLOCK)
        s = jnp.einsum('bqhd,bkhd->bhqk', qi, k).astype(jnp.float32) * scale
        s = s + ci.transpose(0, 2, 1)[..., None] - c_keys[:, :, None, :]
        s = jnp.where(key_pos[None, :] <= qpos[:, None], s, -jnp.inf)
        p = jax.nn.softmax(s, axis=-1)
        return jnp.einsum('bhqk,bkhd->bqhd', p.astype(v.dtype), v)

    out = lax.map(block, (to_blocks(q), to_blocks(c), jnp.arange(S // Q_BLOCK)))
    return from_blocks(out).reshape(B, S, H * hd)


def dsa_attention(q, k, v, iq, ik, iw, k_sel):
    B, S, H, hd = q.shape
    key_pos = jnp.arange(S)
    scale = hd ** -0.5
    gather = jax.vmap(lambda a, i: a[i])

    def block(args):
        qi, iqi, iwi, bi = args
        qpos = bi * Q_BLOCK + jnp.arange(Q_BLOCK)
        sc = jnp.einsum('bqhd,bkd->bqhk', iqi, ik).astype(jnp.float32)
        score = jnp.einsum('bqh,bqhk->bqk', iwi.astype(jnp.float32), jax.nn.relu(sc))
        score = jnp.where((key_pos[None, :] <= qpos[:, None])[None], score, -jnp.inf)
        _, idx = lax.top_k(score, k_sel)
        valid = idx <= qpos[None, :, None]
        ks = gather(k, idx)
        vs = gather(v, idx)
        s = jnp.einsum('bqhd,bqkd->bqhk', qi, ks).astype(jnp.float32) * scale
        s = jnp.where(valid[:, :, None, :], s, -jnp.inf)
        p = jax.nn.softmax(s, axis=-1)
        return jnp.einsum('bqhk,bqkd->bqhd', p.astype(vs.dtype), vs)

    out = lax.map(block, (to_blocks(q), to_blocks(iq), to_blocks(iw), jnp.arange(S // Q_BLOCK)))
    return from_blocks(out).reshape(B, S, H * hd)


def memory_attention(q, km, vm):
    B, S, H, hd = q.shape
    s = jnp.einsum('bshd,bmhd->bhsm', q, km).astype(jnp.float32) * hd ** -0.5
    p = jax.nn.softmax(s, axis=-1)
    return jnp.einsum('bhsm,bmhd->bshd', p.astype(vm.dtype), vm).reshape(B, S, H * hd)


def setup_inputs(seed: int = 0) -> dict:
    key = jax.random.key(seed)
    ks = jax.random.split(key, 24)
    f32 = jnp.float32

    def w(k, shape, fan_in):
        return jax.random.normal(k, shape, f32) * fan_in ** -0.5

    def gain(k, shape):
        return 1.0 + 0.02 * jax.random.normal(k, shape, f32)

    L = DEPTH
    return {
        "x": jax.random.normal(ks[0], (BATCH, SEQ, D_MODEL), f32),
        "mem": jax.random.normal(ks[1], (BATCH, MEM_LEN, D_MODEL), f32),
        "positions": jnp.broadcast_to(jnp.arange(SEQ, dtype=jnp.int32), (BATCH, SEQ)),
        "ffn1_norm": gain(ks[2], (L, D_MODEL)),
        "ffn1_w_gate": w(ks[3], (L, D_MODEL, D_FF), D_MODEL),
        "ffn1_w_up": w(ks[4], (L, D_MODEL, D_FF), D_MODEL),
        "ffn1_w_down": w(ks[5], (L, D_FF, D_MODEL), D_FF),
        "mix_norm": gain(ks[6], (L, D_MODEL)),
        "mem_norm": gain(ks[7], (L, D_MODEL)),
        "w_in": w(ks[8], (L, D_MODEL, D_IN), D_MODEL),
        "b_forget": 3.0 + 0.5 * jax.random.normal(ks[9], (L, FOX_HEADS), f32),
        "fox_q_norm": gain(ks[10], (L, FOX_HEAD_DIM)),
        "fox_k_norm": gain(ks[11], (L, FOX_HEAD_DIM)),
        "dsa_q_norm": gain(ks[12], (L, DSA_HEAD_DIM)),
        "dsa_k_norm": gain(ks[13], (L, DSA_HEAD_DIM)),
        "mem_q_norm": gain(ks[14], (L, MEM_HEAD_DIM)),
        "mem_k_norm": gain(ks[15], (L, MEM_HEAD_DIM)),
        "w_mem_kv": w(ks[16], (L, D_MODEL, 2 * BRANCH_WIDTH), D_MODEL),
        "w_branch": w(ks[17], (L, N_BRANCHES, BRANCH_WIDTH, D_MODEL), BRANCH_WIDTH),
        "w_out": w(ks[18], (L, D_MODEL, D_MODEL), D_MODEL),
        "ffn2_norm": gain(ks[19], (L, D_MODEL)),
        "ffn2_w_gate": w(ks[20], (L, D_MODEL, D_FF), D_MODEL),
        "ffn2_w_up": w(ks[21], (L, D_MODEL, D_FF), D_MODEL),
        "ffn2_w_down": w(ks[22], (L, D_FF, D_MODEL), D_FF),
    }


def reference(x, mem, positions, ffn1_norm, ffn1_w_gate, ffn1_w_up, ffn1_w_down, mix_norm,
              mem_norm, w_in, b_forget, fox_q_norm, fox_k_norm, dsa_q_norm, dsa_k_norm,
              mem_q_norm, mem_k_norm, w_mem_kv, w_branch, w_out, ffn2_norm, ffn2_w_gate,
              ffn2_w_up, ffn2_w_down):
    B, S, D = x.shape
    M = mem.shape[1]
    k_sel = min(TOPK_MAX, S // 4)
    cos, sin = rope_tables(positions, DSA_HEAD_DIM, x.dtype)
    cos_i, sin_i = rope_tables(positions, IDX_DIM, x.dtype)
    h = x
    for l in range(DEPTH):
        h = h + 0.5 * swiglu(rmsnorm(h, ffn1_norm[l]), ffn1_w_gate[l], ffn1_w_up[l], ffn1_w_down[l])

        u = rmsnorm(h, mix_norm[l])
        (fq, fk, fv, ff, dq, dk, dv, iq, ik, iw, mq, g) = jnp.split(u @ w_in[l], SPLIT_POINTS, axis=-1)

        fq = rmsnorm(fq.reshape(B, S, FOX_HEADS, FOX_HEAD_DIM), fox_q_norm[l])
        fk = rmsnorm(fk.reshape(B, S, FOX_HEADS, FOX_HEAD_DIM), fox_k_norm[l])
        fv = fv.reshape(B, S, FOX_HEADS, FOX_HEAD_DIM)
        logf = jax.nn.log_sigmoid(ff.astype(jnp.float32) + b_forget[l].astype(jnp.float32))
        o_a = fox_attention(fq, fk, fv, logf)

        dq = rope(rmsnorm(dq.reshape(B, S, DSA_HEADS, DSA_HEAD_DIM), dsa_q_norm[l]), cos, sin)
        dk = rope(rmsnorm(dk[:, :, None, :], dsa_k_norm[l]), cos, sin)[:, :, 0, :]
        iq = rope(iq.reshape(B, S, IDX_HEADS, IDX_DIM), cos_i, sin_i)
        ik = rope(ik[:, :, None, :], cos_i, sin_i)[:, :, 0, :]
        iw = iw * (IDX_HEADS ** -0.5 * IDX_DIM ** -0.5)
        o_b = dsa_attention(dq, dk, dv, iq, ik, iw, k_sel)

        km, vm = jnp.split(rmsnorm(mem, mem_norm[l]) @ w_mem_kv[l], 2, axis=-1)
        km = rmsnorm(km.reshape(B, M, MEM_HEADS, MEM_HEAD_DIM), mem_k_norm[l])
        vm = vm.reshape(B, M, MEM_HEADS, MEM_HEAD_DIM)
        mq = rmsnorm(mq.reshape(B, S, MEM_HEADS, MEM_HEAD_DIM), mem_q_norm[l])
        o_c = memory_attention(mq, km, vm)

        branches = jnp.stack([o_a, o_b, o_c], axis=2)
        proj = jnp.einsum('bsnw,nwd->bsnd', branches, w_branch[l])
        gates = jax.nn.sigmoid(g.reshape(B, S, N_BRANCHES, D))
        h = h + jnp.sum(gates * proj, axis=2) @ w_out[l]

        h = h + 0.5 * swiglu(rmsnorm(h, ffn2_norm[l]), ffn2_w_gate[l], ffn2_w_up[l], ffn2_w_down[l])
    return h
```

```python
import numpy as np
from contextlib import ExitStack
import concourse.bass as bass
import concourse.mybir as mybir
from concourse.bass_utils import run_bass_kernel_spmd

F32 = mybir.dt.float32
BF16 = mybir.dt.bfloat16
I32 = mybir.dt.int32
AF = mybir.ActivationFunctionType
ALU = mybir.AluOpType
AX = mybir.AxisListType

NCORES = 8
D = 1024
S = 8192
TOK = 2048
NT = 16
DFF = 2816
NFC = 22
DIN = 6352
EPS = 1e-6
ENG = ['pe', 'act', 'dve', 'pool', 'sp']
SAME_ENG_SYNC = True
MAXQ = {'sp': 6, 'pool': 4, 'act': 6}


class Buf:
    __slots__ = ('name', 'w', 'r')

    def __init__(self, name):
        self.name = name
        self.w = {}
        self.r = {}


class Sched:
    def __init__(self, nc, es):
        self.nc = nc
        self.es = es
        self.semobj = {}
        self.cnt = {}
        self.prog = {e: [] for e in ENG}
        self.waited = {e: {} for e in ENG}
        self.dsem = {}
        self.epoch = 0
        self.ekey = {}
        for e in ENG:
            k = e + '@0'
            self.ekey[e] = k
            self.semobj[k] = es.enter_context(nc.semaphore('sem_' + e + '_0'))
            self.cnt[k] = 0
        self.ninst = {e: 0 for e in ENG}
        self.outq = {}

    def _deps(self, reads, writes, nowaw=False):
        deps = {}

        def add(k, v):
            if deps.get(k, 0) < v:
                deps[k] = v
        for b in reads:
            for k, v in b.w.items():
                add(k, v)
        for b in writes:
            if not nowaw:
                for k, v in b.w.items():
                    add(k, v)
            for k, v in b.r.items():
                add(k, v)
        return deps

    def _mark(self, key, v, reads, writes, nowaw):
        for b in reads:
            if b.r.get(key, 0) < v:
                b.r[key] = v
        for b in writes:
            if nowaw:
                if b.w.get(key, 0) < v:
                    b.w[key] = v
            else:
                b.w = {key: v}
                b.r = {}

    def _emit_waits(self, eng, deps):
        for k, v in deps.items():
            if k.split('@')[0] == eng and (eng == 'pe' or not SAME_ENG_SYNC):
                continue
            if k in self.dsem:
                v = self.dsem[k]
            if self.waited[eng].get(k, 0) >= v:
                continue
            self.waited[eng][k] = v
            sem = self.semobj[k]
            self.prog[eng].append(lambda e, sem=sem, v=v: e.wait_ge(sem, v))
            self.ninst[eng] += 1

    def op(self, eng, fn, reads=(), writes=(), nowaw=False):
        deps = self._deps(reads, writes, nowaw)
        self._emit_waits(eng, deps)
        key = self.ekey[eng]
        self.cnt[key] += 1
        n = self.cnt[key]
        sem = self.semobj[key]
        self.prog[eng].append(lambda e, fn=fn, sem=sem: fn(e).then_inc(sem, 1))
        self.ninst[eng] += 1
        self._mark(key, n, reads, writes, nowaw)

    def dma(self, q, out, in_, reads=(), writes=(), sem='dma', nowaw=False, **kw):
        deps = self._deps(reads, writes, nowaw)
        self._emit_waits(q, deps)
        if sem not in self.dsem:
            self.semobj[sem] = self.es.enter_context(self.nc.semaphore('dsem_' + sem))
            self.dsem[sem] = 0
        fifo = self.outq.setdefault(q, [])
        while len(fifo) >= MAXQ[q]:
            osem = fifo[0][0]
            ov = self.dsem[osem]
            fifo[:] = [x for x in fifo if x[0] != osem]
            if self.waited[q].get(osem, 0) < ov:
                self.waited[q][osem] = ov
                so = self.semobj[osem]
                self.prog[q].append(lambda e, so=so, ov=ov: e.wait_ge(so, ov))
                self.ninst[q] += 1
        self.dsem[sem] += 16
        v = self.dsem[sem]
        fifo.append((sem, v))
        s = self.semobj[sem]
        self.prog[q].append(lambda e, out=out, in_=in_, kw=kw, s=s:
                            e.dma_start(out=out, in_=in_, **kw).then_inc(s, 16))
        self.ninst[q] += 1
        self._mark(sem, v, reads, writes, nowaw)

    def custom(self, q, fn, reads=(), writes=(), sem='cc', inc=1):
        deps = self._deps(reads, writes)
        self._emit_waits(q, deps)
        if sem not in self.dsem:
            self.semobj[sem] = self.es.enter_context(self.nc.semaphore('dsem_' + sem))
            self.dsem[sem] = 0
        self.dsem[sem] += inc
        v = self.dsem[sem]
        s = self.semobj[sem]
        self.prog[q].append(lambda e, fn=fn, s=s, inc=inc: fn(e).then_inc(s, inc))
        self._mark(sem, v, reads, writes, False)

    def barrier(self):
        for e in ENG:
            deps = {self.ekey[k]: self.cnt[self.ekey[k]] for k in ENG
                    if self.cnt[self.ekey[k]] > 0 and k != e}
            for k, v in self.dsem.items():
                if v > 0:
                    deps[k] = v
            self._emit_waits(e, deps)

    def new_epoch(self):
        self.epoch += 1
        for e in ENG:
            k = '%s@%d' % (e, self.epoch)
            self.ekey[e] = k
            self.semobj[k] = self.es.enter_context(self.nc.semaphore('sem_%s_%d' % (e, self.epoch)))
            self.cnt[k] = 0

    def run(self):
        nc = self.nc
        with nc.Block() as block:
            @block.tensor
            def _(e):
                for f in self.prog['pe']:
                    f(e)

            @block.scalar
            def _(e):
                for f in self.prog['act']:
                    f(e)

            @block.vector
            def _(e):
                for f in self.prog['dve']:
                    f(e)

            @block.gpsimd
            def _(e):
                for f in self.prog['pool']:
                    f(e)

            @block.sync
            def _(e):
                for f in self.prog['sp']:
                    f(e)
        self.prog = {e: [] for e in ENG}
        self.new_epoch()


class Ctx:
    pass


def zig_tiles(j):
    out = []
    for m in range(8):
        out.append(8 * m + j)
        out.append(8 * m + 7 - j)
    return out


SPLITS = dict(fq=(0, 512), fk=(512, 1024), fv=(1024, 1536), ff=(1536, 1544), dq=(1544, 2056),
              dk=(2056, 2120), dv=(2120, 2184), iq=(2184, 2696), ik=(2696, 2760),
              iw=(2760, 2768), mq=(2768, 3280), g=(3280, 6352))
KROWS = 1225
WNAMES = {
    'ffn1_w_gate': (D, DFF), 'ffn1_w_up': (D, DFF), 'ffn1_w_down': (DFF, D),
    'w_in': (D, DIN), 'w_mem_kv': (D, D), 'w_branch': (3 * 512, D), 'w_out': (D, D),
    'ffn2_w_gate': (D, DFF), 'ffn2_w_up': (D, DFF), 'ffn2_w_down': (DFF, D),
}
VNAMES = {'ffn1_norm': D, 'mix_norm': D, 'mem_norm': D, 'ffn2_norm': D, 'b_forget': 8,
          'fox_q_norm': 64, 'fox_k_norm': 64, 'dsa_q_norm': 64, 'dsa_k_norm': 64,
          'mem_q_norm': 128, 'mem_k_norm': 128}
STAGES = ['ffn1', 'a2', 'full']


def build(stage='full'):
    nc = bass.Bass("TRN2", target_bir_lowering=False)
    es = ExitStack()
    sc = Sched(nc, es)
    c = Ctx()
    dbg = {}
    uq = [0]

    def un(name):
        uq[0] += 1
        return '%s_%d' % (name, uq[0])

    def din(name, shape, dt=F32):
        return nc.dram_tensor(name, list(shape), dt, kind="ExternalInput")

    def dscr(name, shape, dt, out=False):
        if out:
            dbg[name] = True
        return nc.dram_tensor(name, list(shape), dt, kind="ExternalOutput" if out else "Internal")

    x_d = din('x', [TOK, D])
    pos_d = din('pos', [TOK], I32)
    qposf_d = din('qposf', [TOK])
    ident_d = din('ident', [128, 128])
    invf_d = din('invf', [32])
    mem_d = din('mem', [256, D])
    tri_d = din('tri', [128, 128])
    kposc_d = din('kposc', [128, 64])
    pow2_d = din('pow2', [32])
    iota512_d = din('iota512', [512])
    w_d = {k: din(k, s) for k, s in WNAMES.items()}
    v_d = {k: din(k, [n]) for k, n in VNAMES.items()}
    h1_d = dscr('h1s', [TOK, D], F32, out=(stage == 'ffn1'))
    b_h1d = Buf('h1s')
    A2O = (stage == 'a2')
    fqT_d = dscr('fqT_s', [128, 4, TOK], BF16, out=A2O)
    dqT_d = dscr('dqT_s', [128, 4, TOK], BF16, out=A2O)
    iqT_d = dscr('iqT_s', [128, 4, TOK], BF16, out=A2O)
    mqT_d = dscr('mqT_s', [128, 4, TOK], BF16, out=A2O)
    fkT_in = [dscr('fkT_in%d' % i, [256, TOK], BF16, out=A2O) for i in range(2)]
    fvA_in = [dscr('fvA_in%d' % i, [TOK, 130], BF16, out=A2O) for i in range(4)]
    dikT_in = dscr('dikT_in', [128, TOK], BF16, out=A2O)
    dvA_in = dscr('dvA_in', [TOK, 65], BF16, out=A2O)
    lin_d = dscr('lin', [TOK, 8], F32, out=A2O)
    b_fqTd, b_dqTd, b_iqTd, b_mqTd, b_kind, b_lind = [Buf(n) for n in
                                                    ['fqTd', 'dqTd', 'iqTd', 'mqTd', 'kind', 'lind']]
    o = 0
    kin_dikT = dikT_in
    kin_dvA = dvA_in
    GSH = {'lall': (TOK, 8, F32), 'fkT_g0': (256, TOK, BF16), 'fkT_g1': (256, TOK, BF16),
           'fvA_g0': (TOK, 130, BF16), 'fvA_g1': (TOK, 130, BF16), 'fvA_g2': (TOK, 130, BF16),
           'fvA_g3': (TOK, 130, BF16), 'dikT_g': (128, TOK, BF16), 'dvA_g': (TOK, 65, BF16)}
    g_d = {k: dscr(k, [4 * r, c_], dt) for k, (r, c_, dt) in GSH.items()}
    y_d = nc.dram_tensor('y', [TOK, D], F32, kind="ExternalOutput")
    b_yd = Buf('y')
    h2_d = dscr('h2s', [TOK, D], F32)
    b_h2d = Buf('h2s')
    wb_d = {k: dscr(k + '_bf', s_, BF16) for k, s_ in WNAMES.items()}
    wb_buf = {k: Buf(k + '_bf') for k in WNAMES}

    def sbg(name, shape, dt):
        return es.enter_context(nc.sbuf_tensor(un(name), list(shape), dt))

    ident_f = sbg('ident_f', [128, 128], F32)
    ident_b = sbg('ident_b', [128, 128], BF16)
    b_ident_f = Buf('ident_f')
    b_ident_b = Buf('ident_b')
    sc.dma('sp', ident_f[:], ident_d[:, :], writes=[b_ident_f], sem='ident')
    sc.op('dve', lambda e: e.tensor_copy(out=ident_b[:], in_=ident_f[:]),
          reads=[b_ident_f], writes=[b_ident_b])
    isign = sbg('isign', [128, NT, 8], F32)
    b_isign = Buf('isign')
    es_a = ExitStack()
    uT = es_a.enter_context(nc.sbuf_tensor('uT', [128, 8, TOK], BF16))
    b_uT = Buf('uT')

    def cast_weight(name):
        src = w_d[name].reshape([-1, 1024])
        dst = wb_d[name].reshape([-1, 1024])
        rows = src.shape[0]
        r0 = 0
        while r0 < rows:
            r1 = min(rows, r0 + 2048)
            sc.dma('pool', dst[r0:r1, :], src[r0:r1, :], writes=[wb_buf[name]], sem='wcast')
            r0 = r1

    for k in WNAMES:
        cast_weight(k)

    psum = [es.enter_context(nc.psum_tensor('ps%d' % i, [128, 512], F32)) for i in range(8)]
    psb = [Buf('ps%d' % i) for i in range(8)]
    c.ps_i = 0

    def next_ps():
        i = c.ps_i
        c.ps_i = (i + 1) % 8
        return psum[i], psb[i]

    def ts(eng, out, in0, s1, s2, op0, op1=None, reads=(), writes=(), accum=None, nowaw=False):
        if op1 is None:
            sc.op(eng, lambda e: e.tensor_scalar(out=out, in0=in0, scalar1=s1, scalar2=None,
                                                 op0=op0, accum_out=accum), reads, writes, nowaw)
        else:
            sc.op(eng, lambda e: e.tensor_scalar(out=out, in0=in0, scalar1=s1, scalar2=s2,
                                                 op0=op0, op1=op1, accum_out=accum), reads, writes,
                  nowaw)

    def tt(eng, out, in0, in1, op, reads=(), writes=(), nowaw=False):
        sc.op(eng, lambda e: e.tensor_tensor(out=out, in0=in0, in1=in1, op=op), reads, writes,
              nowaw)

    def act(out, in_, func, reads=(), writes=(), nowaw=False, **kw):
        sc.op('act', lambda e: e.activation(out=out, in_=in_, func=func, **kw), reads, writes,
              nowaw)

    def mm(out, lhsT, rhs, start, stop, reads=(), writes=(), **kw):
        sc.op('pe', lambda e: e.matmul(out=out, lhsT=lhsT, rhs=rhs, start=start, stop=stop, **kw),
              reads, writes)

    def tr(out, in_, ident, reads=(), writes=()):
        sc.op('pe', lambda e: e.transpose(out=out, in_=in_, identity=ident), reads, writes)

    def load_gain_T(sbf, name, dram, ncol):
        t = sbf(name, [128, ncol], F32)
        b = Buf(name)
        sc.dma('sp', t[:], dram.rearrange('(kc p) -> p kc', p=128), writes=[b], sem='const',
               allow_slow_non_contiguous=True)
        return t, b

    def load_bc(sbf, name, dram_ap, n):
        t = sbf(name, [128, n], F32)
        b = Buf(name)
        sc.dma('sp', t[:], dram_ap.partition_broadcast(128), writes=[b], sem='const')
        return t, b

    def rstd_from_ss(st, bst, n, dim):
        ts('dve', st[:, n:2 * n], st[:, 0:n], 1.0 / dim, EPS, ALU.mult, ALU.add,
           reads=[bst], writes=[bst])
        act(st[:, 2 * n:3 * n], st[:, n:2 * n], AF.Sqrt, reads=[bst], writes=[bst])
        sc.op('dve', lambda e: e.reciprocal(out=st[:, 3 * n:4 * n], in_=st[:, 2 * n:3 * n]),
              reads=[bst], writes=[bst])
        return st[:, 3 * n:4 * n]

    def make_norm_transpose(sbf):
        sq_junk = sbf('sq_junk', [128, D], BF16)
        b_sq_junk = Buf('sq_junk')
        xn_bf = [sbf('xn_bf%d' % i, [128, D], BF16) for i in range(2)]
        b_xn_bf = [Buf('xn_bf%d' % i) for i in range(2)]
        stat = [sbf('nstat%d' % i, [128, 4], F32) for i in range(2)]
        b_stat = [Buf('nstat%d' % i) for i in range(2)]
        state = {'i': 0}

        def norm_transpose(src_tile, b_src, gT, b_gT, dstT, b_dst, col0):
            i = state['i']
            state['i'] ^= 1
            st, bst = stat[i], b_stat[i]
            xb, bxb = xn_bf[i], b_xn_bf[i]
            act(sq_junk[:], src_tile, AF.Square, reads=[b_src], writes=[b_sq_junk, bst],
                accum_out=st[:, 0:1])
            rs = rstd_from_ss(st, bst, 1, D)
            act(xb[:], src_tile, AF.Copy, reads=[b_src, bst], writes=[bxb], scale=rs)
            ps, bps = next_ps()
            psv = ps[:].bitcast(BF16)
            for kc in range(8):
                tr(psv[:, kc * 128:(kc + 1) * 128], xb[:, kc * 128:(kc + 1) * 128], ident_b[:],
                   reads=[bxb, b_ident_b], writes=[bps])
            for kc in range(8):
                ts('dve', dstT[:, kc, col0:col0 + 128], psv[:, kc * 128:(kc + 1) * 128],
                   gT[:, kc:kc + 1], None, ALU.mult, reads=[bps, b_gT], writes=[b_dst],
                   nowaw=True)
        return norm_transpose

    HALF = 1024

    def ffn_phase(src_d, b_srcd, dst_d, b_dstd, gname, wgn, wun, wdn, post_gname=None,
                  postT=None, b_postT=None):
        pes = ExitStack()

        def sbf(name, shape, dt):
            return pes.enter_context(nc.sbuf_tensor(un(name), list(shape), dt))
        norm_transpose = make_norm_transpose(sbf)
        gT, b_gT = load_gain_T(sbf, 'gT_' + gname, v_d[gname], 8)
        if post_gname:
            pgT, b_pgT = load_gain_T(sbf, 'gT_' + post_gname, v_d[post_gname], 8)
        xt = [sbf('xt%d' % i, [128, D], F32) for i in range(3)]
        b_xt = [Buf('xt%d' % i) for i in range(3)]
        xnT = sbf('xnT', [128, 8, HALF], BF16)
        b_xnT = Buf('xnT')
        actT = sbf('actT', [128, NFC, HALF], BF16)
        b_actT = [Buf('actT%d' % i) for i in range(NFC)]
        wd_sb = sbf('wd_sb', [128, NFC, D], BF16)
        b_wd = Buf('wd_sb')
        WC = 256
        wg_sb = [sbf('wg_sb%d' % i, [128, 8, WC], BF16) for i in range(2)]
        wu_sb = [sbf('wu_sb%d' % i, [128, 8, WC], BF16) for i in range(2)]
        b_wg = [Buf('wg%d' % i) for i in range(2)]
        b_wu = [Buf('wu%d' % i) for i in range(2)]
        sg_sb = [sbf('sg_sb%d' % i, [128, 512], BF16) for i in range(2)]
        b_sg = [Buf('sg%d' % i) for i in range(2)]
        ho = [sbf('ho%d' % i, [128, D], F32) for i in range(2)]
        b_ho = [Buf('ho%d' % i) for i in range(2)]
        st = {'xt': 0, 'w': 0, 'sg': 0, 'ho': 0}
        wg_d, wu_d, wdd = wb_d[wgn], wb_d[wun], wb_d[wdn]
        sc.dma('sp', wd_sb[:], wdd.rearrange('(fc p) d -> p fc d', p=128),
               reads=[wb_buf[wdn]], writes=[b_wd], sem='wd')
        for half in range(TOK // HALF):
            for t in range(HALF // 128):
                i = st['xt']
                st['xt'] = (i + 1) % 3
                r0 = half * HALF + t * 128
                sc.dma('sp', xt[i][:], src_d[r0:r0 + 128, :], reads=[b_srcd], writes=[b_xt[i]],
                       sem='xt%d' % i)
                norm_transpose(xt[i][:], b_xt[i], gT, b_gT, xnT, b_xnT, t * 128)
            for wc in range(DFF // WC):
                wi = st['w']
                st['w'] ^= 1
                sc.dma('sp', wg_sb[wi][:],
                       wg_d[:, wc * WC:(wc + 1) * WC].rearrange('(kc p) f -> p kc f', p=128),
                       reads=[wb_buf[wgn]], writes=[b_wg[wi]], sem='wg%d' % wi)
                sc.dma('sp', wu_sb[wi][:],
                       wu_d[:, wc * WC:(wc + 1) * WC].rearrange('(kc p) f -> p kc f', p=128),
                       reads=[wb_buf[wun]], writes=[b_wu[wi]], sem='wu%d' % wi)
                for fl in range(WC // 128):
                    fc = wc * (WC // 128) + fl
                    for tb in range(HALF // 512):
                        pg, bpg = next_ps()
                        pu, bpu = next_ps()
                        for kc in range(8):
                            mm(pg[:], wg_sb[wi][:, kc, fl * 128:(fl + 1) * 128],
                               xnT[:, kc, tb * 512:(tb + 1) * 512], kc == 0, kc == 7,
                               reads=[b_wg[wi], b_xnT], writes=[bpg])
                        for kc in range(8):
                            mm(pu[:], wu_sb[wi][:, kc, fl * 128:(fl + 1) * 128],
                               xnT[:, kc, tb * 512:(tb + 1) * 512], kc == 0, kc == 7,
                               reads=[b_wu[wi], b_xnT], writes=[bpu])
                        si = st['sg']
                        st['sg'] ^= 1
                        act(sg_sb[si][:], pg[:], AF.Silu, reads=[bpg], writes=[b_sg[si]])
                        tt('dve', actT[:, fc, tb * 512:(tb + 1) * 512], pu[:], sg_sb[si][:],
                           ALU.mult, reads=[bpu, b_sg[si]], writes=[b_actT[fc]])
            for t in range(HALF // 128):
                r0 = half * HALF + t * 128
                i = st['xt']
                st['xt'] = (i + 1) % 3
                sc.dma('sp', xt[i][:], src_d[r0:r0 + 128, :], reads=[b_srcd], writes=[b_xt[i]],
                       sem='xt%d' % i)
                hi = st['ho']
                st['ho'] ^= 1
                for dh in range(2):
                    pd, bpd = next_ps()
                    for fc in range(NFC):
                        mm(pd[:], actT[:, fc, t * 128:(t + 1) * 128],
                           wd_sb[:, fc, dh * 512:(dh + 1) * 512], fc == 0, fc == NFC - 1,
                           reads=[b_actT[fc], b_wd], writes=[bpd])
                    sc.op('dve', lambda e, pd=pd, hi=hi, i=i, dh=dh: e.scalar_tensor_tensor(
                        out=ho[hi][:, dh * 512:(dh + 1) * 512], in0=pd[:], scalar=0.5,
                        in1=xt[i][:, dh * 512:(dh + 1) * 512], op0=ALU.mult, op1=ALU.add),
                        reads=[bpd, b_xt[i]], writes=[b_ho[hi]])
                sc.dma('sp', dst_d[r0:r0 + 128, :], ho[hi][:], reads=[b_ho[hi]], writes=[b_dstd],
                       sem='ho%d' % hi, nowaw=True)
                if post_gname:
                    norm_transpose(ho[hi][:], b_ho[hi], pgT, b_pgT, postT, b_postT, r0)
        sc.barrier()
        sc.run()
        pes.close()

    b_x = Buf('x_d')
    ffn_phase(x_d, b_x, h1_d, b_h1d, 'ffn1_norm', 'ffn1_w_gate', 'ffn1_w_up', 'ffn1_w_down',
              post_gname='mix_norm', postT=uT, b_postT=b_uT)

    if stage == 'ffn1':
        es.close()
        print('ninst', sc.ninst, 'max sem', max(sc.cnt.values()), max(sc.dsem.values()), 'nsem', len(sc.semobj))
        return nc, dbg

    def phase_a2():
        pes = ExitStack()

        def sbf(name, shape, dt):
            return pes.enter_context(nc.sbuf_tensor(un(name), list(shape), dt))
        NG = 3280
        win_sb = sbf('win_sb', [128, 8, NG], BF16)
        b_win = Buf('win_sb')
        wv = wb_d['w_in']

        def ldw(dst0, c0, c1):
            sc.dma('sp', win_sb[:, :, dst0:dst0 + (c1 - c0)],
                   wv[:, c0:c1].rearrange('(kc p) f -> p kc f', p=128),
                   reads=[wb_buf['w_in']], writes=[b_win], sem='win')
        GOFF = {k: v[0] for k, v in SPLITS.items()}
        for q4 in range(4):
            ldw(q4 * 820, q4 * 820, (q4 + 1) * 820)

        gq2 = sbf('gq2', [128, 1], F32)
        gk2 = sbf('gk2', [128, 1], F32)
        gmq = sbf('gmq', [128, 1], F32)
        b_g2 = Buf('g2')
        for hh in range(2):
            sc.dma('sp', gq2[hh * 64:(hh + 1) * 64, :],
                   v_d['fox_q_norm'].rearrange('(p o) -> p o', o=1), writes=[b_g2], sem='const')
            sc.dma('sp', gk2[hh * 64:(hh + 1) * 64, :],
                   v_d['fox_k_norm'].rearrange('(p o) -> p o', o=1), writes=[b_g2], sem='const')
        sc.dma('sp', gmq[:], v_d['mem_q_norm'].rearrange('(p o) -> p o', o=1), writes=[b_g2],
               sem='const')
        gdq, b_gdq = load_bc(sbf, 'gdq', v_d['dsa_q_norm'].ap(), 64)
        gdk, b_gdk = load_bc(sbf, 'gdk', v_d['dsa_k_norm'].ap(), 64)
        bfor, b_bfor = load_bc(sbf, 'bfor', v_d['b_forget'].ap(), 8)
        invf, b_invf = load_bc(sbf, 'invf_sb', invf_d.ap(), 32)
        posi = sbf('posi', [128, NT], I32)
        posf = sbf('posf', [128, NT], F32)
        b_pos = Buf('pos')
        sc.dma('sp', posi[:], pos_d.rearrange('(t p) -> p t', p=128), writes=[b_pos], sem='const',
               allow_slow_non_contiguous=True)
        sc.op('dve', lambda e: e.tensor_copy(out=posf[:], in_=posi[:]), reads=[b_pos],
              writes=[b_pos])
        ang = sbf('ang', [128, NT, 64], F32)
        kk = sbf('kk', [128, NT, 64], F32)
        kki = sbf('kki', [128, NT, 64], I32)
        cs = sbf('cs', [128, NT, 64], F32)
        b_ang = Buf('ang')
        b_cs = Buf('cs')
        TWO_PI = float(2 * np.pi)
        for t in range(NT):
            ts('dve', ang[:, t, 32:64], invf[:], posf[:, t:t + 1], None, ALU.mult,
               reads=[b_invf, b_pos], writes=[b_ang])
        ts('dve', ang[:, :, 0:32], ang[:, :, 32:64], float(np.pi / 2), None, ALU.add,
           reads=[b_ang], writes=[b_ang])
        ts('dve', kk[:], ang[:], 1.0 / TWO_PI, 0.5, ALU.mult, ALU.add, reads=[b_ang], writes=[b_ang])
        sc.op('dve', lambda e: e.tensor_copy(out=kki[:], in_=kk[:]), reads=[b_ang], writes=[b_ang])
        sc.op('dve', lambda e: e.tensor_copy(out=kk[:], in_=kki[:]), reads=[b_ang], writes=[b_ang])
        sc.op('dve', lambda e: e.scalar_tensor_tensor(out=ang[:], in0=kk[:], scalar=-TWO_PI,
                                                      in1=ang[:], op0=ALU.mult, op1=ALU.add),
              reads=[b_ang], writes=[b_ang])
        ts('dve', kk[:], ang[:], float(-np.pi), TWO_PI, ALU.is_lt, ALU.mult, reads=[b_ang],
           writes=[b_ang])
        tt('dve', ang[:], ang[:], kk[:], ALU.add, reads=[b_ang], writes=[b_ang])
        ts('dve', kk[:], ang[:], float(np.pi), -TWO_PI, ALU.is_gt, ALU.mult, reads=[b_ang],
           writes=[b_ang])
        tt('dve', ang[:], ang[:], kk[:], ALU.add, reads=[b_ang], writes=[b_ang])
        ts('dve', ang[:], ang[:], float(np.pi), float(-np.pi), ALU.min, ALU.max, reads=[b_ang],
           writes=[b_ang])
        act(cs[:], ang[:], AF.Sin, reads=[b_ang], writes=[b_cs])
        abq = sbf('abq', [128, NT, 4, 32], F32)
        abk = sbf('abk', [128, NT, 4, 32], F32)
        b_abq = Buf('abq')
        b_abk = Buf('abk')
        for (ab, bab, g, bg) in [(abq, b_abq, gdq, b_gdq), (abk, b_abk, gdk, b_gdk)]:
            g1 = g[:, 0:32].unsqueeze(1).broadcast_to([128, NT, 32])
            g2 = g[:, 32:64].unsqueeze(1).broadcast_to([128, NT, 32])
            tt('dve', ab[:, :, 0, :], cs[:, :, 0:32], g1, ALU.mult, reads=[b_cs, bg], writes=[bab])
            tt('dve', ab[:, :, 1, :], cs[:, :, 32:64], g2, ALU.mult, reads=[b_cs, bg], writes=[bab])
            tt('dve', ab[:, :, 2, :], cs[:, :, 0:32], g2, ALU.mult, reads=[b_cs, bg], writes=[bab])
            tt('dve', ab[:, :, 3, :], cs[:, :, 32:64], g1, ALU.mult, reads=[b_cs, bg], writes=[bab])

        sqf = sbf('sqf', [128, 512], F32)
        b_sqf = Buf('sqf')
        st8 = sbf('st8', [128, 32], F32)
        b_st8 = Buf('st8')
        r1 = sbf('r1', [128, 8, 32], F32)
        r2 = sbf('r2', [128, 8, 32], F32)
        ro = sbf('ro', [128, 8, 64], F32)
        b_r1, b_r2, b_ro = Buf('r1'), Buf('r2'), Buf('ro')
        tok_bf = [sbf('tok_bf%d' % i, [128, 512], BF16) for i in range(2)]
        b_tok_bf = [Buf('tok_bf%d' % i) for i in range(2)]
        tT = [sbf('tT%d' % i, [128, 4, 128], BF16) for i in range(2)]
        b_tT = [Buf('tT%d' % i) for i in range(2)]
        pack6 = sbf('pack6', [128, 128], BF16)
        b_pack6 = Buf('pack6')
        p6T = sbf('p6T', [128, 128], BF16)
        b_p6T = Buf('p6T')
        dvb = sbf('dvb', [128, 65], BF16)
        b_dvb = Buf('dvb')
        fvst = sbf('fvst', [128, 8, 65], BF16)
        b_fvst = Buf('fvst')
        sc.op('dve', lambda e: e.memset(dvb[:], 1.0), writes=[b_dvb])
        sc.op('dve', lambda e: e.memset(fvst[:], 1.0), writes=[b_fvst])
        lf = sbf('lf', [128, 3, 8], F32)
        b_lf = Buf('lf')
        absw = sbf('absw', [128, 8], F32)
        b_absw = Buf('absw')
        stt = {'tb': 0, 'tT': 0}
        lfall = sbf('lfall', [128, NT, 8], F32)
        b_lfall = Buf('lfall')
        if stage == 'b':
            lfdbg = sbf('lfdbg', [128, NT, 3, 8], F32)
            b_lfdbg = Buf('lfdbg')

        def nxt(key, n=2):
            i = stt[key]
            stt[key] = (i + 1) % n
            return i

        def rope(src_ps, bsrc, nh, tabs, btabs, t, dst, bdst, plain):
            sv = src_ps.rearrange('p (h two d) -> p h two d', two=2, d=32)
            x1 = sv[:, :, 0, :]
            x2 = sv[:, :, 1, :]

            def tb(j):
                if plain:
                    a = cs[:, t, 0:32] if j in (0, 2) else cs[:, t, 32:64]
                else:
                    a = tabs[:, t, j, :]
                return a.unsqueeze(1).broadcast_to([128, nh, 32])
            A_, B_, C_, D_ = tb(0), tb(1), tb(2), tb(3)
            tt('dve', r1[:, 0:nh, :], x1, A_, ALU.mult, reads=[bsrc, btabs], writes=[b_r1])
            tt('dve', r2[:, 0:nh, :], x2, B_, ALU.mult, reads=[bsrc, btabs], writes=[b_r2])
            tt('dve', dst[:, 0:nh, 0:32], r1[:, 0:nh, :], r2[:, 0:nh, :], ALU.subtract,
               reads=[b_r1, b_r2], writes=[bdst])
            tt('dve', r1[:, 0:nh, :], x2, C_, ALU.mult, reads=[bsrc, btabs], writes=[b_r1])
            tt('dve', r2[:, 0:nh, :], x1, D_, ALU.mult, reads=[bsrc, btabs], writes=[b_r2])
            tt('dve', dst[:, 0:nh, 32:64], r1[:, 0:nh, :], r2[:, 0:nh, :], ALU.add,
               reads=[b_r1, b_r2], writes=[bdst])

        def head_ss(ps, bps, nh, hd):
            act(sqf[:, 0:nh * hd], ps[:, 0:nh * hd], AF.Square, reads=[bps], writes=[b_sqf])
            sc.op('dve', lambda e: e.tensor_reduce(
                out=st8[:, 0:nh], in_=sqf[:, 0:nh * hd].rearrange('p (h d) -> p h d', d=hd),
                axis=AX.X, op=ALU.add), reads=[b_sqf], writes=[b_st8])
            return rstd_from_ss(st8, b_st8, nh, hd)

        def transpose_out(src_bf, bsrc, gcol, bg, dst_dram, bdst, col0):
            ps, bps = next_ps()
            psv = ps[:].bitcast(BF16)
            for k4 in range(4):
                tr(psv[:, k4 * 128:(k4 + 1) * 128], src_bf[:, k4 * 128:(k4 + 1) * 128], ident_b[:],
                   reads=[bsrc, b_ident_b], writes=[bps])
            i = nxt('tT')
            if gcol is None:
                act(tT[i][:].rearrange('p a b -> p (a b)'), psv[:, 0:512], AF.Copy, reads=[bps],
                    writes=[b_tT[i]])
            else:
                ts('dve', tT[i][:].rearrange('p a b -> p (a b)'), psv[:, 0:512], gcol, None,
                   ALU.mult, reads=[bps, bg], writes=[b_tT[i]])
            if dst_dram is None:
                for pr in range(2):
                    sc.dma('sp', fkT_in[pr].rearrange('(hp p) t -> p hp t', p=128)[
                        :, :, col0:col0 + 128], tT[i][:, 2 * pr:2 * pr + 2, :], reads=[b_tT[i]],
                        writes=[bdst], sem='tTo%d' % i, nowaw=True)
            else:
                sc.dma('sp', dst_dram[:, :, col0:col0 + 128], tT[i][:], reads=[b_tT[i]],
                       writes=[bdst], sem='tTo%d' % i, nowaw=True)

        def proj(t, goff, n):
            ps, bps = next_ps()
            for kc in range(8):
                mm(ps[:, 0:n], uT[:, kc, t * 128:(t + 1) * 128], win_sb[:, kc, goff:goff + n],
                   kc == 0, kc == 7, reads=[b_uT, b_win], writes=[bps])
            return ps, bps

        for t in range(NT):
            c0 = t * 128
            pff, bpff = proj(t, GOFF['ff'], 8)
            pdd, bpdd = proj(t, GOFF['dk'], 128)
            pii, bpii = proj(t, GOFF['ik'], 72)
            act(lf[:, 1, :], pff[:, 0:8], AF.Copy, reads=[bpff], writes=[b_lf])
            tt('dve', lf[:, 0, :], lf[:, 1, :], bfor[:], ALU.add, reads=[b_lf, b_bfor],
               writes=[b_lf])
            act(lf[:, 1, :], lf[:, 0, :], AF.Exp, reads=[b_lf], writes=[b_lf], scale=-1.0)
            ts('dve', lf[:, 1, :], lf[:, 1, :], 1.0, None, ALU.add, reads=[b_lf], writes=[b_lf])
            act(lf[:, 2, :], lf[:, 1, :], AF.Ln, reads=[b_lf], writes=[b_lf])
            ts('dve', lfall[:, t, :], lf[:, 2, :], -1.0, None, ALU.mult, reads=[b_lf],
               writes=[b_lfall], nowaw=True)
            if stage == 'b':
                sc.op('dve', lambda e, t=t: e.tensor_copy(out=lfdbg[:, t, :, :], in_=lf[:]),
                      reads=[b_lf], writes=[b_lfdbg], nowaw=True)
            act(absw[:], pii[:, 64:72], AF.Abs, reads=[bpii], writes=[b_absw])
            act(isign[:, t, :], pii[:, 64:72], AF.Sign, reads=[bpii], writes=[b_isign])
            act(sqf[:, 0:64], pdd[:, 0:64], AF.Square, reads=[bpdd], writes=[b_sqf, b_st8],
                accum_out=st8[:, 0:1])
            rsk = rstd_from_ss(st8, b_st8, 1, 64)
            rope(pdd[:, 0:64], bpdd, 1, abk, b_abk, t, ro, b_ro, False)
            ts('dve', pack6[:, 0:64], ro[:, 0, :], rsk, None, ALU.mult, reads=[b_ro, b_st8],
               writes=[b_pack6])
            rope(pii[:, 0:64], bpii, 1, cs, b_cs, t, ro, b_ro, True)
            sc.op('dve', lambda e: e.tensor_copy(out=pack6[:, 64:128], in_=ro[:, 0, :]),
                  reads=[b_ro], writes=[b_pack6])
            act(dvb[:, 0:64], pdd[:, 64:128], AF.Copy, reads=[bpdd], writes=[b_dvb])
            sc.dma('sp', kin_dvA[c0:c0 + 128, :], dvb[:], reads=[b_dvb], writes=[b_kind], sem='dvo',
                   nowaw=True)
            ps, bps = next_ps()
            psv = ps[:].bitcast(BF16)
            tr(psv[:, 0:128], pack6[:], ident_b[:], reads=[b_pack6, b_ident_b], writes=[bps])
            act(p6T[:], psv[:, 0:128], AF.Copy, reads=[bps], writes=[b_p6T])
            sc.dma('sp', kin_dikT[:, c0:c0 + 128], p6T[:], reads=[b_p6T], writes=[b_kind],
                   sem='p6o', nowaw=True)
            for nm, gcol, dst, bdst in [('fq', gq2, fqT_d, b_fqTd), ('fk', gk2, None, b_kind)]:
                ps, bps = proj(t, GOFF[nm], 512)
                rs = head_ss(ps, bps, 8, 64)
                i = nxt('tb')
                tt('dve', tok_bf[i][:].rearrange('p (h d) -> p h d', d=64),
                   ps[:].rearrange('p (h d) -> p h d', d=64),
                   rs.unsqueeze(2).broadcast_to([128, 8, 64]), ALU.mult,
                   reads=[bps, b_st8], writes=[b_tok_bf[i]])
                transpose_out(tok_bf[i], b_tok_bf[i], gcol[:, 0:1], b_g2, dst, bdst, c0)
            ps, bps = proj(t, GOFF['fv'], 512)
            act(fvst[:, :, 0:64], ps[:].rearrange('p (h d) -> p h d', d=64), AF.Copy, reads=[bps],
                writes=[b_fvst])
            for hp_ in range(4):
                sc.dma('sp', fvA_in[hp_][c0:c0 + 128, :],
                       fvst[:, 2 * hp_:2 * hp_ + 2, :].rearrange('p e c -> p (e c)'),
                       reads=[b_fvst], writes=[b_kind], sem='fvo', nowaw=True)
            ps, bps = proj(t, GOFF['dq'], 512)
            rs = head_ss(ps, bps, 8, 64)
            rope(ps[:], bps, 8, abq, b_abq, t, ro, b_ro, False)
            i = nxt('tb')
            tt('dve', tok_bf[i][:].rearrange('p (h d) -> p h d', d=64), ro[:],
               rs.unsqueeze(2).broadcast_to([128, 8, 64]), ALU.mult,
               reads=[b_ro, b_st8], writes=[b_tok_bf[i]])
            transpose_out(tok_bf[i], b_tok_bf[i], None, None, dqT_d, b_dqTd, c0)
            ps, bps = proj(t, GOFF['iq'], 512)
            rope(ps[:], bps, 8, cs, b_cs, t, ro, b_ro, True)
            i = nxt('tb')
            tt('dve', tok_bf[i][:].rearrange('p (h d) -> p h d', d=64), ro[:],
               absw[:].unsqueeze(2).broadcast_to([128, 8, 64]), ALU.mult,
               reads=[b_ro, b_absw], writes=[b_tok_bf[i]])
            transpose_out(tok_bf[i], b_tok_bf[i], None, None, iqT_d, b_iqTd, c0)
            ps, bps = proj(t, GOFF['mq'], 512)
            rs = head_ss(ps, bps, 4, 128)
            i = nxt('tb')
            tt('dve', tok_bf[i][:].rearrange('p (h d) -> p h d', d=128),
               ps[:].rearrange('p (h d) -> p h d', d=128),
               rs.unsqueeze(2).broadcast_to([128, 4, 128]), ALU.mult,
               reads=[bps, b_st8], writes=[b_tok_bf[i]])
            transpose_out(tok_bf[i], b_tok_bf[i], gmq[:, 0:1], b_g2, mqT_d, b_mqTd, c0)
        sc.dma('sp', lin_d.rearrange('(t p) h -> p t h', p=128), lfall[:], reads=[b_lfall],
               writes=[b_lind], sem='lfo')
        if stage == 'b':
            d1 = dscr('lf_dbg', [128, NT, 3, 8], F32, out=True)
            sc.dma('sp', d1[:, :, :, :], lfdbg[:], reads=[b_lfdbg], writes=[Buf('d1')], sem='dbg0')
            d2 = dscr('bfor_dbg', [128, 8], F32, out=True)
            sc.dma('sp', d2[:, :], bfor[:], reads=[b_bfor], writes=[Buf('d2')], sem='dbg0')
            d3 = dscr('wff_dbg', [128, 8, 8], BF16, out=True)
            sc.dma('sp', d3[:, :, :], win_sb[:, :, 3072:3080], reads=[b_win], writes=[Buf('d3')],
                   sem='dbg0')
        sc.barrier()
        sc.run()
        pes.close()

    phase_a2()
    es_a.close()
    o_sb = {n: sbg('o_' + n, [128, NT, 512], BF16) for n in 'abc'}
    b_o = {n: Buf('o_' + n) for n in 'abc'}
    if stage == 'a2':
        es.close()
        print('ninst', sc.ninst, 'max sem', max(sc.cnt.values()), max(sc.dsem.values()), 'nsem', len(sc.semobj))
        return nc, dbg

    RG = [[0, 1, 2, 3], [4, 5, 6, 7]]
    b_kall, b_lall = Buf('kall'), Buf('lall')

    def gather(name, src, rows, cols, dt, b_src, b_dst):
        g = g_d[name]
        sc.custom('pool', lambda e: e.collective_compute(
            'AllGather', ALU.bypass, replica_groups=RG, ins=[src.ap().opt()],
            outs=[g.ap().opt()]), reads=[b_src], writes=[b_dst], sem='cc')
        return g
    if stage == 'b':
        lo0 = dscr('lin_dbg0', [TOK, 8], F32, out=True)
        b_lo0 = Buf('lo0')
        sc.dma('sp', lo0[:, :], lin_d[:, :], reads=[b_lind], writes=[b_lo0], sem='dbg0')
    bar_in = dscr('bar_in', [1, 64], F32)
    bar_out = dscr('bar_out', [1, 64], F32)
    b_bar = Buf('bar')
    sc.dma('sp', bar_in[:, :], ident_d[0:1, 0:64], reads=[b_kind, b_lind], writes=[b_bar],
           sem='bar')
    sc.custom('pool', lambda e: e.collective_compute(
        'AllReduce', ALU.add, replica_groups=RG, ins=[bar_in.ap().opt()],
        outs=[bar_out.ap().opt()]), reads=[b_bar, b_kind, b_lind], writes=[b_bar, b_kind, b_lind],
        sem='cc')
    lall_d = gather('lall', lin_d, TOK, 8, F32, b_lind, b_lall)
    fkT_g = [gather('fkT_g%d' % i, fkT_in[i], 256, TOK, BF16, b_kind, b_kall) for i in range(2)]
    fvA_g = [gather('fvA_g%d' % i, fvA_in[i], TOK, 130, BF16, b_kind, b_kall) for i in range(4)]
    dikT_g = gather('dikT_g', dikT_in, 128, TOK, BF16, b_kind, b_kall)
    dvA_g = gather('dvA_g', dvA_in, TOK, 65, BF16, b_kind, b_kall)
    if stage == 'b':
        ld = dscr('lall_dbg', [4 * TOK, 8], F32, out=True)
        sc.dma('sp', ld[:, :], lall_d[:, :], reads=[b_lall], writes=[Buf('ldbg')], sem='dbg')
        lo_ = dscr('lin_dbg', [TOK, 8], F32, out=True)
        sc.dma('sp', lo_[:, :], lin_d[:, :], reads=[b_lind, b_lall], writes=[Buf('lodbg')],
               sem='dbg')
        kd = dscr('kall_dbg', [64, 2048], BF16, out=True)
        sc.dma('sp', kd[:, :], fkT_g[0][3 * 256:3 * 256 + 64, :], reads=[b_kall],
               writes=[Buf('kdbg')], sem='dbg')
        sc.barrier()
        sc.run()
        es.close()
        return nc, dbg

    def rank_of(jj):
        return jj if jj < 4 else 7 - jj

    NKI = [8 * (i // 2) + 4 if i % 2 == 0 else 8 * (i // 2) + 8 for i in range(NT)]
    BOFF = [0]
    for i in range(NT):
        BOFF.append(BOFF[-1] + NKI[i])

    def load_seq_T(q, dst, b_dst, p0, p1, gt, rows_per_rank, row0, sem):
        dv = dst[p0:p1, :].rearrange('p (m j c) -> p m j c', m=8, j=8)
        n = p1 - p0
        for r in range(4):
            src = gt[r * rows_per_rank + row0:r * rows_per_rank + row0 + n, :].rearrange(
                'p (m two c) -> p m two c', two=2, c=128)
            sc.dma(q, dv[:, :, r, :], src[:, :, 0, :], reads=[b_kall], writes=[b_dst], sem=sem,
                   nowaw=True)
            sc.dma(q, dv[:, :, 7 - r, :], src[:, :, 1, :], reads=[b_kall], writes=[b_dst], sem=sem,
                   nowaw=True)

    def load_seq_tok(q, dst, b_dst, gt, sem):
        dv = dst.rearrange('p (m j) c -> p m j c', j=8)
        for r in range(4):
            src = gt[r * TOK:(r + 1) * TOK, :].rearrange('(m two p) c -> p m two c', two=2, p=128)
            sc.dma(q, dv[:, :, r, :], src[:, :, 0, :], reads=[b_kall], writes=[b_dst], sem=sem,
                   nowaw=True)
            sc.dma(q, dv[:, :, 7 - r, :], src[:, :, 1, :], reads=[b_kall], writes=[b_dst], sem=sem,
                   nowaw=True)

    def phase_c1():
        pes = ExitStack()

        def sbf(name, shape, dt):
            return pes.enter_context(nc.sbuf_tensor(un(name), list(shape), dt))
        tri = sbf('tri_sb', [128, 128], F32)
        ones_f = sbf('ones_f', [128, 128], F32)
        kposc = sbf('kposc_sb', [128, 64], F32)
        qpos_bc = sbf('qpos_bc', [128, TOK], F32)
        b_cst = Buf('c1const')
        sc.dma('sp', tri[:], tri_d[:, :], writes=[b_cst], sem='const', nowaw=True)
        sc.dma('sp', kposc[:], kposc_d[:, :], writes=[b_cst], sem='const', nowaw=True)
        sc.dma('sp', qpos_bc[:], qposf_d.ap().partition_broadcast(128), writes=[b_cst],
               sem='const', nowaw=True)
        b_ones = Buf('ones_f')
        sc.op('dve', lambda e: e.memset(ones_f[:], 1.0), writes=[b_ones])
        L_sb = sbf('L_sb', [128, 64, 8], F32)
        b_L = Buf('L_sb')
        Lv = L_sb[:].rearrange('p (m j) h -> p m j h', j=8)
        for r in range(4):
            src = lall_d[r * TOK:(r + 1) * TOK, :].rearrange('(m two p) h -> p m two h',
                                                              two=2, p=128)
            sc.dma('sp', Lv[:, :, r, :], src[:, :, 0, :], reads=[b_lall], writes=[b_L],
                   sem='Lld', nowaw=True)
            sc.dma('sp', Lv[:, :, 7 - r, :], src[:, :, 1, :], reads=[b_lall], writes=[b_L],
                   sem='Lld', nowaw=True)
        Lf = L_sb[:].rearrange('p g h -> p (g h)')
        cinc = sbf('cinc', [128, 64, 8], F32)
        tot = sbf('tot', [128, 64, 8], F32)
        scA = sbf('scA', [128, 64, 8], F32)
        scB = sbf('scB', [128, 64, 8], F32)
        c_all = sbf('c_all', [128, 64, 8], F32)
        Tpre = sbf('Tpre', [128, 64, 8], F32)
        b_cinc, b_tot, b_scA, b_scB, b_call, b_Tpre = [Buf(n) for n in
                                                       ['cinc', 'tot', 'scA', 'scB', 'c_all', 'Tpre']]
        p1, bp1 = next_ps()
        mm(p1[:], tri[:], Lf, True, True, reads=[b_cst, b_L], writes=[bp1])
        p2, bp2 = next_ps()
        mm(p2[:], ones_f[:], Lf, True, True, reads=[b_ones, b_L], writes=[bp2])
        act(cinc[:].rearrange('p g h -> p (g h)'), p1[:], AF.Copy, reads=[bp1], writes=[b_cinc])
        act(tot[:].rearrange('p g h -> p (g h)'), p2[:], AF.Copy, reads=[bp2], writes=[b_tot])
        sc.op('dve', lambda e: e.tensor_copy(out=scA[:], in_=tot[:]), reads=[b_tot], writes=[b_scA])
        cur, bcur, oth, both = scA, b_scA, scB, b_scB
        sft = 1
        while sft < 64:
            tt('dve', oth[:, sft:, :], cur[:, sft:, :], cur[:, :64 - sft, :], ALU.add,
               reads=[bcur], writes=[both])
            sc.op('dve', lambda e, oth=oth, cur=cur, sft=sft: e.tensor_copy(
                out=oth[:, :sft, :], in_=cur[:, :sft, :]), reads=[bcur], writes=[both])
            cur, bcur, oth, both = oth, both, cur, bcur
            sft *= 2
        tt('dve', Tpre[:], cur[:], tot[:], ALU.subtract, reads=[bcur, b_tot], writes=[b_Tpre])
        tt('dve', c_all[:], cinc[:], Tpre[:], ALU.add, reads=[b_cinc, b_Tpre], writes=[b_call])
        biasT = sbf('biasT', [128, BOFF[NT], 8], F32)
        b_biasT = Buf('biasT')
        negmT = sbf('negmT', [128, NT, 4, 128], BF16)
        b_negmT = Buf('negmT')
        mcol = sbf('mcol', [128, 4], F32)
        b_mcol = Buf('mcol')
        mrep = [sbf('mrep%d' % i, [128, 128], F32) for i in range(2)] * 2
        b_mrep = [Buf('mrep%d' % i) for i in range(2)] * 2
        cref = sbf('cref', [128, 8], F32)
        b_cref = Buf('cref')
        for i in range(NT):
            nk = NKI[i]
            g0 = nk - 4
            pc, bpc = next_ps()
            qmid = qpos_bc[:, i * 128 + 64:i * 128 + 65]
            for b in range(4):
                tt('dve', mcol[:, b:b + 1], qmid, kposc[:, g0 + b:g0 + b + 1], ALU.is_ge,
                   reads=[b_cst], writes=[b_mcol])
                ts('dve', mrep[b][:], ones_f[:], mcol[:, b:b + 1], None, ALU.mult,
                   reads=[b_ones, b_mcol], writes=[b_mrep[b]])
                mm(pc[:, 0:8], mrep[b][:], L_sb[:, g0 + b, :], b == 0, b == 3,
                   reads=[b_mrep[b], b_L], writes=[bpc])
                ts('dve', negmT[:, i, b, :], qpos_bc[:, i * 128:(i + 1) * 128],
                   kposc[:, g0 + b:g0 + b + 1], -30000.0, ALU.is_lt, ALU.mult,
                   reads=[b_cst], writes=[b_negmT], nowaw=True)
            tt('dve', cref[:], pc[:, 0:8], Tpre[:, g0, :], ALU.add, reads=[bpc, b_Tpre],
               writes=[b_cref])
            tt('dve', biasT[:, BOFF[i]:BOFF[i] + nk, :],
               cref[:].unsqueeze(1).broadcast_to([128, nk, 8]), c_all[:, 0:nk, :], ALU.subtract,
               reads=[b_cref, b_call], writes=[b_biasT], nowaw=True)
            ts('dve', biasT[:, BOFF[i]:BOFF[i] + nk, :], biasT[:, BOFF[i]:BOFF[i] + nk, :], 60.0,
               None, ALU.min, reads=[b_biasT], writes=[b_biasT], nowaw=True)
        if stage == 'c1a':
            cd = dscr('call_dbg', [128, 64, 8], F32, out=True)
            sc.dma('sp', cd[:, :, :], c_all[:], reads=[b_call], writes=[Buf('cad')], sem='dbg')
            bd = dscr('bias_dbg', [128, BOFF[NT], 8], F32, out=True)
            sc.dma('sp', bd[:, :, :], biasT[:], reads=[b_biasT], writes=[Buf('bad')], sem='dbg')
            sc.barrier()
            sc.run()
            pes.close()
            return
        fqT_sb = sbf('fqT_sb', [128, 4, TOK], BF16)
        b_fqT = Buf('fqT_sb')
        sc.dma('sp', fqT_sb[:], fqT_d[:, :, :], reads=[b_fqTd], writes=[b_fqT], sem='fqld')
        KT = [sbf('KT%d' % i, [128, S], BF16) for i in range(2)]
        VA = sbf('VA', [128, 64, 130], BF16)
        b_KT = [Buf('KT%d' % i) for i in range(2)]
        b_VA = Buf('VA')
        Vp = [sbf('Vp%d' % i, [128, 64, 130], BF16) for i in range(2)]
        b_Vp = [Buf('Vp%d' % i) for i in range(2)]
        wexp = [sbf('wexp%d' % i, [128, 64, 2], F32) for i in range(2)]
        b_wexp = [Buf('wexp%d' % i) for i in range(2)]
        PT = [sbf('PT%d' % i, [128, 512], BF16) for i in range(2)]
        b_PT = [Buf('PT%d' % i) for i in range(2)]
        rcp = sbf('rcp', [128, 2], F32)
        b_rcp = Buf('rcp')
        cnt = {'s': 0, 'o': 0, 'p': 0, 'v': 0}
        for hp in range(4):
            kb = hp % 2
            load_seq_T('sp', KT[kb], b_KT[kb], 0, 128, g_d['fkT_g%d' % (hp // 2)], 256,
                       (hp % 2) * 128, 'KT%d' % kb)
            load_seq_tok('sp', VA[:], b_VA, g_d['fvA_g%d' % hp], 'VA')
            for i in range(NT):
                nk = NKI[i]
                vi = cnt['v'] % 2
                cnt['v'] += 1
                act(wexp[vi][:, 0:nk, :], biasT[:, BOFF[i]:BOFF[i] + nk, 2 * hp:2 * hp + 2], AF.Exp,
                    reads=[b_biasT], writes=[b_wexp[vi]])
                tt('dve', Vp[vi][:, 0:nk, :].rearrange('p k (e c) -> p k e c', e=2),
                   VA[:, 0:nk, :].rearrange('p k (e c) -> p k e c', e=2),
                   wexp[vi][:, 0:nk, :].unsqueeze(3).broadcast_to([128, nk, 2, 65]), ALU.mult,
                   reads=[b_VA, b_wexp[vi]], writes=[b_Vp[vi]])
                ob = 4 + 2 * (cnt['o'] % 2)
                cnt['o'] += 1
                for e in range(2):
                    h = 2 * hp + e
                    pO, bpO = psum[ob + e], psb[ob + e]
                    for g4 in range(nk // 4):
                        sbk = cnt['s'] % 4
                        cnt['s'] += 1
                        pS, bpS = psum[sbk], psb[sbk]
                        last = (g4 == nk // 4 - 1)
                        if last:
                            mm(pS[:], ident_b[:], negmT[:, i, :, :].rearrange('p a b -> p (a b)'),
                               True, False, reads=[b_ident_b, b_negmT], writes=[bpS])
                        for kk in range(4):
                            kt = g4 * 4 + kk
                            mm(pS[:, kk * 128:(kk + 1) * 128],
                               KT[kb][e * 64:(e + 1) * 64, kt * 128:(kt + 1) * 128],
                               fqT_sb[e * 64:(e + 1) * 64, hp, i * 128:(i + 1) * 128],
                               not last, (not last) or kk == 3, reads=[b_KT[kb], b_fqT],
                               writes=[bpS], skip_group_check=True)
                        pi = cnt['p'] % 2
                        cnt['p'] += 1
                        act(PT[pi][:], pS[:], AF.Exp, reads=[bpS], writes=[b_PT[pi]], scale=0.125)
                        for kk in range(4):
                            kt = g4 * 4 + kk
                            mm(pO[:, 0:65], PT[pi][:, kk * 128:(kk + 1) * 128],
                               Vp[vi][:, kt, e * 65:(e + 1) * 65], kt == 0, kt == nk - 1,
                               reads=[b_PT[pi], b_Vp[vi]], writes=[bpO])
                    sc.op('dve', lambda e_, pO=pO, e=e: e_.reciprocal(out=rcp[:, e:e + 1],
                                                                      in_=pO[:, 64:65]),
                          reads=[bpO], writes=[b_rcp])
                    ts('dve', o_sb['a'][:, i, h * 64:(h + 1) * 64], pO[:, 0:64], rcp[:, e:e + 1],
                       None, ALU.mult, reads=[bpO, b_rcp], writes=[b_o['a']], nowaw=True)
        if stage == 'c1':
            od = dscr('oa_dbg', [128, NT, 512], BF16, out=True)
            sc.dma('sp', od[:, :, :], o_sb['a'][:], reads=[b_o['a']], writes=[Buf('oad')],
                   sem='dbg')
            cd = dscr('call_dbg', [128, 64, 8], F32, out=True)
            sc.dma('sp', cd[:, :, :], c_all[:], reads=[b_call], writes=[Buf('cad')], sem='dbg')
        sc.barrier()
        sc.run()
        pes.close()

    phase_c1()
    if stage in ('c1', 'c1a'):
        es.close()
        print('ninst', sc.ninst, 'max sem', max(sc.cnt.values()), max(sc.dsem.values()), 'nsem', len(sc.semobj))
        return nc, dbg

    NBIS = 28

    def phase_c2():
        pes = ExitStack()

        def sbf(name, shape, dt):
            return pes.enter_context(nc.sbuf_tensor(un(name), list(shape), dt))
        b_cst = Buf('c2const')
        iota = sbf('iota_sb', [128, 512], F32)
        sc.dma('sp', iota[:], iota512_d.ap().partition_broadcast(128), writes=[b_cst],
               sem='const', nowaw=True)
        pow2 = sbf('pow2_sb', [128, 32], F32)
        sc.dma('sp', pow2[:], pow2_d.ap().partition_broadcast(128), writes=[b_cst], sem='const',
               nowaw=True)
        qcol = sbf('qcol', [128, NT], F32)
        sc.dma('sp', qcol[:], qposf_d.rearrange('(t p) -> p t', p=128), writes=[b_cst],
               sem='const', nowaw=True, allow_slow_non_contiguous=True)
        ident4 = sbf('ident4', [128, 4, 128], BF16)
        b_id4 = Buf('ident4')
        for k4 in range(4):
            sc.op('dve', lambda e, k4=k4: e.tensor_copy(out=ident4[:, k4, :], in_=ident_b[:]),
                  reads=[b_ident_b], writes=[b_id4], nowaw=True)
        zer = sbf('zer', [128, 272], BF16)
        b_zer = Buf('zer')
        sc.op('dve', lambda e: e.memset(zer[:], 0.0), writes=[b_zer])
        dkT2 = sbf('dkT2', [128, S], BF16)
        ikT2 = sbf('ikT2', [128, S], BF16)
        dva = sbf('dva', [128, 64, 65], BF16)
        b_dk, b_ik, b_dva = Buf('dkT2'), Buf('ikT2'), Buf('dva')
        for hh in range(2):
            load_seq_T('sp', dkT2, b_dk, hh * 64, hh * 64 + 64, g_d['dikT_g'], 128, 0, 'dkld')
            load_seq_T('sp', ikT2, b_ik, hh * 64, hh * 64 + 64, g_d['dikT_g'], 128, 64, 'ikld')
        load_seq_tok('sp', dva[:], b_dva, g_d['dvA_g'], 'dvld')
        dqT_sb = sbf('dqT_sb', [128, 4, TOK], BF16)
        iqT_sb = sbf('iqT_sb', [128, 4, TOK], BF16)
        b_dqT, b_iqT = Buf('dqT_sb'), Buf('iqT_sb')
        sc.dma('sp', dqT_sb[:], dqT_d[:, :, :], reads=[b_dqTd], writes=[b_dqT], sem='dqld')
        sc.dma('sp', iqT_sb[:], iqT_d[:, :, :], reads=[b_iqTd], writes=[b_iqT], sem='iqld')
        score = sbf('score', [128, S], F32)
        negm2 = [sbf('negm%d' % i_, [128, S], BF16) for i_ in range(2)]
        b_negm2 = [Buf('negm%d' % i_) for i_ in range(2)]
        b_score = Buf('score')
        PTd = [sbf('PTd%d' % i, [128, 512], BF16) for i in range(3)]
        b_PTd = [Buf('PTd%d' % i) for i in range(3)]
        sm = sbf('sm', [128, 8], F32)
        b_sm = Buf('sm')
        steps = sbf('steps', [128, 32], F32)
        b_steps = Buf('steps')
        cneg = sbf('cneg', [128, 512], F32)
        b_cneg = Buf('cneg')
        rc4 = sbf('rc4', [128, 8], F32)
        b_rc4 = Buf('rc4')
        cnt = {'s': 0, 'p': 0, 'o': 0}

        def sbank():
            i = cnt['s'] % 4
            cnt['s'] += 1
            return psum[i], psb[i]
        def stage1(i):
            nk = NKI[i]
            n = nk * 128
            negm, b_negm = negm2[i % 2], b_negm2[i % 2]
            for c4 in range(nk // 4):
                for h in range(8):
                    e_, hp = h % 2, h // 2
                    ps, bps = sbank()
                    mm(ps[:], iqT_sb[e_ * 64:(e_ + 1) * 64, hp, i * 128:(i + 1) * 128],
                       ikT2[e_ * 64:(e_ + 1) * 64, c4 * 512:(c4 + 1) * 512], True, True,
                       reads=[b_iqT, b_ik], writes=[bps])
                    act(ps[:], ps[:], AF.Relu, reads=[bps], writes=[bps])
                    if h == 0:
                        ts('dve', score[:, c4 * 512:(c4 + 1) * 512], ps[:], isign[:, i, 0:1], None,
                           ALU.mult, reads=[bps, b_isign], writes=[b_score])
                    else:
                        sc.op('dve', lambda e, ps=ps, c4=c4, h=h, i=i: e.scalar_tensor_tensor(
                            out=score[:, c4 * 512:(c4 + 1) * 512], in0=ps[:],
                            scalar=isign[:, i, h:h + 1], in1=score[:, c4 * 512:(c4 + 1) * 512],
                            op0=ALU.mult, op1=ALU.add), reads=[bps, b_isign, b_score],
                            writes=[b_score])
            sc.op('dve', lambda e, n=n: e.tensor_reduce(out=sm[:, 0:1], in_=score[:, 0:n], axis=AX.X,
                                                       op=ALU.max, apply_absolute_value=True),
                  reads=[b_score], writes=[b_sm])
            ts('dve', sm[:, 1:2], sm[:, 0:1], -1.0, -1e-3, ALU.mult, ALU.add, reads=[b_sm],
               writes=[b_sm])
            ts('dve', sm[:, 2:3], sm[:, 0:1], 2.0, 2e-3, ALU.mult, ALU.add, reads=[b_sm],
               writes=[b_sm])
            ts('dve', steps[:], pow2[:], sm[:, 2:3], None, ALU.mult, reads=[b_sm, b_cst],
               writes=[b_steps])
            ts('dve', sm[:, 6:7], qcol[:, i:i + 1], float(-(n - 512)), None, ALU.add,
               reads=[b_cst], writes=[b_sm])
            ts('dve', cneg[:], iota[:], sm[:, 6:7], -1e9, ALU.is_gt, ALU.mult, reads=[b_cst, b_sm],
               writes=[b_cneg])
            tt('dve', score[:, n - 512:n], score[:, n - 512:n], cneg[:], ALU.add,
               reads=[b_score, b_cneg], writes=[b_score])
            for k in range(NBIS):
                tt('dve', sm[:, 3:4], sm[:, 1:2], steps[:, k:k + 1], ALU.add, reads=[b_sm, b_steps],
                   writes=[b_sm])
                sc.op('dve', lambda e, n=n: e.tensor_scalar(
                    out=negm[:, 0:n], in0=score[:, 0:n], scalar1=sm[:, 3:4], scalar2=None,
                    op0=ALU.is_ge, op1=ALU.add, accum_out=sm[:, 4:5]),
                    reads=[b_score, b_sm], writes=[b_negm, b_sm])
                ts('dve', sm[:, 5:6], sm[:, 4:5], 255.5, steps[:, k:k + 1], ALU.is_ge, ALU.mult,
                   reads=[b_sm, b_steps], writes=[b_sm])
                tt('dve', sm[:, 1:2], sm[:, 1:2], sm[:, 5:6], ALU.add, reads=[b_sm], writes=[b_sm])
            ts('dve', negm[:, 0:n], score[:, 0:n], sm[:, 1:2], -30000.0, ALU.is_lt, ALU.mult,
               reads=[b_score, b_sm], writes=[b_negm])
        def stage2(i):
            nk = NKI[i]
            negm, b_negm = negm2[i % 2], b_negm2[i % 2]
            ob = 4 + 2 * (cnt['o'] % 2)
            cnt['o'] += 1
            for half in range(2):
                mm(psum[ob + half][:, 0:272], zer[:, 0:128], zer[:, 0:272], True, False,
                   reads=[b_zer], writes=[psb[ob + half]])
            for kt in range(nk):
                for half in range(2):
                    pS, bpS = sbank()
                    pO, bpO = psum[ob + half], psb[ob + half]
                    mm(pS[:], negm[:, kt * 128:(kt + 1) * 128],
                       ident4[:].rearrange('p a b -> p (a b)'), True, False,
                       reads=[b_negm, b_id4], writes=[bpS])
                    for hh in range(4):
                        h = 2 * hh + half
                        e_, hp = h % 2, h // 2
                        mm(pS[:, hh * 128:(hh + 1) * 128],
                           dkT2[e_ * 64:(e_ + 1) * 64, kt * 128:(kt + 1) * 128],
                           dqT_sb[e_ * 64:(e_ + 1) * 64, hp, i * 128:(i + 1) * 128], False, hh == 3,
                           reads=[b_dk, b_dqT], writes=[bpS])
                    pi = cnt['p'] % 3
                    cnt['p'] += 1
                    act(PTd[pi][:], pS[:], AF.Exp, reads=[bpS], writes=[b_PTd[pi]], scale=0.125)
                    for hh in range(4):
                        mm(pO[:, hh * 68:hh * 68 + 65], PTd[pi][:, hh * 128:(hh + 1) * 128],
                           dva[:, kt, :], False, kt == nk - 1, reads=[b_PTd[pi], b_dva],
                           writes=[bpO], skip_group_check=True)
            for half in range(2):
                pO, bpO = psum[ob + half], psb[ob + half]
                pv = pO[:, 0:272].rearrange('p (h c) -> p h c', c=68)
                sc.op('dve', lambda e, pv=pv, half=half: e.reciprocal(
                    out=rc4[:, half * 4:(half + 1) * 4], in_=pv[:, :, 64]), reads=[bpO],
                    writes=[b_rc4])
                tt('dve', o_sb['b'][:, i, :].rearrange(
                    'p (hh par d) -> p hh par d', par=2, d=64)[:, :, half, :], pv[:, :, 0:64],
                   rc4[:, half * 4:(half + 1) * 4].unsqueeze(2).broadcast_to([128, 4, 64]),
                   ALU.mult, reads=[bpO, b_rc4], writes=[b_o['b']], nowaw=True)
        for i in range(NT):
            stage1(i)
            if i >= 1:
                stage2(i - 1)
        stage2(NT - 1)
        sc.barrier()
        sc.run()
        pes.close()

    phase_c2()
    if stage == 'c2':
        od = dscr('ob_dbg', [128, NT, 512], BF16, out=True)
        sc.dma('sp', od[:, :, :], o_sb['b'][:], reads=[b_o['b']], writes=[Buf('obd')], sem='dbg')
        od2 = dscr('oa_dbg', [128, NT, 512], BF16, out=True)
        sc.dma('sp', od2[:, :, :], o_sb['a'][:], reads=[b_o['a']], writes=[Buf('oad')], sem='dbg')
        sc.barrier()
        sc.run()
        es.close()
        return nc, dbg

    def phase_c3():
        pes = ExitStack()

        def sbf(name, shape, dt):
            return pes.enter_context(nc.sbuf_tensor(un(name), list(shape), dt))
        norm_transpose = make_norm_transpose(sbf)
        gTm, b_gTm = load_gain_T(sbf, 'gT_mem', v_d['mem_norm'], 8)
        gmk = sbf('gmk', [128, 1], F32)
        b_gmk = Buf('gmk')
        sc.dma('sp', gmk[:], v_d['mem_k_norm'].rearrange('(p o) -> p o', o=1), writes=[b_gmk],
               sem='const')
        wkv = sbf('wkv', [128, 8, D], BF16)
        b_wkv = Buf('wkv')
        sc.dma('sp', wkv[:], wb_d['w_mem_kv'].rearrange('(kc p) f -> p kc f', p=128),
               reads=[wb_buf['w_mem_kv']], writes=[b_wkv], sem='wkv')
        memT = sbf('memT', [128, 8, 256], BF16)
        b_memT = Buf('memT')
        mt_sb = [sbf('memt%d' % i, [128, D], F32) for i in range(2)]
        b_mt = [Buf('memt%d' % i) for i in range(2)]
        for m in range(2):
            sc.dma('sp', mt_sb[m][:], mem_d[m * 128:(m + 1) * 128, :], writes=[b_mt[m]],
                   sem='memld')
            norm_transpose(mt_sb[m][:], b_mt[m], gTm, b_gTm, memT, b_memT, m * 128)
        kmT = sbf('kmT', [128, 4, 256], BF16)
        b_kmT = Buf('kmT')
        vma = sbf('vma', [128, 2, 4, 129], BF16)
        b_vma = Buf('vma')
        sc.op('dve', lambda e: e.memset(vma[:], 1.0), writes=[b_vma])
        sqf = sbf('sqf3', [128, 512], F32)
        b_sqf = Buf('sqf3')
        st8 = sbf('st83', [128, 16], F32)
        b_st8 = Buf('st83')
        kmb = sbf('kmb', [128, 512], BF16)
        b_kmb = Buf('kmb')
        for m in range(2):
            ps, bps = next_ps()
            for kc in range(8):
                mm(ps[:], memT[:, kc, m * 128:(m + 1) * 128], wkv[:, kc, 0:512], kc == 0, kc == 7,
                   reads=[b_memT, b_wkv], writes=[bps])
            act(sqf[:], ps[:], AF.Square, reads=[bps], writes=[b_sqf])
            sc.op('dve', lambda e: e.tensor_reduce(
                out=st8[:, 0:4], in_=sqf[:].rearrange('p (h d) -> p h d', d=128), axis=AX.X,
                op=ALU.add), reads=[b_sqf], writes=[b_st8])
            rs = rstd_from_ss(st8, b_st8, 4, 128)
            tt('dve', kmb[:].rearrange('p (h d) -> p h d', d=128),
               ps[:].rearrange('p (h d) -> p h d', d=128),
               rs.unsqueeze(2).broadcast_to([128, 4, 128]), ALU.mult, reads=[bps, b_st8],
               writes=[b_kmb])
            pt_, bpt = next_ps()
            ptv = pt_[:].bitcast(BF16)
            for h in range(4):
                tr(ptv[:, h * 128:(h + 1) * 128], kmb[:, h * 128:(h + 1) * 128], ident_b[:],
                   reads=[b_kmb, b_ident_b], writes=[bpt])
            ts('dve', kmT[:, :, m * 128:(m + 1) * 128],
               ptv[:, 0:512].rearrange('p (h k) -> p h k', k=128), gmk[:, 0:1], None, ALU.mult,
               reads=[bpt, b_gmk], writes=[b_kmT], nowaw=True)
            ps2, bps2 = next_ps()
            for kc in range(8):
                mm(ps2[:], memT[:, kc, m * 128:(m + 1) * 128], wkv[:, kc, 512:1024], kc == 0,
                   kc == 7, reads=[b_memT, b_wkv], writes=[bps2])
            act(vma[:, m, :, 0:128], ps2[:].rearrange('p (h d) -> p h d', d=128), AF.Copy,
                reads=[bps2], writes=[b_vma], nowaw=True)
        mqT_sb = sbf('mqT_sb', [128, 4, TOK], BF16)
        b_mqT = Buf('mqT_sb')
        sc.dma('sp', mqT_sb[:], mqT_d[:, :, :], reads=[b_mqTd], writes=[b_mqT], sem='mqld')
        zer = sbf('zer3', [128, 264], BF16)
        b_zer = Buf('zer3')
        sc.op('dve', lambda e: e.memset(zer[:], 0.0), writes=[b_zer])
        PTm = [sbf('PTm%d' % i, [128, 512], BF16) for i in range(2)]
        b_PTm = [Buf('PTm%d' % i) for i in range(2)]
        rc4 = sbf('rc43', [128, 4], F32)
        b_rc4 = Buf('rc43')
        cnt = {'o': 0, 'p': 0}
        for i in range(NT):
            ob = 4 + 2 * (cnt['o'] % 2)
            cnt['o'] += 1
            for half in range(2):
                pO, bpO = psum[ob + half], psb[ob + half]
                mm(pO[:, 0:264], zer[:, 0:128], zer[:, 0:264], True, False, reads=[b_zer],
                   writes=[bpO])
                pS, bpS = psum[half], psb[half]
                for hh in range(2):
                    h = 2 * half + hh
                    for m in range(2):
                        mm(pS[:, (hh * 2 + m) * 128:(hh * 2 + m + 1) * 128],
                           kmT[:, h, m * 128:(m + 1) * 128], mqT_sb[:, h, i * 128:(i + 1) * 128],
                           True, True, reads=[b_kmT, b_mqT], writes=[bpS], skip_group_check=True)
                pi = cnt['p'] % 2
                cnt['p'] += 1
                act(PTm[pi][:], pS[:], AF.Exp, reads=[bpS], writes=[b_PTm[pi]],
                    scale=float(128 ** -0.5))
                for hh in range(2):
                    h = 2 * half + hh
                    for m in range(2):
                        mm(pO[:, hh * 132:hh * 132 + 129],
                           PTm[pi][:, (hh * 2 + m) * 128:(hh * 2 + m + 1) * 128], vma[:, m, h, :],
                           False, m == 1, reads=[b_PTm[pi], b_vma], writes=[bpO],
                           skip_group_check=True)
                pv = pO[:, 0:264].rearrange('p (h c) -> p h c', c=132)
                sc.op('dve', lambda e, pv=pv, half=half: e.reciprocal(
                    out=rc4[:, half * 2:(half + 1) * 2], in_=pv[:, :, 128]), reads=[bpO],
                    writes=[b_rc4])
                tt('dve', o_sb['c'][:, i, half * 256:(half + 1) * 256].rearrange(
                    'p (h d) -> p h d', d=128), pv[:, :, 0:128],
                   rc4[:, half * 2:(half + 1) * 2].unsqueeze(2).broadcast_to([128, 2, 128]),
                   ALU.mult, reads=[bpO, b_rc4], writes=[b_o['c']], nowaw=True)
        sc.barrier()
        sc.run()
        pes.close()

    phase_c3()
    if stage == 'c3':
        od = dscr('oc_dbg', [128, NT, 512], BF16, out=True)
        sc.dma('sp', od[:, :, :], o_sb['c'][:], reads=[b_o['c']], writes=[Buf('ocd')], sem='dbg')
        od3 = dscr('ob_dbg', [128, NT, 512], BF16, out=True)
        sc.dma('sp', od3[:, :, :], o_sb['b'][:], reads=[b_o['b']], writes=[Buf('obd')], sem='dbg')
        od2 = dscr('oa_dbg', [128, NT, 512], BF16, out=True)
        sc.dma('sp', od2[:, :, :], o_sb['a'][:], reads=[b_o['a']], writes=[Buf('oad')], sem='dbg')
        sc.barrier()
        sc.run()
        es.close()
        return nc, dbg

    def phase_d():
        pes = ExitStack()

        def sbf(name, shape, dt):
            return pes.enter_context(nc.sbuf_tensor(un(name), list(shape), dt))
        norm_transpose = make_norm_transpose(sbf)
        gTx, b_gTx = load_gain_T(sbf, 'gT_mix2', v_d['mix_norm'], 8)
        wing = sbf('wing', [128, 8, 3072], BF16)
        wbr = sbf('wbr', [128, 12, D], BF16)
        wout = sbf('wout', [128, 8, D], BF16)
        b_wing, b_wbr, b_wout = Buf('wing'), Buf('wbr'), Buf('wout')
        for n3 in range(3):
            sc.dma('sp', wing[:, :, n3 * 1024:(n3 + 1) * 1024],
                   wb_d['w_in'][:, 3280 + n3 * 1024:3280 + (n3 + 1) * 1024].rearrange(
                       '(kc p) f -> p kc f', p=128), reads=[wb_buf['w_in']], writes=[b_wing],
                   sem='wing', nowaw=True)
        sc.dma('sp', wbr[:], wb_d['w_branch'].rearrange('(c p) d -> p c d', p=128),
               reads=[wb_buf['w_branch']], writes=[b_wbr], sem='wbr')
        sc.dma('sp', wout[:], wb_d['w_out'].rearrange('(c p) d -> p c d', p=128),
               reads=[wb_buf['w_out']], writes=[b_wout], sem='wout')
        uTb = sbf('uTb', [128, 8, 512], BF16)
        b_uTb = Buf('uTb')
        oT = {n_: sbf('oT_' + n_, [128, 4, 512], BF16) for n_ in 'abc'}
        b_oT = {n_: Buf('oT_' + n_) for n_ in 'abc'}
        mergedT = sbf('mergedT', [128, 8, 512], BF16)
        b_mg = Buf('mergedT')
        h1t = [sbf('h1t%d' % i, [128, D], F32) for i in range(4)]
        b_h1t = [Buf('h1t%d' % i) for i in range(4)]
        h2t = [sbf('h2t%d' % i, [128, D], F32) for i in range(2)]
        b_h2t = [Buf('h2t%d' % i) for i in range(2)]
        gs = [sbf('gs%d' % i, [128, 512], BF16) for i in range(2)]
        b_gs = [Buf('gs%d' % i) for i in range(2)]
        acc = sbf('acc', [128, 512], F32)
        tmpm = sbf('tmpm', [128, 512], F32)
        b_acc, b_tmpm = Buf('acc'), Buf('tmpm')
        cnt = {'g': 0, 'h2': 0}
        for blk in range(4):
            for tl in range(4):
                t = blk * 4 + tl
                sc.dma('sp', h1t[tl][:], h1_d[t * 128:(t + 1) * 128, :], reads=[b_h1d],
                       writes=[b_h1t[tl]], sem='h1t%d' % tl)
                norm_transpose(h1t[tl][:], b_h1t[tl], gTx, b_gTx, uTb, b_uTb, tl * 128)
                for n_ in 'abc':
                    ps, bps = next_ps()
                    psv = ps[:].bitcast(BF16)
                    for wc in range(4):
                        tr(psv[:, wc * 128:(wc + 1) * 128], o_sb[n_][:, t, wc * 128:(wc + 1) * 128],
                           ident_b[:], reads=[b_o[n_], b_ident_b], writes=[bps])
                    act(oT[n_][:, :, tl * 128:(tl + 1) * 128],
                        psv[:, 0:512].rearrange('p (w k) -> p w k', k=128), AF.Copy, reads=[bps],
                        writes=[b_oT[n_]], nowaw=True)
            for dc in range(8):
                for n3, n_ in enumerate('abc'):
                    pg, bpg = next_ps()
                    for kc in range(8):
                        mm(pg[:], wing[:, kc, n3 * 1024 + dc * 128:n3 * 1024 + (dc + 1) * 128],
                           uTb[:, kc, :], kc == 0, kc == 7, reads=[b_wing, b_uTb], writes=[bpg])
                    gi = cnt['g'] % 2
                    cnt['g'] += 1
                    act(gs[gi][:], pg[:], AF.Sigmoid, reads=[bpg], writes=[b_gs[gi]])
                    pp, bpp = next_ps()
                    for wc in range(4):
                        mm(pp[:], wbr[:, n3 * 4 + wc, dc * 128:(dc + 1) * 128], oT[n_][:, wc, :],
                           wc == 0, wc == 3, reads=[b_wbr, b_oT[n_]], writes=[bpp])
                    if n3 == 0:
                        tt('dve', acc[:], pp[:], gs[gi][:], ALU.mult, reads=[bpp, b_gs[gi]],
                           writes=[b_acc])
                    else:
                        tt('dve', tmpm[:], pp[:], gs[gi][:], ALU.mult, reads=[bpp, b_gs[gi]],
                           writes=[b_tmpm])
                        if n3 == 1:
                            tt('dve', acc[:], acc[:], tmpm[:], ALU.add, reads=[b_acc, b_tmpm],
                               writes=[b_acc])
                        else:
                            tt('dve', mergedT[:, dc, :], acc[:], tmpm[:], ALU.add,
                               reads=[b_acc, b_tmpm], writes=[b_mg], nowaw=True)
            for tl in range(4):
                t = blk * 4 + tl
                hi = cnt['h2'] % 2
                cnt['h2'] += 1
                for dh in range(2):
                    po, bpo = next_ps()
                    for dc in range(8):
                        mm(po[:], mergedT[:, dc, tl * 128:(tl + 1) * 128],
                           wout[:, dc, dh * 512:(dh + 1) * 512], dc == 0, dc == 7,
                           reads=[b_mg, b_wout], writes=[bpo])
                    tt('dve', h2t[hi][:, dh * 512:(dh + 1) * 512], po[:],
                       h1t[tl][:, dh * 512:(dh + 1) * 512], ALU.add, reads=[bpo, b_h1t[tl]],
                       writes=[b_h2t[hi]])
                sc.dma('sp', h2_d[t * 128:(t + 1) * 128, :], h2t[hi][:], reads=[b_h2t[hi]],
                       writes=[b_h2d], sem='h2o%d' % hi, nowaw=True)
        sc.barrier()
        sc.run()
        pes.close()

    phase_d()
    ffn_phase(h2_d, b_h2d, y_d, b_yd, 'ffn2_norm', 'ffn2_w_gate', 'ffn2_w_up', 'ffn2_w_down')
    es.close()
    print('ninst', sc.ninst, 'max sem', max(sc.cnt.values()), max(sc.dsem.values()), 'nsem', len(sc.semobj))
    return nc, dbg


_NC_CACHE = {}


def _core_rows(cid):
    b, j = divmod(cid, 4)
    tiles = zig_tiles(j)
    rows = np.concatenate([np.arange(g * 128, (g + 1) * 128) for g in tiles])
    return b, rows


def kernel(**inputs):
    stage = inputs.pop('_stage', 'full')
    if stage not in _NC_CACHE:
        _NC_CACHE[stage] = build(stage)
    nc, dbg = _NC_CACHE[stage]
    f = lambda k: np.asarray(inputs[k], dtype=np.float32)
    x = f('x')
    mem = f('mem')
    pos = np.asarray(inputs['positions']).astype(np.int32)
    ident = np.eye(128, dtype=np.float32)
    invf = (10000.0 ** (-np.arange(0, 64, 2, dtype=np.float32) / 64)).astype(np.float32)
    tri = np.triu(np.ones((128, 128), np.float32))
    kposc = (np.arange(64, dtype=np.float32)[None, :] * 128 + np.arange(128, dtype=np.float32)[:, None])
    pow2 = (0.5 ** np.arange(1, 33)).astype(np.float32)
    shared = {'ident': ident, 'invf': invf, 'tri': tri, 'kposc': np.ascontiguousarray(kposc),
              'pow2': pow2, 'iota512': np.arange(512, dtype=np.float32)}
    for k, shp in WNAMES.items():
        shared[k] = np.ascontiguousarray(f(k)[0].reshape(shp))
    for k in VNAMES:
        shared[k] = np.ascontiguousarray(f(k)[0])
    in_maps = []
    for cid in range(NCORES):
        b, rows = _core_rows(cid)
        m = dict(shared)
        m['x'] = np.ascontiguousarray(x[b][rows])
        m['pos'] = np.ascontiguousarray(pos[b][rows])
        m['qposf'] = rows.astype(np.float32)
        m['mem'] = np.ascontiguousarray(mem[b])
        in_maps.append(m)
    res = run_bass_kernel_spmd(nc, in_maps, core_ids=list(range(NCORES)))
    if stage != 'full':
        return res.results
    out = np.zeros((2, S, D), np.float32)
    for cid in range(NCORES):
        b, rows = _core_rows(cid)
        out[b][rows] = res.results[cid]['y']
    return out
```

```python
import numpy as np
from contextlib import ExitStack
import concourse.bass as bass
import concourse.mybir as mybir
from concourse.bass_utils import run_bass_kernel_spmd

F32 = mybir.dt.float32
BF16 = mybir.dt.bfloat16
I32 = mybir.dt.int32
AF = mybir.ActivationFunctionType
ALU = mybir.AluOpType
AX = mybir.AxisListType

NCORES = 8
D = 1024
S = 8192
TOK = 2048
NT = 16
DFF = 2816
NFC = 22
DIN = 6352
EPS = 1e-6
ENG = ['pe', 'act', 'dve', 'pool', 'sp']
SAME_ENG_SYNC = True
MAXQ = {'sp': 6, 'pool': 4, 'act': 6}


class Buf:
    __slots__ = ('name', 'w', 'r')

    def __init__(self, name):
        self.name = name
        self.w = {}
        self.r = {}


class Sched:
    def __init__(self, nc, es):
        self.nc = nc
        self.es = es
        self.semobj = {}
        self.cnt = {}
        self.prog = {e: [] for e in ENG}
        self.waited = {e: {} for e in ENG}
        self.dsem = {}
        self.epoch = 0
        self.ekey = {}
        for e in ENG:
            k = e + '@0'
            self.ekey[e] = k
            self.semobj[k] = es.enter_context(nc.semaphore('sem_' + e + '_0'))
            self.cnt[k] = 0
        self.ninst = {e: 0 for e in ENG}
        self.outq = {}

    def _deps(self, reads, writes, nowaw=False):
        deps = {}

        def add(k, v):
            if deps.get(k, 0) < v:
                deps[k] = v
        for b in reads:
            for k, v in b.w.items():
                add(k, v)
        for b in writes:
            if not nowaw:
                for k, v in b.w.items():
                    add(k, v)
            for k, v in b.r.items():
                add(k, v)
        return deps

    def _mark(self, key, v, reads, writes, nowaw):
        for b in reads:
            if b.r.get(key, 0) < v:
                b.r[key] = v
        for b in writes:
            if nowaw:
                if b.w.get(key, 0) < v:
                    b.w[key] = v
            else:
                b.w = {key: v}
                b.r = {}

    def _emit_waits(self, eng, deps):
        for k, v in deps.items():
            if k.split('@')[0] == eng and (eng == 'pe' or not SAME_ENG_SYNC):
                continue
            if k in self.dsem:
                v = self.dsem[k]
            if self.waited[eng].get(k, 0) >= v:
                continue
            self.waited[eng][k] = v
            sem = self.semobj[k]
            self.prog[eng].append(lambda e, sem=sem, v=v: e.wait_ge(sem, v))
            self.ninst[eng] += 1

    def op(self, eng, fn, reads=(), writes=(), nowaw=False):
        deps = self._deps(reads, writes, nowaw)
        self._emit_waits(eng, deps)
        key = self.ekey[eng]
        self.cnt[key] += 1
        n = self.cnt[key]
        sem = self.semobj[key]
        self.prog[eng].append(lambda e, fn=fn, sem=sem: fn(e).then_inc(sem, 1))
        self.ninst[eng] += 1
        self._mark(key, n, reads, writes, nowaw)

    def dma(self, q, out, in_, reads=(), writes=(), sem='dma', nowaw=False, **kw):
        deps = self._deps(reads, writes, nowaw)
        self._emit_waits(q, deps)
        if sem not in self.dsem:
            self.semobj[sem] = self.es.enter_context(self.nc.semaphore('dsem_' + sem))
            self.dsem[sem] = 0
        fifo = self.outq.setdefault(q, [])
        while len(fifo) >= MAXQ[q]:
            osem = fifo[0][0]
            ov = self.dsem[osem]
            fifo[:] = [x for x in fifo if x[0] != osem]
            if self.waited[q].get(osem, 0) < ov:
                self.waited[q][osem] = ov
                so = self.semobj[osem]
                self.prog[q].append(lambda e, so=so, ov=ov: e.wait_ge(so, ov))
                self.ninst[q] += 1
        self.dsem[sem] += 16
        v = self.dsem[sem]
        fifo.append((sem, v))
        s = self.semobj[sem]
        self.prog[q].append(lambda e, out=out, in_=in_, kw=kw, s=s:
                            e.dma_start(out=out, in_=in_, **kw).then_inc(s, 16))
        self.ninst[q] += 1
        self._mark(sem, v, reads, writes, nowaw)

    def custom(self, q, fn, reads=(), writes=(), sem='cc', inc=1):
        deps = self._deps(reads, writes)
        self._emit_waits(q, deps)
        if sem not in self.dsem:
            self.semobj[sem] = self.es.enter_context(self.nc.semaphore('dsem_' + sem))
            self.dsem[sem] = 0
        self.dsem[sem] += inc
        v = self.dsem[sem]
        s = self.semobj[sem]
        self.prog[q].append(lambda e, fn=fn, s=s, inc=inc: fn(e).then_inc(s, inc))
        self._mark(sem, v, reads, writes, False)

    def barrier(self):
        for e in ENG:
            deps = {self.ekey[k]: self.cnt[self.ekey[k]] for k in ENG
                    if self.cnt[self.ekey[k]] > 0 and k != e}
            for k, v in self.dsem.items():
                if v > 0:
                    deps[k] = v
            self._emit_waits(e, deps)

    def new_epoch(self):
        self.epoch += 1
        for e in ENG:
            k = '%s@%d' % (e, self.epoch)
            self.ekey[e] = k
            self.semobj[k] = self.es.enter_context(self.nc.semaphore('sem_%s_%d' % (e, self.epoch)))
            self.cnt[k] = 0

    def run(self):
        nc = self.nc
        with nc.Block() as block:
            @block.tensor
            def _(e):
                for f in self.prog['pe']:
                    f(e)

            @block.scalar
            def _(e):
                for f in self.prog['act']:
                    f(e)

            @block.vector
            def _(e):
                for f in self.prog['dve']:
                    f(e)

            @block.gpsimd
            def _(e):
                for f in self.prog['pool']:
                    f(e)

            @block.sync
            def _(e):
                for f in self.prog['sp']:
                    f(e)
        self.prog = {e: [] for e in ENG}
        self.new_epoch()


class Ctx:
    pass


def zig_tiles(j):
    out = []
    for m in range(8):
        out.append(8 * m + j)
        out.append(8 * m + 7 - j)
    return out


SPLITS = dict(fq=(0, 512), fk=(512, 1024), fv=(1024, 1536), ff=(1536, 1544), dq=(1544, 2056),
              dk=(2056, 2120), dv=(2120, 2184), iq=(2184, 2696), ik=(2696, 2760),
              iw=(2760, 2768), mq=(2768, 3280), g=(3280, 6352))
KROWS = 1225
WNAMES = {
    'ffn1_w_gate': (D, DFF), 'ffn1_w_up': (D, DFF), 'ffn1_w_down': (DFF, D),
    'w_in': (D, DIN), 'w_mem_kv': (D, D), 'w_branch': (3 * 512, D), 'w_out': (D, D),
    'ffn2_w_gate': (D, DFF), 'ffn2_w_up': (D, DFF), 'ffn2_w_down': (DFF, D),
}
VNAMES = {'ffn1_norm': D, 'mix_norm': D, 'mem_norm': D, 'ffn2_norm': D, 'b_forget': 8,
          'fox_q_norm': 64, 'fox_k_norm': 64, 'dsa_q_norm': 64, 'dsa_k_norm': 64,
          'mem_q_norm': 128, 'mem_k_norm': 128}
STAGES = ['ffn1', 'a2', 'full']


def build(stage='full'):
    nc = bass.Bass("TRN2", target_bir_lowering=False)
    es = ExitStack()
    sc = Sched(nc, es)
    c = Ctx()
    dbg = {}
    uq = [0]

    def un(name):
        uq[0] += 1
        return '%s_%d' % (name, uq[0])

    def din(name, shape, dt=F32):
        return nc.dram_tensor(name, list(shape), dt, kind="ExternalInput")

    def dscr(name, shape, dt, out=False):
        if out:
            dbg[name] = True
        return nc.dram_tensor(name, list(shape), dt, kind="ExternalOutput" if out else "Internal")

    x_d = din('x', [TOK, D])
    pos_d = din('pos', [TOK], I32)
    qposf_d = din('qposf', [TOK])
    ident_d = din('ident', [128, 128])
    invf_d = din('invf', [32])
    mem_d = din('mem', [256, D])
    tri_d = din('tri', [128, 128])
    kposc_d = din('kposc', [128, 64])
    pow2_d = din('pow2', [32])
    iota512_d = din('iota512', [512])
    w_d = {k: din(k, s) for k, s in WNAMES.items()}
    v_d = {k: din(k, [n]) for k, n in VNAMES.items()}
    h1_d = dscr('h1s', [TOK, D], F32, out=(stage == 'ffn1'))
    b_h1d = Buf('h1s')
    A2O = (stage == 'a2')
    fqT_d = dscr('fqT_s', [128, 4, TOK], BF16, out=A2O)
    dqT_d = dscr('dqT_s', [128, 4, TOK], BF16, out=A2O)
    iqT_d = dscr('iqT_s', [128, 4, TOK], BF16, out=A2O)
    mqT_d = dscr('mqT_s', [128, 4, TOK], BF16, out=A2O)
    fkT_in = [dscr('fkT_in%d' % i, [256, TOK], BF16, out=A2O) for i in range(2)]
    fvA_in = [dscr('fvA_in%d' % i, [TOK, 130], BF16, out=A2O) for i in range(4)]
    dikT_in = dscr('dikT_in', [128, TOK], BF16, out=A2O)
    dvA_in = dscr('dvA_in', [TOK, 65], BF16, out=A2O)
    lin_d = dscr('lin', [TOK, 8], F32, out=A2O)
    b_fqTd, b_dqTd, b_iqTd, b_mqTd, b_kind, b_lind = [Buf(n) for n in
                                                    ['fqTd', 'dqTd', 'iqTd', 'mqTd', 'kind', 'lind']]
    o = 0
    kin_dikT = dikT_in
    kin_dvA = dvA_in
    GSH = {'lall': (TOK, 8, F32), 'fkT_g0': (256, TOK, BF16), 'fkT_g1': (256, TOK, BF16),
           'fvA_g0': (TOK, 130, BF16), 'fvA_g1': (TOK, 130, BF16), 'fvA_g2': (TOK, 130, BF16),
           'fvA_g3': (TOK, 130, BF16), 'dikT_g': (128, TOK, BF16), 'dvA_g': (TOK, 65, BF16)}
    g_d = {k: dscr(k, [4 * r, c_], dt) for k, (r, c_, dt) in GSH.items()}
    y_d = nc.dram_tensor('y', [TOK, D], F32, kind="ExternalOutput")
    b_yd = Buf('y')
    h2_d = dscr('h2s', [TOK, D], F32)
    b_h2d = Buf('h2s')
    wb_d = {k: dscr(k + '_bf', s_, BF16) for k, s_ in WNAMES.items()}
    wb_buf = {k: Buf(k + '_bf') for k in WNAMES}

    def sbg(name, shape, dt):
        return es.enter_context(nc.sbuf_tensor(un(name), list(shape), dt))

    ident_f = sbg('ident_f', [128, 128], F32)
    ident_b = sbg('ident_b', [128, 128], BF16)
    b_ident_f = Buf('ident_f')
    b_ident_b = Buf('ident_b')
    sc.dma('sp', ident_f[:], ident_d[:, :], writes=[b_ident_f], sem='ident')
    sc.op('dve', lambda e: e.tensor_copy(out=ident_b[:], in_=ident_f[:]),
          reads=[b_ident_f], writes=[b_ident_b])
    isign = sbg('isign', [128, NT, 8], F32)
    b_isign = Buf('isign')
    es_a = ExitStack()
    uT = es_a.enter_context(nc.sbuf_tensor('uT', [128, 8, TOK], BF16))
    b_uT = Buf('uT')

    def cast_weight(name):
        src = w_d[name].reshape([-1, 1024])
        dst = wb_d[name].reshape([-1, 1024])
        rows = src.shape[0]
        r0 = 0
        while r0 < rows:
            r1 = min(rows, r0 + 2048)
            sc.dma('pool', dst[r0:r1, :], src[r0:r1, :], writes=[wb_buf[name]], sem='wcast')
            r0 = r1

    for k in WNAMES:
        cast_weight(k)

    psum = [es.enter_context(nc.psum_tensor('ps%d' % i, [128, 512], F32)) for i in range(8)]
    psb = [Buf('ps%d' % i) for i in range(8)]
    c.ps_i = 0

    def next_ps():
        i = c.ps_i
        c.ps_i = (i + 1) % 8
        return psum[i], psb[i]

    def ts(eng, out, in0, s1, s2, op0, op1=None, reads=(), writes=(), accum=None, nowaw=False):
        if op1 is None:
            sc.op(eng, lambda e: e.tensor_scalar(out=out, in0=in0, scalar1=s1, scalar2=None,
                                                 op0=op0, accum_out=accum), reads, writes, nowaw)
        else:
            sc.op(eng, lambda e: e.tensor_scalar(out=out, in0=in0, scalar1=s1, scalar2=s2,
                                                 op0=op0, op1=op1, accum_out=accum), reads, writes,
                  nowaw)

    def tt(eng, out, in0, in1, op, reads=(), writes=(), nowaw=False):
        sc.op(eng, lambda e: e.tensor_tensor(out=out, in0=in0, in1=in1, op=op), reads, writes,
              nowaw)

    def act(out, in_, func, reads=(), writes=(), nowaw=False, **kw):
        sc.op('act', lambda e: e.activation(out=out, in_=in_, func=func, **kw), reads, writes,
              nowaw)

    def mm(out, lhsT, rhs, start, stop, reads=(), writes=(), **kw):
        sc.op('pe', lambda e: e.matmul(out=out, lhsT=lhsT, rhs=rhs, start=start, stop=stop, **kw),
              reads, writes)

    def tr(out, in_, ident, reads=(), writes=()):
        sc.op('pe', lambda e: e.transpose(out=out, in_=in_, identity=ident), reads, writes)

    def load_gain_T(sbf, name, dram, ncol):
        t = sbf(name, [128, ncol], F32)
        b = Buf(name)
        sc.dma('sp', t[:], dram.rearrange('(kc p) -> p kc', p=128), writes=[b], sem='const',
               allow_slow_non_contiguous=True)
        return t, b

    def load_bc(sbf, name, dram_ap, n):
        t = sbf(name, [128, n], F32)
        b = Buf(name)
        sc.dma('sp', t[:], dram_ap.partition_broadcast(128), writes=[b], sem='const')
        return t, b

    def rstd_from_ss(st, bst, n, dim):
        ts('dve', st[:, n:2 * n], st[:, 0:n], 1.0 / dim, EPS, ALU.mult, ALU.add,
           reads=[bst], writes=[bst])
        act(st[:, 2 * n:3 * n], st[:, n:2 * n], AF.Sqrt, reads=[bst], writes=[bst])
        sc.op('dve', lambda e: e.reciprocal(out=st[:, 3 * n:4 * n], in_=st[:, 2 * n:3 * n]),
              reads=[bst], writes=[bst])
        return st[:, 3 * n:4 * n]

    def make_norm_transpose(sbf):
        sq_junk = sbf('sq_junk', [128, D], BF16)
        b_sq_junk = Buf('sq_junk')
        xn_bf = [sbf('xn_bf%d' % i, [128, D], BF16) for i in range(2)]
        b_xn_bf = [Buf('xn_bf%d' % i) for i in range(2)]
        stat = [sbf('nstat%d' % i, [128, 4], F32) for i in range(2)]
        b_stat = [Buf('nstat%d' % i) for i in range(2)]
        state = {'i': 0}

        def norm_transpose(src_tile, b_src, gT, b_gT, dstT, b_dst, col0):
            i = state['i']
            state['i'] ^= 1
            st, bst = stat[i], b_stat[i]
            xb, bxb = xn_bf[i], b_xn_bf[i]
            act(sq_junk[:], src_tile, AF.Square, reads=[b_src], writes=[b_sq_junk, bst],
                accum_out=st[:, 0:1])
            rs = rstd_from_ss(st, bst, 1, D)
            act(xb[:], src_tile, AF.Copy, reads=[b_src, bst], writes=[bxb], scale=rs)
            ps, bps = next_ps()
            psv = ps[:].bitcast(BF16)
            for kc in range(8):
                tr(psv[:, kc * 128:(kc + 1) * 128], xb[:, kc * 128:(kc + 1) * 128], ident_b[:],
                   reads=[bxb, b_ident_b], writes=[bps])
            for kc in range(8):
                ts('dve', dstT[:, kc, col0:col0 + 128], psv[:, kc * 128:(kc + 1) * 128],
                   gT[:, kc:kc + 1], None, ALU.mult, reads=[bps, b_gT], writes=[b_dst],
                   nowaw=True)
        return norm_transpose

    HALF = 1024

    def ffn_phase(src_d, b_srcd, dst_d, b_dstd, gname, wgn, wun, wdn, post_gname=None,
                  postT=None, b_postT=None):
        pes = ExitStack()

        def sbf(name, shape, dt):
            return pes.enter_context(nc.sbuf_tensor(un(name), list(shape), dt))
        norm_transpose = make_norm_transpose(sbf)
        gT, b_gT = load_gain_T(sbf, 'gT_' + gname, v_d[gname], 8)
        if post_gname:
            pgT, b_pgT = load_gain_T(sbf, 'gT_' + post_gname, v_d[post_gname], 8)
        xt = [sbf('xt%d' % i, [128, D], F32) for i in range(3)]
        b_xt = [Buf('xt%d' % i) for i in range(3)]
        xnT = sbf('xnT', [128, 8, HALF], BF16)
        b_xnT = Buf('xnT')
        actT = sbf('actT', [128, NFC, HALF], BF16)
        b_actT = [Buf('actT%d' % i) for i in range(NFC)]
        wd_sb = sbf('wd_sb', [128, NFC, D], BF16)
        b_wd = Buf('wd_sb')
        WC = 256
        wg_sb = [sbf('wg_sb%d' % i, [128, 8, WC], BF16) for i in range(2)]
        wu_sb = [sbf('wu_sb%d' % i, [128, 8, WC], BF16) for i in range(2)]
        b_wg = [Buf('wg%d' % i) for i in range(2)]
        b_wu = [Buf('wu%d' % i) for i in range(2)]
        sg_sb = [sbf('sg_sb%d' % i, [128, 512], BF16) for i in range(2)]
        b_sg = [Buf('sg%d' % i) for i in range(2)]
        ho = [sbf('ho%d' % i, [128, D], F32) for i in range(2)]
        b_ho = [Buf('ho%d' % i) for i in range(2)]
        st = {'xt': 0, 'w': 0, 'sg': 0, 'ho': 0}
        wg_d, wu_d, wdd = wb_d[wgn], wb_d[wun], wb_d[wdn]
        sc.dma('sp', wd_sb[:], wdd.rearrange('(fc p) d -> p fc d', p=128),
               reads=[wb_buf[wdn]], writes=[b_wd], sem='wd')
        for half in range(TOK // HALF):
            for t in range(HALF // 128):
                i = st['xt']
                st['xt'] = (i + 1) % 3
                r0 = half * HALF + t * 128
                sc.dma('sp', xt[i][:], src_d[r0:r0 + 128, :], reads=[b_srcd], writes=[b_xt[i]],
                       sem='xt%d' % i)
                norm_transpose(xt[i][:], b_xt[i], gT, b_gT, xnT, b_xnT, t * 128)
            for wc in range(DFF // WC):
                wi = st['w']
                st['w'] ^= 1
                sc.dma('sp', wg_sb[wi][:],
                       wg_d[:, wc * WC:(wc + 1) * WC].rearrange('(kc p) f -> p kc f', p=128),
                       reads=[wb_buf[wgn]], writes=[b_wg[wi]], sem='wg%d' % wi)
                sc.dma('sp', wu_sb[wi][:],
                       wu_d[:, wc * WC:(wc + 1) * WC].rearrange('(kc p) f -> p kc f', p=128),
                       reads=[wb_buf[wun]], writes=[b_wu[wi]], sem='wu%d' % wi)
                for fl in range(WC // 128):
                    fc = wc * (WC // 128) + fl
                    for tb in range(HALF // 512):
                        pg, bpg = next_ps()
                        pu, bpu = next_ps()
                        for kc in range(8):
                            mm(pg[:], wg_sb[wi][:, kc, fl * 128:(fl + 1) * 128],
                               xnT[:, kc, tb * 512:(tb + 1) * 512], kc == 0, kc == 7,
                               reads=[b_wg[wi], b_xnT], writes=[bpg])
                        for kc in range(8):
                            mm(pu[:], wu_sb[wi][:, kc, fl * 128:(fl + 1) * 128],
                               xnT[:, kc, tb * 512:(tb + 1) * 512], kc == 0, kc == 7,
                               reads=[b_wu[wi], b_xnT], writes=[bpu])
                        si = st['sg']
                        st['sg'] ^= 1
                        act(sg_sb[si][:], pg[:], AF.Silu, reads=[bpg], writes=[b_sg[si]])
                        tt('dve', actT[:, fc, tb * 512:(tb + 1) * 512], pu[:], sg_sb[si][:],
                           ALU.mult, reads=[bpu, b_sg[si]], writes=[b_actT[fc]])
            for t in range(HALF // 128):
                r0 = half * HALF + t * 128
                i = st['xt']
                st['xt'] = (i + 1) % 3
                sc.dma('sp', xt[i][:], src_d[r0:r0 + 128, :], reads=[b_srcd], writes=[b_xt[i]],
                       sem='xt%d' % i)
                hi = st['ho']
                st['ho'] ^= 1
                for dh in range(2):
                    pd, bpd = next_ps()
                    for fc in range(NFC):
                        mm(pd[:], actT[:, fc, t * 128:(t + 1) * 128],
                           wd_sb[:, fc, dh * 512:(dh + 1) * 512], fc == 0, fc == NFC - 1,
                           reads=[b_actT[fc], b_wd], writes=[bpd])
                    sc.op('dve', lambda e, pd=pd, hi=hi, i=i, dh=dh: e.scalar_tensor_tensor(
                        out=ho[hi][:, dh * 512:(dh + 1) * 512], in0=pd[:], scalar=0.5,
                        in1=xt[i][:, dh * 512:(dh + 1) * 512], op0=ALU.mult, op1=ALU.add),
                        reads=[bpd, b_xt[i]], writes=[b_ho[hi]])
                sc.dma('sp', dst_d[r0:r0 + 128, :], ho[hi][:], reads=[b_ho[hi]], writes=[b_dstd],
                       sem='ho%d' % hi, nowaw=True)
                if post_gname:
                    norm_transpose(ho[hi][:], b_ho[hi], pgT, b_pgT, postT, b_postT, r0)
        sc.barrier()
        sc.run()
        pes.close()

    b_x = Buf('x_d')
    ffn_phase(x_d, b_x, h1_d, b_h1d, 'ffn1_norm', 'ffn1_w_gate', 'ffn1_w_up', 'ffn1_w_down',
              post_gname='mix_norm', postT=uT, b_postT=b_uT)

    if stage == 'ffn1':
        es.close()
        print('ninst', sc.ninst, 'max sem', max(sc.cnt.values()), max(sc.dsem.values()), 'nsem', len(sc.semobj))
        return nc, dbg

    def phase_a2():
        pes = ExitStack()

        def sbf(name, shape, dt):
            return pes.enter_context(nc.sbuf_tensor(un(name), list(shape), dt))
        NG = 3280
        win_sb = sbf('win_sb', [128, 8, NG], BF16)
        b_win = Buf('win_sb')
        wv = wb_d['w_in']

        def ldw(dst0, c0, c1):
            sc.dma('sp', win_sb[:, :, dst0:dst0 + (c1 - c0)],
                   wv[:, c0:c1].rearrange('(kc p) f -> p kc f', p=128),
                   reads=[wb_buf['w_in']], writes=[b_win], sem='win')
        GOFF = {k: v[0] for k, v in SPLITS.items()}
        for q4 in range(4):
            ldw(q4 * 820, q4 * 820, (q4 + 1) * 820)

        gq2 = sbf('gq2', [128, 1], F32)
        gk2 = sbf('gk2', [128, 1], F32)
        gmq = sbf('gmq', [128, 1], F32)
        b_g2 = Buf('g2')
        for hh in range(2):
            sc.dma('sp', gq2[hh * 64:(hh + 1) * 64, :],
                   v_d['fox_q_norm'].rearrange('(p o) -> p o', o=1), writes=[b_g2], sem='const')
            sc.dma('sp', gk2[hh * 64:(hh + 1) * 64, :],
                   v_d['fox_k_norm'].rearrange('(p o) -> p o', o=1), writes=[b_g2], sem='const')
        sc.dma('sp', gmq[:], v_d['mem_q_norm'].rearrange('(p o) -> p o', o=1), writes=[b_g2],
               sem='const')
        gdq, b_gdq = load_bc(sbf, 'gdq', v_d['dsa_q_norm'].ap(), 64)
        gdk, b_gdk = load_bc(sbf, 'gdk', v_d['dsa_k_norm'].ap(), 64)
        bfor, b_bfor = load_bc(sbf, 'bfor', v_d['b_forget'].ap(), 8)
        invf, b_invf = load_bc(sbf, 'invf_sb', invf_d.ap(), 32)
        posi = sbf('posi', [128, NT], I32)
        posf = sbf('posf', [128, NT], F32)
        b_pos = Buf('pos')
        sc.dma('sp', posi[:], pos_d.rearrange('(t p) -> p t', p=128), writes=[b_pos], sem='const',
               allow_slow_non_contiguous=True)
        sc.op('dve', lambda e: e.tensor_copy(out=posf[:], in_=posi[:]), reads=[b_pos],
              writes=[b_pos])
        ang = sbf('ang', [128, NT, 64], F32)
        kk = sbf('kk', [128, NT, 64], F32)
        kki = sbf('kki', [128, NT, 64], I32)
        cs = sbf('cs', [128, NT, 64], F32)
        b_ang = Buf('ang')
        b_cs = Buf('cs')
        TWO_PI = float(2 * np.pi)
        for t in range(NT):
            ts('dve', ang[:, t, 32:64], invf[:], posf[:, t:t + 1], None, ALU.mult,
               reads=[b_invf, b_pos], writes=[b_ang])
        ts('dve', ang[:, :, 0:32], ang[:, :, 32:64], float(np.pi / 2), None, ALU.add,
           reads=[b_ang], writes=[b_ang])
        ts('dve', kk[:], ang[:], 1.0 / TWO_PI, 0.5, ALU.mult, ALU.add, reads=[b_ang], writes=[b_ang])
        sc.op('dve', lambda e: e.tensor_copy(out=kki[:], in_=kk[:]), reads=[b_ang], writes=[b_ang])
        sc.op('dve', lambda e: e.tensor_copy(out=kk[:], in_=kki[:]), reads=[b_ang], writes=[b_ang])
        sc.op('dve', lambda e: e.scalar_tensor_tensor(out=ang[:], in0=kk[:], scalar=-TWO_PI,
                                                      in1=ang[:], op0=ALU.mult, op1=ALU.add),
              reads=[b_ang], writes=[b_ang])
        ts('dve', kk[:], ang[:], float(-np.pi), TWO_PI, ALU.is_lt, ALU.mult, reads=[b_ang],
           writes=[b_ang])
        tt('dve', ang[:], ang[:], kk[:], ALU.add, reads=[b_ang], writes=[b_ang])
        ts('dve', kk[:], ang[:], float(np.pi), -TWO_PI, ALU.is_gt, ALU.mult, reads=[b_ang],
           writes=[b_ang])
        tt('dve', ang[:], ang[:], kk[:], ALU.add, reads=[b_ang], writes=[b_ang])
        ts('dve', ang[:], ang[:], float(np.pi), float(-np.pi), ALU.min, ALU.max, reads=[b_ang],
           writes=[b_ang])
        act(cs[:], ang[:], AF.Sin, reads=[b_ang], writes=[b_cs])
        abq = sbf('abq', [128, NT, 4, 32], F32)
        abk = sbf('abk', [128, NT, 4, 32], F32)
        b_abq = Buf('abq')
        b_abk = Buf('abk')
        for (ab, bab, g, bg) in [(abq, b_abq, gdq, b_gdq), (abk, b_abk, gdk, b_gdk)]:
            g1 = g[:, 0:32].unsqueeze(1).broadcast_to([128, NT, 32])
            g2 = g[:, 32:64].unsqueeze(1).broadcast_to([128, NT, 32])
            tt('dve', ab[:, :, 0, :], cs[:, :, 0:32], g1, ALU.mult, reads=[b_cs, bg], writes=[bab])
            tt('dve', ab[:, :, 1, :], cs[:, :, 32:64], g2, ALU.mult, reads=[b_cs, bg], writes=[bab])
            tt('dve', ab[:, :, 2, :], cs[:, :, 0:32], g2, ALU.mult, reads=[b_cs, bg], writes=[bab])
            tt('dve', ab[:, :, 3, :], cs[:, :, 32:64], g1, ALU.mult, reads=[b_cs, bg], writes=[bab])

        sqf = sbf('sqf', [128, 512], F32)
        b_sqf = Buf('sqf')
        st8 = sbf('st8', [128, 32], F32)
        b_st8 = Buf('st8')
        r1 = sbf('r1', [128, 8, 32], F32)
        r2 = sbf('r2', [128, 8, 32], F32)
        ro = sbf('ro', [128, 8, 64], F32)
        b_r1, b_r2, b_ro = Buf('r1'), Buf('r2'), Buf('ro')
        tok_bf = [sbf('tok_bf%d' % i, [128, 512], BF16) for i in range(2)]
        b_tok_bf = [Buf('tok_bf%d' % i) for i in range(2)]
        tT = [sbf('tT%d' % i, [128, 4, 128], BF16) for i in range(2)]
        b_tT = [Buf('tT%d' % i) for i in range(2)]
        pack6 = sbf('pack6', [128, 128], BF16)
        b_pack6 = Buf('pack6')
        p6T = sbf('p6T', [128, 128], BF16)
        b_p6T = Buf('p6T')
        dvb = sbf('dvb', [128, 65], BF16)
        b_dvb = Buf('dvb')
        fvst = sbf('fvst', [128, 8, 65], BF16)
        b_fvst = Buf('fvst')
        sc.op('dve', lambda e: e.memset(dvb[:], 1.0), writes=[b_dvb])
        sc.op('dve', lambda e: e.memset(fvst[:], 1.0), writes=[b_fvst])
        lf = sbf('lf', [128, 3, 8], F32)
        b_lf = Buf('lf')
        absw = sbf('absw', [128, 8], F32)
        b_absw = Buf('absw')
        stt = {'tb': 0, 'tT': 0}
        lfall = sbf('lfall', [128, NT, 8], F32)
        b_lfall = Buf('lfall')
        if stage == 'b':
            lfdbg = sbf('lfdbg', [128, NT, 3, 8], F32)
            b_lfdbg = Buf('lfdbg')

        def nxt(key, n=2):
            i = stt[key]
            stt[key] = (i + 1) % n
            return i

        def rope(src_ps, bsrc, nh, tabs, btabs, t, dst, bdst, plain):
            sv = src_ps.rearrange('p (h two d) -> p h two d', two=2, d=32)
            x1 = sv[:, :, 0, :]
            x2 = sv[:, :, 1, :]

            def tb(j):
                if plain:
                    a = cs[:, t, 0:32] if j in (0, 2) else cs[:, t, 32:64]
                else:
                    a = tabs[:, t, j, :]
                return a.unsqueeze(1).broadcast_to([128, nh, 32])
            A_, B_, C_, D_ = tb(0), tb(1), tb(2), tb(3)
            tt('dve', r1[:, 0:nh, :], x1, A_, ALU.mult, reads=[bsrc, btabs], writes=[b_r1])
            tt('dve', r2[:, 0:nh, :], x2, B_, ALU.mult, reads=[bsrc, btabs], writes=[b_r2])
            tt('dve', dst[:, 0:nh, 0:32], r1[:, 0:nh, :], r2[:, 0:nh, :], ALU.subtract,
               reads=[b_r1, b_r2], writes=[bdst])
            tt('dve', r1[:, 0:nh, :], x2, C_, ALU.mult, reads=[bsrc, btabs], writes=[b_r1])
            tt('dve', r2[:, 0:nh, :], x1, D_, ALU.mult, reads=[bsrc, btabs], writes=[b_r2])
            tt('dve', dst[:, 0:nh, 32:64], r1[:, 0:nh, :], r2[:, 0:nh, :], ALU.add,
               reads=[b_r1, b_r2], writes=[bdst])

        def head_ss(ps, bps, nh, hd):
            act(sqf[:, 0:nh * hd], ps[:, 0:nh * hd], AF.Square, reads=[bps], writes=[b_sqf])
            sc.op('dve', lambda e: e.tensor_reduce(
                out=st8[:, 0:nh], in_=sqf[:, 0:nh * hd].rearrange('p (h d) -> p h d', d=hd),
                axis=AX.X, op=ALU.add), reads=[b_sqf], writes=[b_st8])
            return rstd_from_ss(st8, b_st8, nh, hd)

        def transpose_out(src_bf, bsrc, gcol, bg, dst_dram, bdst, col0):
            ps, bps = next_ps()
            psv = ps[:].bitcast(BF16)
            for k4 in range(4):
                tr(psv[:, k4 * 128:(k4 + 1) * 128], src_bf[:, k4 * 128:(k4 + 1) * 128], ident_b[:],
                   reads=[bsrc, b_ident_b], writes=[bps])
            i = nxt('tT')
            if gcol is None:
                act(tT[i][:].rearrange('p a b -> p (a b)'), psv[:, 0:512], AF.Copy, reads=[bps],
                    writes=[b_tT[i]])
            else:
                ts('dve', tT[i][:].rearrange('p a b -> p (a b)'), psv[:, 0:512], gcol, None,
                   ALU.mult, reads=[bps, bg], writes=[b_tT[i]])
            if dst_dram is None:
                for pr in range(2):
                    sc.dma('sp', fkT_in[pr].rearrange('(hp p) t -> p hp t', p=128)[
                        :, :, col0:col0 + 128], tT[i][:, 2 * pr:2 * pr + 2, :], reads=[b_tT[i]],
                        writes=[bdst], sem='tTo%d' % i, nowaw=True)
            else:
                sc.dma('sp', dst_dram[:, :, col0:col0 + 128], tT[i][:], reads=[b_tT[i]],
                       writes=[bdst], sem='tTo%d' % i, nowaw=True)

        def proj(t, goff, n):
            ps, bps = next_ps()
            for kc in range(8):
                mm(ps[:, 0:n], uT[:, kc, t * 128:(t + 1) * 128], win_sb[:, kc, goff:goff + n],
                   kc == 0, kc == 7, reads=[b_uT, b_win], writes=[bps])
            return ps, bps

        for t in range(NT):
            c0 = t * 128
            pff, bpff = proj(t, GOFF['ff'], 8)
            pdd, bpdd = proj(t, GOFF['dk'], 128)
            pii, bpii = proj(t, GOFF['ik'], 72)
            act(lf[:, 1, :], pff[:, 0:8], AF.Copy, reads=[bpff], writes=[b_lf])
            tt('dve', lf[:, 0, :], lf[:, 1, :], bfor[:], ALU.add, reads=[b_lf, b_bfor],
               writes=[b_lf])
            act(lf[:, 1, :], lf[:, 0, :], AF.Exp, reads=[b_lf], writes=[b_lf], scale=-1.0)
            ts('dve', lf[:, 1, :], lf[:, 1, :], 1.0, None, ALU.add, reads=[b_lf], writes=[b_lf])
            act(lf[:, 2, :], lf[:, 1, :], AF.Ln, reads=[b_lf], writes=[b_lf])
            ts('dve', lfall[:, t, :], lf[:, 2, :], -1.0, None, ALU.mult, reads=[b_lf],
               writes=[b_lfall], nowaw=True)
            if stage == 'b':
                sc.op('dve', lambda e, t=t: e.tensor_copy(out=lfdbg[:, t, :, :], in_=lf[:]),
                      reads=[b_lf], writes=[b_lfdbg], nowaw=True)
            act(absw[:], pii[:, 64:72], AF.Abs, reads=[bpii], writes=[b_absw])
            act(isign[:, t, :], pii[:, 64:72], AF.Sign, reads=[bpii], writes=[b_isign])
            act(sqf[:, 0:64], pdd[:, 0:64], AF.Square, reads=[bpdd], writes=[b_sqf, b_st8],
                accum_out=st8[:, 0:1])
            rsk = rstd_from_ss(st8, b_st8, 1, 64)
            rope(pdd[:, 0:64], bpdd, 1, abk, b_abk, t, ro, b_ro, False)
            ts('dve', pack6[:, 0:64], ro[:, 0, :], rsk, None, ALU.mult, reads=[b_ro, b_st8],
               writes=[b_pack6])
            rope(pii[:, 0:64], bpii, 1, cs, b_cs, t, ro, b_ro, True)
            sc.op('dve', lambda e: e.tensor_copy(out=pack6[:, 64:128], in_=ro[:, 0, :]),
                  reads=[b_ro], writes=[b_pack6])
            act(dvb[:, 0:64], pdd[:, 64:128], AF.Copy, reads=[bpdd], writes=[b_dvb])
            sc.dma('sp', kin_dvA[c0:c0 + 128, :], dvb[:], reads=[b_dvb], writes=[b_kind], sem='dvo',
                   nowaw=True)
            ps, bps = next_ps()
            psv = ps[:].bitcast(BF16)
            tr(psv[:, 0:128], pack6[:], ident_b[:], reads=[b_pack6, b_ident_b], writes=[bps])
            act(p6T[:], psv[:, 0:128], AF.Copy, reads=[bps], writes=[b_p6T])
            sc.dma('sp', kin_dikT[:, c0:c0 + 128], p6T[:], reads=[b_p6T], writes=[b_kind],
                   sem='p6o', nowaw=True)
            for nm, gcol, dst, bdst in [('fq', gq2, fqT_d, b_fqTd), ('fk', gk2, None, b_kind)]:
                ps, bps = proj(t, GOFF[nm], 512)
                rs = head_ss(ps, bps, 8, 64)
                i = nxt('tb')
                tt('dve', tok_bf[i][:].rearrange('p (h d) -> p h d', d=64),
                   ps[:].rearrange('p (h d) -> p h d', d=64),
                   rs.unsqueeze(2).broadcast_to([128, 8, 64]), ALU.mult,
                   reads=[bps, b_st8], writes=[b_tok_bf[i]])
                transpose_out(tok_bf[i], b_tok_bf[i], gcol[:, 0:1], b_g2, dst, bdst, c0)
            ps, bps = proj(t, GOFF['fv'], 512)
            act(fvst[:, :, 0:64], ps[:].rearrange('p (h d) -> p h d', d=64), AF.Copy, reads=[bps],
                writes=[b_fvst])
            for hp_ in range(4):
                sc.dma('sp', fvA_in[hp_][c0:c0 + 128, :],
                       fvst[:, 2 * hp_:2 * hp_ + 2, :].rearrange('p e c -> p (e c)'),
                       reads=[b_fvst], writes=[b_kind], sem='fvo', nowaw=True)
            ps, bps = proj(t, GOFF['dq'], 512)
            rs = head_ss(ps, bps, 8, 64)
            rope(ps[:], bps, 8, abq, b_abq, t, ro, b_ro, False)
            i = nxt('tb')
            tt('dve', tok_bf[i][:].rearrange('p (h d) -> p h d', d=64), ro[:],
               rs.unsqueeze(2).broadcast_to([128, 8, 64]), ALU.mult,
               reads=[b_ro, b_st8], writes=[b_tok_bf[i]])
            transpose_out(tok_bf[i], b_tok_bf[i], None, None, dqT_d, b_dqTd, c0)
            ps, bps = proj(t, GOFF['iq'], 512)
            rope(ps[:], bps, 8, cs, b_cs, t, ro, b_ro, True)
            i = nxt('tb')
            tt('dve', tok_bf[i][:].rearrange('p (h d) -> p h d', d=64), ro[:],
               absw[:].unsqueeze(2).broadcast_to([128, 8, 64]), ALU.mult,
               reads=[b_ro, b_absw], writes=[b_tok_bf[i]])
            transpose_out(tok_bf[i], b_tok_bf[i], None, None, iqT_d, b_iqTd, c0)
            ps, bps = proj(t, GOFF['mq'], 512)
            rs = head_ss(ps, bps, 4, 128)
            i = nxt('tb')
            tt('dve', tok_bf[i][:].rearrange('p (h d) -> p h d', d=128),
               ps[:].rearrange('p (h d) -> p h d', d=128),
               rs.unsqueeze(2).broadcast_to([128, 4, 128]), ALU.mult,
               reads=[bps, b_st8], writes=[b_tok_bf[i]])
            transpose_out(tok_bf[i], b_tok_bf[i], gmq[:, 0:1], b_g2, mqT_d, b_mqTd, c0)
        sc.dma('sp', lin_d.rearrange('(t p) h -> p t h', p=128), lfall[:], reads=[b_lfall],
               writes=[b_lind], sem='lfo')
        if stage == 'b':
            d1 = dscr('lf_dbg', [128, NT, 3, 8], F32, out=True)
            sc.dma('sp', d1[:, :, :, :], lfdbg[:], reads=[b_lfdbg], writes=[Buf('d1')], sem='dbg0')
            d2 = dscr('bfor_dbg', [128, 8], F32, out=True)
            sc.dma('sp', d2[:, :], bfor[:], reads=[b_bfor], writes=[Buf('d2')], sem='dbg0')
            d3 = dscr('wff_dbg', [128, 8, 8], BF16, out=True)
            sc.dma('sp', d3[:, :, :], win_sb[:, :, 3072:3080], reads=[b_win], writes=[Buf('d3')],
                   sem='dbg0')
        sc.barrier()
        sc.run()
        pes.close()

    phase_a2()
    es_a.close()
    o_sb = {n: sbg('o_' + n, [128, NT, 512], BF16) for n in 'abc'}
    b_o = {n: Buf('o_' + n) for n in 'abc'}
    if stage == 'a2':
        es.close()
        print('ninst', sc.ninst, 'max sem', max(sc.cnt.values()), max(sc.dsem.values()), 'nsem', len(sc.semobj))
        return nc, dbg

    RG = [[0, 1, 2, 3], [4, 5, 6, 7]]
    b_kall, b_lall = Buf('kall'), Buf('lall')

    def gather(name, src, rows, cols, dt, b_src, b_dst):
        g = g_d[name]
        sc.custom('pool', lambda e: e.collective_compute(
            'AllGather', ALU.bypass, replica_groups=RG, ins=[src.ap().opt()],
            outs=[g.ap().opt()]), reads=[b_src], writes=[b_dst], sem='cc')
        return g
    if stage == 'b':
        lo0 = dscr('lin_dbg0', [TOK, 8], F32, out=True)
        b_lo0 = Buf('lo0')
        sc.dma('sp', lo0[:, :], lin_d[:, :], reads=[b_lind], writes=[b_lo0], sem='dbg0')
    bar_in = dscr('bar_in', [1, 64], F32)
    bar_out = dscr('bar_out', [1, 64], F32)
    b_bar = Buf('bar')
    sc.dma('sp', bar_in[:, :], ident_d[0:1, 0:64], reads=[b_kind, b_lind], writes=[b_bar],
           sem='bar')
    sc.custom('pool', lambda e: e.collective_compute(
        'AllReduce', ALU.add, replica_groups=RG, ins=[bar_in.ap().opt()],
        outs=[bar_out.ap().opt()]), reads=[b_bar, b_kind, b_lind], writes=[b_bar, b_kind, b_lind],
        sem='cc')
    lall_d = gather('lall', lin_d, TOK, 8, F32, b_lind, b_lall)
    fkT_g = [gather('fkT_g%d' % i, fkT_in[i], 256, TOK, BF16, b_kind, b_kall) for i in range(2)]
    fvA_g = [gather('fvA_g%d' % i, fvA_in[i], TOK, 130, BF16, b_kind, b_kall) for i in range(4)]
    dikT_g = gather('dikT_g', dikT_in, 128, TOK, BF16, b_kind, b_kall)
    dvA_g = gather('dvA_g', dvA_in, TOK, 65, BF16, b_kind, b_kall)
    if stage == 'b':
        ld = dscr('lall_dbg', [4 * TOK, 8], F32, out=True)
        sc.dma('sp', ld[:, :], lall_d[:, :], reads=[b_lall], writes=[Buf('ldbg')], sem='dbg')
        lo_ = dscr('lin_dbg', [TOK, 8], F32, out=True)
        sc.dma('sp', lo_[:, :], lin_d[:, :], reads=[b_lind, b_lall], writes=[Buf('lodbg')],
               sem='dbg')
        kd = dscr('kall_dbg', [64, 2048], BF16, out=True)
        sc.dma('sp', kd[:, :], fkT_g[0][3 * 256:3 * 256 + 64, :], reads=[b_kall],
               writes=[Buf('kdbg')], sem='dbg')
        sc.barrier()
        sc.run()
        es.close()
        return nc, dbg

    def rank_of(jj):
        return jj if jj < 4 else 7 - jj

    NKI = [8 * (i // 2) + 4 if i % 2 == 0 else 8 * (i // 2) + 8 for i in range(NT)]
    BOFF = [0]
    for i in range(NT):
        BOFF.append(BOFF[-1] + NKI[i])

    def load_seq_T(q, dst, b_dst, p0, p1, gt, rows_per_rank, row0, sem):
        dv = dst[p0:p1, :].rearrange('p (m j c) -> p m j c', m=8, j=8)
        n = p1 - p0
        for r in range(4):
            src = gt[r * rows_per_rank + row0:r * rows_per_rank + row0 + n, :].rearrange(
                'p (m two c) -> p m two c', two=2, c=128)
            sc.dma(q, dv[:, :, r, :], src[:, :, 0, :], reads=[b_kall], writes=[b_dst], sem=sem,
                   nowaw=True)
            sc.dma(q, dv[:, :, 7 - r, :], src[:, :, 1, :], reads=[b_kall], writes=[b_dst], sem=sem,
                   nowaw=True)

    def load_seq_tok(q, dst, b_dst, gt, sem):
        dv = dst.rearrange('p (m j) c -> p m j c', j=8)
        for r in range(4):
            src = gt[r * TOK:(r + 1) * TOK, :].rearrange('(m two p) c -> p m two c', two=2, p=128)
            sc.dma(q, dv[:, :, r, :], src[:, :, 0, :], reads=[b_kall], writes=[b_dst], sem=sem,
                   nowaw=True)
            sc.dma(q, dv[:, :, 7 - r, :], src[:, :, 1, :], reads=[b_kall], writes=[b_dst], sem=sem,
                   nowaw=True)

    def phase_c1():
        pes = ExitStack()

        def sbf(name, shape, dt):
            return pes.enter_context(nc.sbuf_tensor(un(name), list(shape), dt))
        tri = sbf('tri_sb', [128, 128], F32)
        ones_f = sbf('ones_f', [128, 128], F32)
        kposc = sbf('kposc_sb', [128, 64], F32)
        qpos_bc = sbf('qpos_bc', [128, TOK], F32)
        b_cst = Buf('c1const')
        sc.dma('sp', tri[:], tri_d[:, :], writes=[b_cst], sem='const', nowaw=True)
        sc.dma('sp', kposc[:], kposc_d[:, :], writes=[b_cst], sem='const', nowaw=True)
        sc.dma('sp', qpos_bc[:], qposf_d.ap().partition_broadcast(128), writes=[b_cst],
               sem='const', nowaw=True)
        b_ones = Buf('ones_f')
        sc.op('dve', lambda e: e.memset(ones_f[:], 1.0), writes=[b_ones])
        L_sb = sbf('L_sb', [128, 64, 8], F32)
        b_L = Buf('L_sb')
        Lv = L_sb[:].rearrange('p (m j) h -> p m j h', j=8)
        for r in range(4):
            src = lall_d[r * TOK:(r + 1) * TOK, :].rearrange('(m two p) h -> p m two h',
                                                              two=2, p=128)
            sc.dma('sp', Lv[:, :, r, :], src[:, :, 0, :], reads=[b_lall], writes=[b_L],
                   sem='Lld', nowaw=True)
            sc.dma('sp', Lv[:, :, 7 - r, :], src[:, :, 1, :], reads=[b_lall], writes=[b_L],
                   sem='Lld', nowaw=True)
        Lf = L_sb[:].rearrange('p g h -> p (g h)')
        cinc = sbf('cinc', [128, 64, 8], F32)
        tot = sbf('tot', [128, 64, 8], F32)
        scA = sbf('scA', [128, 64, 8], F32)
        scB = sbf('scB', [128, 64, 8], F32)
        c_all = sbf('c_all', [128, 64, 8], F32)
        Tpre = sbf('Tpre', [128, 64, 8], F32)
        b_cinc, b_tot, b_scA, b_scB, b_call, b_Tpre = [Buf(n) for n in
                                                       ['cinc', 'tot', 'scA', 'scB', 'c_all', 'Tpre']]
        p1, bp1 = next_ps()
        mm(p1[:], tri[:], Lf, True, True, reads=[b_cst, b_L], writes=[bp1])
        p2, bp2 = next_ps()
        mm(p2[:], ones_f[:], Lf, True, True, reads=[b_ones, b_L], writes=[bp2])
        act(cinc[:].rearrange('p g h -> p (g h)'), p1[:], AF.Copy, reads=[bp1], writes=[b_cinc])
        act(tot[:].rearrange('p g h -> p (g h)'), p2[:], AF.Copy, reads=[bp2], writes=[b_tot])
        sc.op('dve', lambda e: e.tensor_copy(out=scA[:], in_=tot[:]), reads=[b_tot], writes=[b_scA])
        cur, bcur, oth, both = scA, b_scA, scB, b_scB
        sft = 1
        while sft < 64:
            tt('dve', oth[:, sft:, :], cur[:, sft:, :], cur[:, :64 - sft, :], ALU.add,
               reads=[bcur], writes=[both])
            sc.op('dve', lambda e, oth=oth, cur=cur, sft=sft: e.tensor_copy(
                out=oth[:, :sft, :], in_=cur[:, :sft, :]), reads=[bcur], writes=[both])
            cur, bcur, oth, both = oth, both, cur, bcur
            sft *= 2
        tt('dve', Tpre[:], cur[:], tot[:], ALU.subtract, reads=[bcur, b_tot], writes=[b_Tpre])
        tt('dve', c_all[:], cinc[:], Tpre[:], ALU.add, reads=[b_cinc, b_Tpre], writes=[b_call])
        biasT = sbf('biasT', [128, BOFF[NT], 8], F32)
        b_biasT = Buf('biasT')
        negmT = sbf('negmT', [128, NT, 4, 128], BF16)
        b_negmT = Buf('negmT')
        mcol = sbf('mcol', [128, 4], F32)
        b_mcol = Buf('mcol')
        mrep = [sbf('mrep%d' % i, [128, 128], F32) for i in range(2)] * 2
        b_mrep = [Buf('mrep%d' % i) for i in range(2)] * 2
        cref = sbf('cref', [128, 8], F32)
        b_cref = Buf('cref')
        for i in range(NT):
            nk = NKI[i]
            g0 = nk - 4
            pc, bpc = next_ps()
            qmid = qpos_bc[:, i * 128 + 64:i * 128 + 65]
            for b in range(4):
                tt('dve', mcol[:, b:b + 1], qmid, kposc[:, g0 + b:g0 + b + 1], ALU.is_ge,
                   reads=[b_cst], writes=[b_mcol])
                ts('dve', mrep[b][:], ones_f[:], mcol[:, b:b + 1], None, ALU.mult,
                   reads=[b_ones, b_mcol], writes=[b_mrep[b]])
                mm(pc[:, 0:8], mrep[b][:], L_sb[:, g0 + b, :], b == 0, b == 3,
                   reads=[b_mrep[b], b_L], writes=[bpc])
                ts('dve', negmT[:, i, b, :], qpos_bc[:, i * 128:(i + 1) * 128],
                   kposc[:, g0 + b:g0 + b + 1], -30000.0, ALU.is_lt, ALU.mult,
                   reads=[b_cst], writes=[b_negmT], nowaw=True)
            tt('dve', cref[:], pc[:, 0:8], Tpre[:, g0, :], ALU.add, reads=[bpc, b_Tpre],
               writes=[b_cref])
            tt('dve', biasT[:, BOFF[i]:BOFF[i] + nk, :],
               cref[:].unsqueeze(1).broadcast_to([128, nk, 8]), c_all[:, 0:nk, :], ALU.subtract,
               reads=[b_cref, b_call], writes=[b_biasT], nowaw=True)
            ts('dve', biasT[:, BOFF[i]:BOFF[i] + nk, :], biasT[:, BOFF[i]:BOFF[i] + nk, :], 60.0,
               None, ALU.min, reads=[b_biasT], writes=[b_biasT], nowaw=True)
        if stage == 'c1a':
            cd = dscr('call_dbg', [128, 64, 8], F32, out=True)
            sc.dma('sp', cd[:, :, :], c_all[:], reads=[b_call], writes=[Buf('cad')], sem='dbg')
            bd = dscr('bias_dbg', [128, BOFF[NT], 8], F32, out=True)
            sc.dma('sp', bd[:, :, :], biasT[:], reads=[b_biasT], writes=[Buf('bad')], sem='dbg')
            sc.barrier()
            sc.run()
            pes.close()
            return
        fqT_sb = sbf('fqT_sb', [128, 4, TOK], BF16)
        b_fqT = Buf('fqT_sb')
        sc.dma('sp', fqT_sb[:], fqT_d[:, :, :], reads=[b_fqTd], writes=[b_fqT], sem='fqld')
        KT = [sbf('KT%d' % i, [128, S], BF16) for i in range(2)]
        VA = sbf('VA', [128, 64, 130], BF16)
        b_KT = [Buf('KT%d' % i) for i in range(2)]
        b_VA = Buf('VA')
        Vp = [sbf('Vp%d' % i, [128, 64, 130], BF16) for i in range(2)]
        b_Vp = [Buf('Vp%d' % i) for i in range(2)]
        wexp = [sbf('wexp%d' % i, [128, 64, 2], F32) for i in range(2)]
        b_wexp = [Buf('wexp%d' % i) for i in range(2)]
        PT = [sbf('PT%d' % i, [128, 512], BF16) for i in range(2)]
        b_PT = [Buf('PT%d' % i) for i in range(2)]
        rcp = sbf('rcp', [128, 2], F32)
        b_rcp = Buf('rcp')
        cnt = {'s': 0, 'o': 0, 'p': 0, 'v': 0}
        for hp in range(4):
            kb = hp % 2
            load_seq_T('sp', KT[kb], b_KT[kb], 0, 128, g_d['fkT_g%d' % (hp // 2)], 256,
                       (hp % 2) * 128, 'KT%d' % kb)
            load_seq_tok('sp', VA[:], b_VA, g_d['fvA_g%d' % hp], 'VA')
            for i in range(NT):
                nk = NKI[i]
                vi = cnt['v'] % 2
                cnt['v'] += 1
                act(wexp[vi][:, 0:nk, :], biasT[:, BOFF[i]:BOFF[i] + nk, 2 * hp:2 * hp + 2], AF.Exp,
                    reads=[b_biasT], writes=[b_wexp[vi]])
                tt('dve', Vp[vi][:, 0:nk, :].rearrange('p k (e c) -> p k e c', e=2),
                   VA[:, 0:nk, :].rearrange('p k (e c) -> p k e c', e=2),
                   wexp[vi][:, 0:nk, :].unsqueeze(3).broadcast_to([128, nk, 2, 65]), ALU.mult,
                   reads=[b_VA, b_wexp[vi]], writes=[b_Vp[vi]])
                ob = 4 + 2 * (cnt['o'] % 2)
                cnt['o'] += 1
                for e in range(2):
                    h = 2 * hp + e
                    pO, bpO = psum[ob + e], psb[ob + e]
                    for g4 in range(nk // 4):
                        sbk = cnt['s'] % 4
                        cnt['s'] += 1
                        pS, bpS = psum[sbk], psb[sbk]
                        last = (g4 == nk // 4 - 1)
                        if last:
                            mm(pS[:], ident_b[:], negmT[:, i, :, :].rearrange('p a b -> p (a b)'),
                               True, False, reads=[b_ident_b, b_negmT], writes=[bpS])
                        for kk in range(4):
                            kt = g4 * 4 + kk
                            mm(pS[:, kk * 128:(kk + 1) * 128],
                               KT[kb][e * 64:(e + 1) * 64, kt * 128:(kt + 1) * 128],
                               fqT_sb[e * 64:(e + 1) * 64, hp, i * 128:(i + 1) * 128],
                               not last, (not last) or kk == 3, reads=[b_KT[kb], b_fqT],
                               writes=[bpS], skip_group_check=True)
                        pi = cnt['p'] % 2
                        cnt['p'] += 1
                        act(PT[pi][:], pS[:], AF.Exp, reads=[bpS], writes=[b_PT[pi]], scale=0.125)
                        for kk in range(4):
                            kt = g4 * 4 + kk
                            mm(pO[:, 0:65], PT[pi][:, kk * 128:(kk + 1) * 128],
                               Vp[vi][:, kt, e * 65:(e + 1) * 65], kt == 0, kt == nk - 1,
                               reads=[b_PT[pi], b_Vp[vi]], writes=[bpO])
                    sc.op('dve', lambda e_, pO=pO, e=e: e_.reciprocal(out=rcp[:, e:e + 1],
                                                                      in_=pO[:, 64:65]),
                          reads=[bpO], writes=[b_rcp])
                    ts('dve', o_sb['a'][:, i, h * 64:(h + 1) * 64], pO[:, 0:64], rcp[:, e:e + 1],
                       None, ALU.mult, reads=[bpO, b_rcp], writes=[b_o['a']], nowaw=True)
        if stage == 'c1':
            od = dscr('oa_dbg', [128, NT, 512], BF16, out=True)
            sc.dma('sp', od[:, :, :], o_sb['a'][:], reads=[b_o['a']], writes=[Buf('oad')],
                   sem='dbg')
            cd = dscr('call_dbg', [128, 64, 8], F32, out=True)
            sc.dma('sp', cd[:, :, :], c_all[:], reads=[b_call], writes=[Buf('cad')], sem='dbg')
        sc.barrier()
        sc.run()
        pes.close()

    phase_c1()
    if stage in ('c1', 'c1a'):
        es.close()
        print('ninst', sc.ninst, 'max sem', max(sc.cnt.values()), max(sc.dsem.values()), 'nsem', len(sc.semobj))
        return nc, dbg

    NBIS = 25

    def phase_c2():
        pes = ExitStack()

        def sbf(name, shape, dt):
            return pes.enter_context(nc.sbuf_tensor(un(name), list(shape), dt))
        b_cst = Buf('c2const')
        iota = sbf('iota_sb', [128, 512], F32)
        sc.dma('sp', iota[:], iota512_d.ap().partition_broadcast(128), writes=[b_cst],
               sem='const', nowaw=True)
        pow2 = sbf('pow2_sb', [128, 32], F32)
        sc.dma('sp', pow2[:], pow2_d.ap().partition_broadcast(128), writes=[b_cst], sem='const',
               nowaw=True)
        qcol = sbf('qcol', [128, NT], F32)
        sc.dma('sp', qcol[:], qposf_d.rearrange('(t p) -> p t', p=128), writes=[b_cst],
               sem='const', nowaw=True, allow_slow_non_contiguous=True)
        ident4 = sbf('ident4', [128, 4, 128], BF16)
        b_id4 = Buf('ident4')
        for k4 in range(4):
            sc.op('dve', lambda e, k4=k4: e.tensor_copy(out=ident4[:, k4, :], in_=ident_b[:]),
                  reads=[b_ident_b], writes=[b_id4], nowaw=True)
        zer = sbf('zer', [128, 272], BF16)
        b_zer = Buf('zer')
        sc.op('dve', lambda e: e.memset(zer[:], 0.0), writes=[b_zer])
        dkT2 = sbf('dkT2', [128, S], BF16)
        ikT2 = sbf('ikT2', [128, S], BF16)
        dva = sbf('dva', [128, 64, 65], BF16)
        b_dk, b_ik, b_dva = Buf('dkT2'), Buf('ikT2'), Buf('dva')
        for hh in range(2):
            load_seq_T('sp', dkT2, b_dk, hh * 64, hh * 64 + 64, g_d['dikT_g'], 128, 0, 'dkld')
            load_seq_T('sp', ikT2, b_ik, hh * 64, hh * 64 + 64, g_d['dikT_g'], 128, 64, 'ikld')
        load_seq_tok('sp', dva[:], b_dva, g_d['dvA_g'], 'dvld')
        dqT_sb = sbf('dqT_sb', [128, 4, TOK], BF16)
        iqT_sb = sbf('iqT_sb', [128, 4, TOK], BF16)
        b_dqT, b_iqT = Buf('dqT_sb'), Buf('iqT_sb')
        sc.dma('sp', dqT_sb[:], dqT_d[:, :, :], reads=[b_dqTd], writes=[b_dqT], sem='dqld')
        sc.dma('sp', iqT_sb[:], iqT_d[:, :, :], reads=[b_iqTd], writes=[b_iqT], sem='iqld')
        score = sbf('score', [128, S], F32)
        negm2 = [sbf('negm%d' % i_, [128, S], BF16) for i_ in range(2)]
        b_negm2 = [Buf('negm%d' % i_) for i_ in range(2)]
        b_score = Buf('score')
        PTd = [sbf('PTd%d' % i, [128, 512], BF16) for i in range(3)]
        b_PTd = [Buf('PTd%d' % i) for i in range(3)]
        sm = sbf('sm', [128, 8], F32)
        b_sm = Buf('sm')
        steps = sbf('steps', [128, 32], F32)
        b_steps = Buf('steps')
        cneg = sbf('cneg', [128, 512], F32)
        b_cneg = Buf('cneg')
        rc4 = sbf('rc4', [128, 8], F32)
        b_rc4 = Buf('rc4')
        cnt = {'s': 0, 'p': 0, 'o': 0}

        def sbank():
            i = cnt['s'] % 4
            cnt['s'] += 1
            return psum[i], psb[i]
        def stage1(i):
            nk = NKI[i]
            n = nk * 128
            negm, b_negm = negm2[i % 2], b_negm2[i % 2]
            for c4 in range(nk // 4):
                for h in range(8):
                    e_, hp = h % 2, h // 2
                    ps, bps = sbank()
                    mm(ps[:], iqT_sb[e_ * 64:(e_ + 1) * 64, hp, i * 128:(i + 1) * 128],
                       ikT2[e_ * 64:(e_ + 1) * 64, c4 * 512:(c4 + 1) * 512], True, True,
                       reads=[b_iqT, b_ik], writes=[bps])
                    act(ps[:], ps[:], AF.Relu, reads=[bps], writes=[bps])
                    if h == 0:
                        ts('dve', score[:, c4 * 512:(c4 + 1) * 512], ps[:], isign[:, i, 0:1], None,
                           ALU.mult, reads=[bps, b_isign], writes=[b_score])
                    else:
                        sc.op('dve', lambda e, ps=ps, c4=c4, h=h, i=i: e.scalar_tensor_tensor(
                            out=score[:, c4 * 512:(c4 + 1) * 512], in0=ps[:],
                            scalar=isign[:, i, h:h + 1], in1=score[:, c4 * 512:(c4 + 1) * 512],
                            op0=ALU.mult, op1=ALU.add), reads=[bps, b_isign, b_score],
                            writes=[b_score])
            sc.op('dve', lambda e, n=n: e.tensor_reduce(out=sm[:, 0:1], in_=score[:, 0:n], axis=AX.X,
                                                       op=ALU.max, apply_absolute_value=True),
                  reads=[b_score], writes=[b_sm])
            ts('dve', sm[:, 1:2], sm[:, 0:1], -1.0, -1e-3, ALU.mult, ALU.add, reads=[b_sm],
               writes=[b_sm])
            ts('dve', sm[:, 2:3], sm[:, 0:1], 2.0, 2e-3, ALU.mult, ALU.add, reads=[b_sm],
               writes=[b_sm])
            ts('dve', steps[:], pow2[:], sm[:, 2:3], None, ALU.mult, reads=[b_sm, b_cst],
               writes=[b_steps])
            ts('dve', sm[:, 6:7], qcol[:, i:i + 1], float(-(n - 512)), None, ALU.add,
               reads=[b_cst], writes=[b_sm])
            ts('dve', cneg[:], iota[:], sm[:, 6:7], -1e9, ALU.is_gt, ALU.mult, reads=[b_cst, b_sm],
               writes=[b_cneg])
            tt('dve', score[:, n - 512:n], score[:, n - 512:n], cneg[:], ALU.add,
               reads=[b_score, b_cneg], writes=[b_score])
            for k in range(NBIS):
                tt('dve', sm[:, 3:4], sm[:, 1:2], steps[:, k:k + 1], ALU.add, reads=[b_sm, b_steps],
                   writes=[b_sm])
                sc.op('dve', lambda e, n=n: e.tensor_scalar(
                    out=negm[:, 0:n], in0=score[:, 0:n], scalar1=sm[:, 3:4], scalar2=None,
                    op0=ALU.is_ge, op1=ALU.add, accum_out=sm[:, 4:5]),
                    reads=[b_score, b_sm], writes=[b_negm, b_sm])
                ts('dve', sm[:, 5:6], sm[:, 4:5], 255.5, steps[:, k:k + 1], ALU.is_ge, ALU.mult,
                   reads=[b_sm, b_steps], writes=[b_sm])
                tt('dve', sm[:, 1:2], sm[:, 1:2], sm[:, 5:6], ALU.add, reads=[b_sm], writes=[b_sm])
            ts('dve', negm[:, 0:n], score[:, 0:n], sm[:, 1:2], -30000.0, ALU.is_lt, ALU.mult,
               reads=[b_score, b_sm], writes=[b_negm])
        def stage2(i):
            nk = NKI[i]
            negm, b_negm = negm2[i % 2], b_negm2[i % 2]
            ob = 4 + 2 * (cnt['o'] % 2)
            cnt['o'] += 1
            for half in range(2):
                mm(psum[ob + half][:, 0:272], zer[:, 0:128], zer[:, 0:272], True, False,
                   reads=[b_zer], writes=[psb[ob + half]])
            for kt in range(nk):
                for half in range(2):
                    pS, bpS = sbank()
                    pO, bpO = psum[ob + half], psb[ob + half]
                    mm(pS[:], negm[:, kt * 128:(kt + 1) * 128],
                       ident4[:].rearrange('p a b -> p (a b)'), True, False,
                       reads=[b_negm, b_id4], writes=[bpS])
                    for hh in range(4):
                        h = 2 * hh + half
                        e_, hp = h % 2, h // 2
                        mm(pS[:, hh * 128:(hh + 1) * 128],
                           dkT2[e_ * 64:(e_ + 1) * 64, kt * 128:(kt + 1) * 128],
                           dqT_sb[e_ * 64:(e_ + 1) * 64, hp, i * 128:(i + 1) * 128], False, hh == 3,
                           reads=[b_dk, b_dqT], writes=[bpS])
                    pi = cnt['p'] % 3
                    cnt['p'] += 1
                    act(PTd[pi][:], pS[:], AF.Exp, reads=[bpS], writes=[b_PTd[pi]], scale=0.125)
                    for hh in range(4):
                        mm(pO[:, hh * 68:hh * 68 + 65], PTd[pi][:, hh * 128:(hh + 1) * 128],
                           dva[:, kt, :], False, kt == nk - 1, reads=[b_PTd[pi], b_dva],
                           writes=[bpO], skip_group_check=True)
            for half in range(2):
                pO, bpO = psum[ob + half], psb[ob + half]
                pv = pO[:, 0:272].rearrange('p (h c) -> p h c', c=68)
                sc.op('dve', lambda e, pv=pv, half=half: e.reciprocal(
                    out=rc4[:, half * 4:(half + 1) * 4], in_=pv[:, :, 64]), reads=[bpO],
                    writes=[b_rc4])
                tt('dve', o_sb['b'][:, i, :].rearrange(
                    'p (hh par d) -> p hh par d', par=2, d=64)[:, :, half, :], pv[:, :, 0:64],
                   rc4[:, half * 4:(half + 1) * 4].unsqueeze(2).broadcast_to([128, 4, 64]),
                   ALU.mult, reads=[bpO, b_rc4], writes=[b_o['b']], nowaw=True)
        for i in range(NT):
            stage1(i)
            if i >= 1:
                stage2(i - 1)
        stage2(NT - 1)
        sc.barrier()
        sc.run()
        pes.close()

    phase_c2()
    if stage == 'c2':
        od = dscr('ob_dbg', [128, NT, 512], BF16, out=True)
        sc.dma('sp', od[:, :, :], o_sb['b'][:], reads=[b_o['b']], writes=[Buf('obd')], sem='dbg')
        od2 = dscr('oa_dbg', [128, NT, 512], BF16, out=True)
        sc.dma('sp', od2[:, :, :], o_sb['a'][:], reads=[b_o['a']], writes=[Buf('oad')], sem='dbg')
        sc.barrier()
        sc.run()
        es.close()
        return nc, dbg

    def phase_c3():
        pes = ExitStack()

        def sbf(name, shape, dt):
            return pes.enter_context(nc.sbuf_tensor(un(name), list(shape), dt))
        norm_transpose = make_norm_transpose(sbf)
        gTm, b_gTm = load_gain_T(sbf, 'gT_mem', v_d['mem_norm'], 8)
        gmk = sbf('gmk', [128, 1], F32)
        b_gmk = Buf('gmk')
        sc.dma('sp', gmk[:], v_d['mem_k_norm'].rearrange('(p o) -> p o', o=1), writes=[b_gmk],
               sem='const')
        wkv = sbf('wkv', [128, 8, D], BF16)
        b_wkv = Buf('wkv')
        sc.dma('sp', wkv[:], wb_d['w_mem_kv'].rearrange('(kc p) f -> p kc f', p=128),
               reads=[wb_buf['w_mem_kv']], writes=[b_wkv], sem='wkv')
        memT = sbf('memT', [128, 8, 256], BF16)
        b_memT = Buf('memT')
        mt_sb = [sbf('memt%d' % i, [128, D], F32) for i in range(2)]
        b_mt = [Buf('memt%d' % i) for i in range(2)]
        for m in range(2):
            sc.dma('sp', mt_sb[m][:], mem_d[m * 128:(m + 1) * 128, :], writes=[b_mt[m]],
                   sem='memld')
            norm_transpose(mt_sb[m][:], b_mt[m], gTm, b_gTm, memT, b_memT, m * 128)
        kmT = sbf('kmT', [128, 4, 256], BF16)
        b_kmT = Buf('kmT')
        vma = sbf('vma', [128, 2, 4, 129], BF16)
        b_vma = Buf('vma')
        sc.op('dve', lambda e: e.memset(vma[:], 1.0), writes=[b_vma])
        sqf = sbf('sqf3', [128, 512], F32)
        b_sqf = Buf('sqf3')
        st8 = sbf('st83', [128, 16], F32)
        b_st8 = Buf('st83')
        kmb = sbf('kmb', [128, 512], BF16)
        b_kmb = Buf('kmb')
        for m in range(2):
            ps, bps = next_ps()
            for kc in range(8):
                mm(ps[:], memT[:, kc, m * 128:(m + 1) * 128], wkv[:, kc, 0:512], kc == 0, kc == 7,
                   reads=[b_memT, b_wkv], writes=[bps])
            act(sqf[:], ps[:], AF.Square, reads=[bps], writes=[b_sqf])
            sc.op('dve', lambda e: e.tensor_reduce(
                out=st8[:, 0:4], in_=sqf[:].rearrange('p (h d) -> p h d', d=128), axis=AX.X,
                op=ALU.add), reads=[b_sqf], writes=[b_st8])
            rs = rstd_from_ss(st8, b_st8, 4, 128)
            tt('dve', kmb[:].rearrange('p (h d) -> p h d', d=128),
               ps[:].rearrange('p (h d) -> p h d', d=128),
               rs.unsqueeze(2).broadcast_to([128, 4, 128]), ALU.mult, reads=[bps, b_st8],
               writes=[b_kmb])
            pt_, bpt = next_ps()
            ptv = pt_[:].bitcast(BF16)
            for h in range(4):
                tr(ptv[:, h * 128:(h + 1) * 128], kmb[:, h * 128:(h + 1) * 128], ident_b[:],
                   reads=[b_kmb, b_ident_b], writes=[bpt])
            ts('dve', kmT[:, :, m * 128:(m + 1) * 128],
               ptv[:, 0:512].rearrange('p (h k) -> p h k', k=128), gmk[:, 0:1], None, ALU.mult,
               reads=[bpt, b_gmk], writes=[b_kmT], nowaw=True)
            ps2, bps2 = next_ps()
            for kc in range(8):
                mm(ps2[:], memT[:, kc, m * 128:(m + 1) * 128], wkv[:, kc, 512:1024], kc == 0,
                   kc == 7, reads=[b_memT, b_wkv], writes=[bps2])
            act(vma[:, m, :, 0:128], ps2[:].rearrange('p (h d) -> p h d', d=128), AF.Copy,
                reads=[bps2], writes=[b_vma], nowaw=True)
        mqT_sb = sbf('mqT_sb', [128, 4, TOK], BF16)
        b_mqT = Buf('mqT_sb')
        sc.dma('sp', mqT_sb[:], mqT_d[:, :, :], reads=[b_mqTd], writes=[b_mqT], sem='mqld')
        zer = sbf('zer3', [128, 264], BF16)
        b_zer = Buf('zer3')
        sc.op('dve', lambda e: e.memset(zer[:], 0.0), writes=[b_zer])
        PTm = [sbf('PTm%d' % i, [128, 512], BF16) for i in range(2)]
        b_PTm = [Buf('PTm%d' % i) for i in range(2)]
        rc4 = sbf('rc43', [128, 4], F32)
        b_rc4 = Buf('rc43')
        cnt = {'o': 0, 'p': 0}
        for i in range(NT):
            ob = 4 + 2 * (cnt['o'] % 2)
            cnt['o'] += 1
            for half in range(2):
                pO, bpO = psum[ob + half], psb[ob + half]
                mm(pO[:, 0:264], zer[:, 0:128], zer[:, 0:264], True, False, reads=[b_zer],
                   writes=[bpO])
                pS, bpS = psum[half], psb[half]
                for hh in range(2):
                    h = 2 * half + hh
                    for m in range(2):
                        mm(pS[:, (hh * 2 + m) * 128:(hh * 2 + m + 1) * 128],
                           kmT[:, h, m * 128:(m + 1) * 128], mqT_sb[:, h, i * 128:(i + 1) * 128],
                           True, True, reads=[b_kmT, b_mqT], writes=[bpS], skip_group_check=True)
                pi = cnt['p'] % 2
                cnt['p'] += 1
                act(PTm[pi][:], pS[:], AF.Exp, reads=[bpS], writes=[b_PTm[pi]],
                    scale=float(128 ** -0.5))
                for hh in range(2):
                    h = 2 * half + hh
                    for m in range(2):
                        mm(pO[:, hh * 132:hh * 132 + 129],
                           PTm[pi][:, (hh * 2 + m) * 128:(hh * 2 + m + 1) * 128], vma[:, m, h, :],
                           False, m == 1, reads=[b_PTm[pi], b_vma], writes=[bpO],
                           skip_group_check=True)
                pv = pO[:, 0:264].rearrange('p (h c) -> p h c', c=132)
                sc.op('dve', lambda e, pv=pv, half=half: e.reciprocal(
                    out=rc4[:, half * 2:(half + 1) * 2], in_=pv[:, :, 128]), reads=[bpO],
                    writes=[b_rc4])
                tt('dve', o_sb['c'][:, i, half * 256:(half + 1) * 256].rearrange(
                    'p (h d) -> p h d', d=128), pv[:, :, 0:128],
                   rc4[:, half * 2:(half + 1) * 2].unsqueeze(2).broadcast_to([128, 2, 128]),
                   ALU.mult, reads=[bpO, b_rc4], writes=[b_o['c']], nowaw=True)
        sc.barrier()
        sc.run()
        pes.close()

    phase_c3()
    if stage == 'c3':
        od = dscr('oc_dbg', [128, NT, 512], BF16, out=True)
        sc.dma('sp', od[:, :, :], o_sb['c'][:], reads=[b_o['c']], writes=[Buf('ocd')], sem='dbg')
        od3 = dscr('ob_dbg', [128, NT, 512], BF16, out=True)
        sc.dma('sp', od3[:, :, :], o_sb['b'][:], reads=[b_o['b']], writes=[Buf('obd')], sem='dbg')
        od2 = dscr('oa_dbg', [128, NT, 512], BF16, out=True)
        sc.dma('sp', od2[:, :, :], o_sb['a'][:], reads=[b_o['a']], writes=[Buf('oad')], sem='dbg')
        sc.barrier()
        sc.run()
        es.close()
        return nc, dbg

    def phase_d():
        pes = ExitStack()

        def sbf(name, shape, dt):
            return pes.enter_context(nc.sbuf_tensor(un(name), list(shape), dt))
        norm_transpose = make_norm_transpose(sbf)
        gTx, b_gTx = load_gain_T(sbf, 'gT_mix2', v_d['mix_norm'], 8)
        wing = sbf('wing', [128, 8, 3072], BF16)
        wbr = sbf('wbr', [128, 12, D], BF16)
        wout = sbf('wout', [128, 8, D], BF16)
        b_wing, b_wbr, b_wout = Buf('wing'), Buf('wbr'), Buf('wout')
        for n3 in range(3):
            sc.dma('sp', wing[:, :, n3 * 1024:(n3 + 1) * 1024],
                   wb_d['w_in'][:, 3280 + n3 * 1024:3280 + (n3 + 1) * 1024].rearrange(
                       '(kc p) f -> p kc f', p=128), reads=[wb_buf['w_in']], writes=[b_wing],
                   sem='wing', nowaw=True)
        sc.dma('sp', wbr[:], wb_d['w_branch'].rearrange('(c p) d -> p c d', p=128),
               reads=[wb_buf['w_branch']], writes=[b_wbr], sem='wbr')
        sc.dma('sp', wout[:], wb_d['w_out'].rearrange('(c p) d -> p c d', p=128),
               reads=[wb_buf['w_out']], writes=[b_wout], sem='wout')
        uTb = sbf('uTb', [128, 8, 512], BF16)
        b_uTb = Buf('uTb')
        oT = {n_: sbf('oT_' + n_, [128, 4, 512], BF16) for n_ in 'abc'}
        b_oT = {n_: Buf('oT_' + n_) for n_ in 'abc'}
        mergedT = sbf('mergedT', [128, 8, 512], BF16)
        b_mg = Buf('mergedT')
        h1t = [sbf('h1t%d' % i, [128, D], F32) for i in range(4)]
        b_h1t = [Buf('h1t%d' % i) for i in range(4)]
        h2t = [sbf('h2t%d' % i, [128, D], F32) for i in range(2)]
        b_h2t = [Buf('h2t%d' % i) for i in range(2)]
        gs = [sbf('gs%d' % i, [128, 512], BF16) for i in range(2)]
        b_gs = [Buf('gs%d' % i) for i in range(2)]
        acc = sbf('acc', [128, 512], F32)
        tmpm = sbf('tmpm', [128, 512], F32)
        b_acc, b_tmpm = Buf('acc'), Buf('tmpm')
        cnt = {'g': 0, 'h2': 0}
        for blk in range(4):
            for tl in range(4):
                t = blk * 4 + tl
                sc.dma('sp', h1t[tl][:], h1_d[t * 128:(t + 1) * 128, :], reads=[b_h1d],
                       writes=[b_h1t[tl]], sem='h1t%d' % tl)
                norm_transpose(h1t[tl][:], b_h1t[tl], gTx, b_gTx, uTb, b_uTb, tl * 128)
                for n_ in 'abc':
                    ps, bps = next_ps()
                    psv = ps[:].bitcast(BF16)
                    for wc in range(4):
                        tr(psv[:, wc * 128:(wc + 1) * 128], o_sb[n_][:, t, wc * 128:(wc + 1) * 128],
                           ident_b[:], reads=[b_o[n_], b_ident_b], writes=[bps])
                    act(oT[n_][:, :, tl * 128:(tl + 1) * 128],
                        psv[:, 0:512].rearrange('p (w k) -> p w k', k=128), AF.Copy, reads=[bps],
                        writes=[b_oT[n_]], nowaw=True)
            for dc in range(8):
                for n3, n_ in enumerate('abc'):
                    pg, bpg = next_ps()
                    for kc in range(8):
                        mm(pg[:], wing[:, kc, n3 * 1024 + dc * 128:n3 * 1024 + (dc + 1) * 128],
                           uTb[:, kc, :], kc == 0, kc == 7, reads=[b_wing, b_uTb], writes=[bpg])
                    gi = cnt['g'] % 2
                    cnt['g'] += 1
                    act(gs[gi][:], pg[:], AF.Sigmoid, reads=[bpg], writes=[b_gs[gi]])
                    pp, bpp = next_ps()
                    for wc in range(4):
                        mm(pp[:], wbr[:, n3 * 4 + wc, dc * 128:(dc + 1) * 128], oT[n_][:, wc, :],
                           wc == 0, wc == 3, reads=[b_wbr, b_oT[n_]], writes=[bpp])
                    if n3 == 0:
                        tt('dve', acc[:], pp[:], gs[gi][:], ALU.mult, reads=[bpp, b_gs[gi]],
                           writes=[b_acc])
                    else:
                        tt('dve', tmpm[:], pp[:], gs[gi][:], ALU.mult, reads=[bpp, b_gs[gi]],
                           writes=[b_tmpm])
                        if n3 == 1:
                            tt('dve', acc[:], acc[:], tmpm[:], ALU.add, reads=[b_acc, b_tmpm],
                               writes=[b_acc])
                        else:
                            tt('dve', mergedT[:, dc, :], acc[:], tmpm[:], ALU.add,
                               reads=[b_acc, b_tmpm], writes=[b_mg], nowaw=True)
            for tl in range(4):
                t = blk * 4 + tl
                hi = cnt['h2'] % 2
                cnt['h2'] += 1
                for dh in range(2):
                    po, bpo = next_ps()
                    for dc in range(8):
                        mm(po[:], mergedT[:, dc, tl * 128:(tl + 1) * 128],
                           wout[:, dc, dh * 512:(dh + 1) * 512], dc == 0, dc == 7,
                           reads=[b_mg, b_wout], writes=[bpo])
                    tt('dve', h2t[hi][:, dh * 512:(dh + 1) * 512], po[:],
                       h1t[tl][:, dh * 512:(dh + 1) * 512], ALU.add, reads=[bpo, b_h1t[tl]],
                       writes=[b_h2t[hi]])
                sc.dma('sp', h2_d[t * 128:(t + 1) * 128, :], h2t[hi][:], reads=[b_h2t[hi]],
                       writes=[b_h2d], sem='h2o%d' % hi, nowaw=True)
        sc.barrier()
        sc.run()
        pes.close()

    phase_d()
    ffn_phase(h2_d, b_h2d, y_d, b_yd, 'ffn2_norm', 'ffn2_w_gate', 'ffn2_w_up', 'ffn2_w_down')
    es.close()
    print('ninst', sc.ninst, 'max sem', max(sc.cnt.values()), max(sc.dsem.values()), 'nsem', len(sc.semobj))
    return nc, dbg


_NC_CACHE = {}


def _core_rows(cid):
    b, j = divmod(cid, 4)
    tiles = zig_tiles(j)
    rows = np.concatenate([np.arange(g * 128, (g + 1) * 128) for g in tiles])
    return b, rows


def kernel(**inputs):
    stage = inputs.pop('_stage', 'full')
    if stage not in _NC_CACHE:
        _NC_CACHE[stage] = build(stage)
    nc, dbg = _NC_CACHE[stage]
    f = lambda k: np.asarray(inputs[k], dtype=np.float32)
    x = f('x')
    mem = f('mem')
    pos = np.asarray(inputs['positions']).astype(np.int32)
    ident = np.eye(128, dtype=np.float32)
    invf = (10000.0 ** (-np.arange(0, 64, 2, dtype=np.float32) / 64)).astype(np.float32)
    tri = np.triu(np.ones((128, 128), np.float32))
    kposc = (np.arange(64, dtype=np.float32)[None, :] * 128 + np.arange(128, dtype=np.float32)[:, None])
    pow2 = (0.5 ** np.arange(1, 33)).astype(np.float32)
    shared = {'ident': ident, 'invf': invf, 'tri': tri, 'kposc': np.ascontiguousarray(kposc),
              'pow2': pow2, 'iota512': np.arange(512, dtype=np.float32)}
    for k, shp in WNAMES.items():
        shared[k] = np.ascontiguousarray(f(k)[0].reshape(shp))
    for k in VNAMES:
        shared[k] = np.ascontiguousarray(f(k)[0])
    in_maps = []
    for cid in range(NCORES):
        b, rows = _core_rows(cid)
        m = dict(shared)
        m['x'] = np.ascontiguousarray(x[b][rows])
        m['pos'] = np.ascontiguousarray(pos[b][rows])
        m['qposf'] = rows.astype(np.float32)
        m['mem'] = np.ascontiguousarray(mem[b])
        in_maps.append(m)
    res = run_bass_kernel_spmd(nc, in_maps, core_ids=list(range(NCORES)))
    if stage != 'full':
        return res.results
    out = np.zeros((2, S, D), np.float32)
    for cid in range(NCORES):
        b, rows = _core_rows(cid)
        out[b][rows] = res.results[cid]['y']
    return out
```

```python
import numpy as np
from contextlib import ExitStack
import concourse.bass as bass
import concourse.mybir as mybir
from concourse.bass_utils import run_bass_kernel_spmd

F32 = mybir.dt.float32
BF16 = mybir.dt.bfloat16
I32 = mybir.dt.int32
AF = mybir.ActivationFunctionType
ALU = mybir.AluOpType
AX = mybir.AxisListType

NCORES = 8
D = 1024
S = 8192
TOK = 2048
NT = 16
DFF = 2816
NFC = 22
DIN = 6352
EPS = 1e-6
ENG = ['pe', 'act', 'dve', 'pool', 'sp']
SAME_ENG_SYNC = True
MAXQ = {'sp': 6, 'pool': 4, 'act': 6}


class Buf:
    __slots__ = ('name', 'w', 'r')

    def __init__(self, name):
        self.name = name
        self.w = {}
        self.r = {}


class Sched:
    def __init__(self, nc, es):
        self.nc = nc
        self.es = es
        self.semobj = {}
        self.cnt = {}
        self.prog = {e: [] for e in ENG}
        self.waited = {e: {} for e in ENG}
        self.dsem = {}
        self.epoch = 0
        self.ekey = {}
        for e in ENG:
            k = e + '@0'
            self.ekey[e] = k
            self.semobj[k] = es.enter_context(nc.semaphore('sem_' + e + '_0'))
            self.cnt[k] = 0
        self.ninst = {e: 0 for e in ENG}
        self.outq = {}

    def _deps(self, reads, writes, nowaw=False):
        deps = {}

        def add(k, v):
            if deps.get(k, 0) < v:
                deps[k] = v
        for b in reads:
            for k, v in b.w.items():
                add(k, v)
        for b in writes:
            if not nowaw:
                for k, v in b.w.items():
                    add(k, v)
            for k, v in b.r.items():
                add(k, v)
        return deps

    def _mark(self, key, v, reads, writes, nowaw):
        for b in reads:
            if b.r.get(key, 0) < v:
                b.r[key] = v
        for b in writes:
            if nowaw:
                if b.w.get(key, 0) < v:
                    b.w[key] = v
            else:
                b.w = {key: v}
                b.r = {}

    def _emit_waits(self, eng, deps):
        for k, v in deps.items():
            if k.split('@')[0] == eng and (eng == 'pe' or not SAME_ENG_SYNC):
                continue
            if k in self.dsem:
                v = self.dsem[k]
            if self.waited[eng].get(k, 0) >= v:
                continue
            self.waited[eng][k] = v
            sem = self.semobj[k]
            self.prog[eng].append(lambda e, sem=sem, v=v: e.wait_ge(sem, v))
            self.ninst[eng] += 1

    def op(self, eng, fn, reads=(), writes=(), nowaw=False):
        deps = self._deps(reads, writes, nowaw)
        self._emit_waits(eng, deps)
        key = self.ekey[eng]
        self.cnt[key] += 1
        n = self.cnt[key]
        sem = self.semobj[key]
        self.prog[eng].append(lambda e, fn=fn, sem=sem: fn(e).then_inc(sem, 1))
        self.ninst[eng] += 1
        self._mark(key, n, reads, writes, nowaw)

    def dma(self, q, out, in_, reads=(), writes=(), sem='dma', nowaw=False, **kw):
        deps = self._deps(reads, writes, nowaw)
        self._emit_waits(q, deps)
        if sem not in self.dsem:
            self.semobj[sem] = self.es.enter_context(self.nc.semaphore('dsem_' + sem))
            self.dsem[sem] = 0
        fifo = self.outq.setdefault(q, [])
        while len(fifo) >= MAXQ[q]:
            osem = fifo[0][0]
            ov = self.dsem[osem]
            fifo[:] = [x for x in fifo if x[0] != osem]
            if self.waited[q].get(osem, 0) < ov:
                self.waited[q][osem] = ov
                so = self.semobj[osem]
                self.prog[q].append(lambda e, so=so, ov=ov: e.wait_ge(so, ov))
                self.ninst[q] += 1
        self.dsem[sem] += 16
        v = self.dsem[sem]
        fifo.append((sem, v))
        s = self.semobj[sem]
        self.prog[q].append(lambda e, out=out, in_=in_, kw=kw, s=s:
                            e.dma_start(out=out, in_=in_, **kw).then_inc(s, 16))
        self.ninst[q] += 1
        self._mark(sem, v, reads, writes, nowaw)

    def custom(self, q, fn, reads=(), writes=(), sem='cc', inc=1):
        deps = self._deps(reads, writes)
        self._emit_waits(q, deps)
        if sem not in self.dsem:
            self.semobj[sem] = self.es.enter_context(self.nc.semaphore('dsem_' + sem))
            self.dsem[sem] = 0
        self.dsem[sem] += inc
        v = self.dsem[sem]
        s = self.semobj[sem]
        self.prog[q].append(lambda e, fn=fn, s=s, inc=inc: fn(e).then_inc(s, inc))
        self._mark(sem, v, reads, writes, False)

    def barrier(self):
        for e in ENG:
            deps = {self.ekey[k]: self.cnt[self.ekey[k]] for k in ENG
                    if self.cnt[self.ekey[k]] > 0 and k != e}
            for k, v in self.dsem.items():
                if v > 0:
                    deps[k] = v
            self._emit_waits(e, deps)

    def new_epoch(self):
        self.epoch += 1
        for e in ENG:
            k = '%s@%d' % (e, self.epoch)
            self.ekey[e] = k
            self.semobj[k] = self.es.enter_context(self.nc.semaphore('sem_%s_%d' % (e, self.epoch)))
            self.cnt[k] = 0

    def run(self):
        nc = self.nc
        with nc.Block() as block:
            @block.tensor
            def _(e):
                for f in self.prog['pe']:
                    f(e)

            @block.scalar
            def _(e):
                for f in self.prog['act']:
                    f(e)

            @block.vector
            def _(e):
                for f in self.prog['dve']:
                    f(e)

            @block.gpsimd
            def _(e):
                for f in self.prog['pool']:
                    f(e)

            @block.sync
            def _(e):
                for f in self.prog['sp']:
                    f(e)
        self.prog = {e: [] for e in ENG}
        self.new_epoch()


class Ctx:
    pass


def zig_tiles(j):
    out = []
    for m in range(8):
        out.append(8 * m + j)
        out.append(8 * m + 7 - j)
    return out


SPLITS = dict(fq=(0, 512), fk=(512, 1024), fv=(1024, 1536), ff=(1536, 1544), dq=(1544, 2056),
              dk=(2056, 2120), dv=(2120, 2184), iq=(2184, 2696), ik=(2696, 2760),
              iw=(2760, 2768), mq=(2768, 3280), g=(3280, 6352))
KROWS = 1225
WNAMES = {
    'ffn1_w_gate': (D, DFF), 'ffn1_w_up': (D, DFF), 'ffn1_w_down': (DFF, D),
    'w_in': (D, DIN), 'w_mem_kv': (D, D), 'w_branch': (3 * 512, D), 'w_out': (D, D),
    'ffn2_w_gate': (D, DFF), 'ffn2_w_up': (D, DFF), 'ffn2_w_down': (DFF, D),
}
VNAMES = {'ffn1_norm': D, 'mix_norm': D, 'mem_norm': D, 'ffn2_norm': D, 'b_forget': 8,
          'fox_q_norm': 64, 'fox_k_norm': 64, 'dsa_q_norm': 64, 'dsa_k_norm': 64,
          'mem_q_norm': 128, 'mem_k_norm': 128}
STAGES = ['ffn1', 'a2', 'full']


def build(stage='full'):
    nc = bass.Bass("TRN2", target_bir_lowering=False)
    es = ExitStack()
    sc = Sched(nc, es)
    c = Ctx()
    dbg = {}
    uq = [0]

    def un(name):
        uq[0] += 1
        return '%s_%d' % (name, uq[0])

    def din(name, shape, dt=F32):
        return nc.dram_tensor(name, list(shape), dt, kind="ExternalInput")

    def dscr(name, shape, dt, out=False):
        if out:
            dbg[name] = True
        return nc.dram_tensor(name, list(shape), dt, kind="ExternalOutput" if out else "Internal")

    x_d = din('x', [TOK, D])
    pos_d = din('pos', [TOK], I32)
    qposf_d = din('qposf', [TOK])
    ident_d = din('ident', [128, 128])
    invf_d = din('invf', [32])
    mem_d = din('mem', [256, D])
    tri_d = din('tri', [128, 128])
    kposc_d = din('kposc', [128, 64])
    pow2_d = din('pow2', [32])
    iota512_d = din('iota512', [512])
    w_d = {k: din(k, s) for k, s in WNAMES.items()}
    v_d = {k: din(k, [n]) for k, n in VNAMES.items()}
    h1_d = dscr('h1s', [TOK, D], F32, out=(stage == 'ffn1'))
    b_h1d = Buf('h1s')
    A2O = (stage == 'a2')
    fqT_d = dscr('fqT_s', [128, 4, TOK], BF16, out=A2O)
    dqT_d = dscr('dqT_s', [128, 4, TOK], BF16, out=A2O)
    iqT_d = dscr('iqT_s', [128, 4, TOK], BF16, out=A2O)
    mqT_d = dscr('mqT_s', [128, 4, TOK], BF16, out=A2O)
    fkT_in = [dscr('fkT_in%d' % i, [256, TOK], BF16, out=A2O) for i in range(2)]
    fvA_in = [dscr('fvA_in%d' % i, [TOK, 130], BF16, out=A2O) for i in range(4)]
    dikT_in = dscr('dikT_in', [128, TOK], BF16, out=A2O)
    dvA_in = dscr('dvA_in', [TOK, 65], BF16, out=A2O)
    lin_d = dscr('lin', [TOK, 8], F32, out=A2O)
    b_fqTd, b_dqTd, b_iqTd, b_mqTd, b_kind, b_lind = [Buf(n) for n in
                                                    ['fqTd', 'dqTd', 'iqTd', 'mqTd', 'kind', 'lind']]
    o = 0
    kin_dikT = dikT_in
    kin_dvA = dvA_in
    GSH = {'lall': (TOK, 8, F32), 'fkT_g0': (256, TOK, BF16), 'fkT_g1': (256, TOK, BF16),
           'fvA_g0': (TOK, 130, BF16), 'fvA_g1': (TOK, 130, BF16), 'fvA_g2': (TOK, 130, BF16),
           'fvA_g3': (TOK, 130, BF16), 'dikT_g': (128, TOK, BF16), 'dvA_g': (TOK, 65, BF16)}
    g_d = {k: dscr(k, [4 * r, c_], dt) for k, (r, c_, dt) in GSH.items()}
    y_d = nc.dram_tensor('y', [TOK, D], F32, kind="ExternalOutput")
    b_yd = Buf('y')
    h2_d = dscr('h2s', [TOK, D], F32)
    b_h2d = Buf('h2s')
    wb_d = {k: dscr(k + '_bf', s_, BF16) for k, s_ in WNAMES.items()}
    wb_buf = {k: Buf(k + '_bf') for k in WNAMES}

    def sbg(name, shape, dt):
        return es.enter_context(nc.sbuf_tensor(un(name), list(shape), dt))

    ident_f = sbg('ident_f', [128, 128], F32)
    ident_b = sbg('ident_b', [128, 128], BF16)
    b_ident_f = Buf('ident_f')
    b_ident_b = Buf('ident_b')
    sc.dma('sp', ident_f[:], ident_d[:, :], writes=[b_ident_f], sem='ident')
    sc.op('dve', lambda e: e.tensor_copy(out=ident_b[:], in_=ident_f[:]),
          reads=[b_ident_f], writes=[b_ident_b])
    isign = sbg('isign', [128, NT, 8], F32)
    b_isign = Buf('isign')
    es_a = ExitStack()
    uT = es_a.enter_context(nc.sbuf_tensor('uT', [128, 8, TOK], BF16))
    b_uT = Buf('uT')

    def cast_weight(name):
        src = w_d[name].reshape([-1, 1024])
        dst = wb_d[name].reshape([-1, 1024])
        rows = src.shape[0]
        r0 = 0
        while r0 < rows:
            r1 = min(rows, r0 + 2048)
            sc.dma('pool', dst[r0:r1, :], src[r0:r1, :], writes=[wb_buf[name]], sem='wcast')
            r0 = r1

    for k in WNAMES:
        cast_weight(k)

    psum = [es.enter_context(nc.psum_tensor('ps%d' % i, [128, 512], F32)) for i in range(8)]
    psb = [Buf('ps%d' % i) for i in range(8)]
    c.ps_i = 0

    def next_ps():
        i = c.ps_i
        c.ps_i = (i + 1) % 8
        return psum[i], psb[i]

    def ts(eng, out, in0, s1, s2, op0, op1=None, reads=(), writes=(), accum=None, nowaw=False):
        if op1 is None:
            sc.op(eng, lambda e: e.tensor_scalar(out=out, in0=in0, scalar1=s1, scalar2=None,
                                                 op0=op0, accum_out=accum), reads, writes, nowaw)
        else:
            sc.op(eng, lambda e: e.tensor_scalar(out=out, in0=in0, scalar1=s1, scalar2=s2,
                                                 op0=op0, op1=op1, accum_out=accum), reads, writes,
                  nowaw)

    def tt(eng, out, in0, in1, op, reads=(), writes=(), nowaw=False):
        sc.op(eng, lambda e: e.tensor_tensor(out=out, in0=in0, in1=in1, op=op), reads, writes,
              nowaw)

    def act(out, in_, func, reads=(), writes=(), nowaw=False, **kw):
        sc.op('act', lambda e: e.activation(out=out, in_=in_, func=func, **kw), reads, writes,
              nowaw)

    def mm(out, lhsT, rhs, start, stop, reads=(), writes=(), **kw):
        sc.op('pe', lambda e: e.matmul(out=out, lhsT=lhsT, rhs=rhs, start=start, stop=stop, **kw),
              reads, writes)

    def tr(out, in_, ident, reads=(), writes=()):
        sc.op('pe', lambda e: e.transpose(out=out, in_=in_, identity=ident), reads, writes)

    def load_gain_T(sbf, name, dram, ncol):
        t = sbf(name, [128, ncol], F32)
        b = Buf(name)
        sc.dma('sp', t[:], dram.rearrange('(kc p) -> p kc', p=128), writes=[b], sem='const',
               allow_slow_non_contiguous=True)
        return t, b

    def load_bc(sbf, name, dram_ap, n):
        t = sbf(name, [128, n], F32)
        b = Buf(name)
        sc.dma('sp', t[:], dram_ap.partition_broadcast(128), writes=[b], sem='const')
        return t, b

    def rstd_from_ss(st, bst, n, dim):
        ts('dve', st[:, n:2 * n], st[:, 0:n], 1.0 / dim, EPS, ALU.mult, ALU.add,
           reads=[bst], writes=[bst])
        act(st[:, 2 * n:3 * n], st[:, n:2 * n], AF.Sqrt, reads=[bst], writes=[bst])
        sc.op('dve', lambda e: e.reciprocal(out=st[:, 3 * n:4 * n], in_=st[:, 2 * n:3 * n]),
              reads=[bst], writes=[bst])
        return st[:, 3 * n:4 * n]

    def make_norm_transpose(sbf):
        sq_junk = sbf('sq_junk', [128, D], BF16)
        b_sq_junk = Buf('sq_junk')
        xn_bf = [sbf('xn_bf%d' % i, [128, D], BF16) for i in range(2)]
        b_xn_bf = [Buf('xn_bf%d' % i) for i in range(2)]
        stat = [sbf('nstat%d' % i, [128, 4], F32) for i in range(2)]
        b_stat = [Buf('nstat%d' % i) for i in range(2)]
        state = {'i': 0}

        def norm_transpose(src_tile, b_src, gT, b_gT, dstT, b_dst, col0):
            i = state['i']
            state['i'] ^= 1
            st, bst = stat[i], b_stat[i]
            xb, bxb = xn_bf[i], b_xn_bf[i]
            act(sq_junk[:], src_tile, AF.Square, reads=[b_src], writes=[b_sq_junk, bst],
                accum_out=st[:, 0:1])
            rs = rstd_from_ss(st, bst, 1, D)
            act(xb[:], src_tile, AF.Copy, reads=[b_src, bst], writes=[bxb], scale=rs)
            ps, bps = next_ps()
            psv = ps[:].bitcast(BF16)
            for kc in range(8):
                tr(psv[:, kc * 128:(kc + 1) * 128], xb[:, kc * 128:(kc + 1) * 128], ident_b[:],
                   reads=[bxb, b_ident_b], writes=[bps])
            for kc in range(8):
                ts('dve', dstT[:, kc, col0:col0 + 128], psv[:, kc * 128:(kc + 1) * 128],
                   gT[:, kc:kc + 1], None, ALU.mult, reads=[bps, b_gT], writes=[b_dst],
                   nowaw=True)
        return norm_transpose

    HALF = 1024

    def ffn_phase(src_d, b_srcd, dst_d, b_dstd, gname, wgn, wun, wdn, post_gname=None,
                  postT=None, b_postT=None):
        pes = ExitStack()

        def sbf(name, shape, dt):
            return pes.enter_context(nc.sbuf_tensor(un(name), list(shape), dt))
        norm_transpose = make_norm_transpose(sbf)
        gT, b_gT = load_gain_T(sbf, 'gT_' + gname, v_d[gname], 8)
        if post_gname:
            pgT, b_pgT = load_gain_T(sbf, 'gT_' + post_gname, v_d[post_gname], 8)
        xt = [sbf('xt%d' % i, [128, D], F32) for i in range(3)]
        b_xt = [Buf('xt%d' % i) for i in range(3)]
        xnT = sbf('xnT', [128, 8, HALF], BF16)
        b_xnT = Buf('xnT')
        actT = sbf('actT', [128, NFC, HALF], BF16)
        b_actT = [Buf('actT%d' % i) for i in range(NFC)]
        wd_sb = sbf('wd_sb', [128, NFC, D], BF16)
        b_wd = Buf('wd_sb')
        WC = 256
        wg_sb = [sbf('wg_sb%d' % i, [128, 8, WC], BF16) for i in range(2)]
        wu_sb = [sbf('wu_sb%d' % i, [128, 8, WC], BF16) for i in range(2)]
        b_wg = [Buf('wg%d' % i) for i in range(2)]
        b_wu = [Buf('wu%d' % i) for i in range(2)]
        sg_sb = [sbf('sg_sb%d' % i, [128, 512], BF16) for i in range(2)]
        b_sg = [Buf('sg%d' % i) for i in range(2)]
        ho = [sbf('ho%d' % i, [128, D], F32) for i in range(2)]
        b_ho = [Buf('ho%d' % i) for i in range(2)]
        st = {'xt': 0, 'w': 0, 'sg': 0, 'ho': 0}
        wg_d, wu_d, wdd = wb_d[wgn], wb_d[wun], wb_d[wdn]
        sc.dma('sp', wd_sb[:], wdd.rearrange('(fc p) d -> p fc d', p=128),
               reads=[wb_buf[wdn]], writes=[b_wd], sem='wd')
        for half in range(TOK // HALF):
            for t in range(HALF // 128):
                i = st['xt']
                st['xt'] = (i + 1) % 3
                r0 = half * HALF + t * 128
                sc.dma('sp', xt[i][:], src_d[r0:r0 + 128, :], reads=[b_srcd], writes=[b_xt[i]],
                       sem='xt%d' % i)
                norm_transpose(xt[i][:], b_xt[i], gT, b_gT, xnT, b_xnT, t * 128)
            for wc in range(DFF // WC):
                wi = st['w']
                st['w'] ^= 1
                sc.dma('sp', wg_sb[wi][:],
                       wg_d[:, wc * WC:(wc + 1) * WC].rearrange('(kc p) f -> p kc f', p=128),
                       reads=[wb_buf[wgn]], writes=[b_wg[wi]], sem='wg%d' % wi)
                sc.dma('sp', wu_sb[wi][:],
                       wu_d[:, wc * WC:(wc + 1) * WC].rearrange('(kc p) f -> p kc f', p=128),
                       reads=[wb_buf[wun]], writes=[b_wu[wi]], sem='wu%d' % wi)
                for fl in range(WC // 128):
                    fc = wc * (WC // 128) + fl
                    for tb in range(HALF // 512):
                        pg, bpg = next_ps()
                        pu, bpu = next_ps()
                        for kc in range(8):
                            mm(pg[:], wg_sb[wi][:, kc, fl * 128:(fl + 1) * 128],
                               xnT[:, kc, tb * 512:(tb + 1) * 512], kc == 0, kc == 7,
                               reads=[b_wg[wi], b_xnT], writes=[bpg])
                        for kc in range(8):
                            mm(pu[:], wu_sb[wi][:, kc, fl * 128:(fl + 1) * 128],
                               xnT[:, kc, tb * 512:(tb + 1) * 512], kc == 0, kc == 7,
                               reads=[b_wu[wi], b_xnT], writes=[bpu])
                        si = st['sg']
                        st['sg'] ^= 1
                        act(sg_sb[si][:], pg[:], AF.Silu, reads=[bpg], writes=[b_sg[si]])
                        tt('dve', actT[:, fc, tb * 512:(tb + 1) * 512], pu[:], sg_sb[si][:],
                           ALU.mult, reads=[bpu, b_sg[si]], writes=[b_actT[fc]])
            for t in range(HALF // 128):
                r0 = half * HALF + t * 128
                i = st['xt']
                st['xt'] = (i + 1) % 3
                sc.dma('sp', xt[i][:], src_d[r0:r0 + 128, :], reads=[b_srcd], writes=[b_xt[i]],
                       sem='xt%d' % i)
                hi = st['ho']
                st['ho'] ^= 1
                for dh in range(2):
                    pd, bpd = next_ps()
                    for fc in range(NFC):
                        mm(pd[:], actT[:, fc, t * 128:(t + 1) * 128],
                           wd_sb[:, fc, dh * 512:(dh + 1) * 512], fc == 0, fc == NFC - 1,
                           reads=[b_actT[fc], b_wd], writes=[bpd])
                    sc.op('dve', lambda e, pd=pd, hi=hi, i=i, dh=dh: e.scalar_tensor_tensor(
                        out=ho[hi][:, dh * 512:(dh + 1) * 512], in0=pd[:], scalar=0.5,
                        in1=xt[i][:, dh * 512:(dh + 1) * 512], op0=ALU.mult, op1=ALU.add),
                        reads=[bpd, b_xt[i]], writes=[b_ho[hi]])
                sc.dma('sp', dst_d[r0:r0 + 128, :], ho[hi][:], reads=[b_ho[hi]], writes=[b_dstd],
                       sem='ho%d' % hi, nowaw=True)
                if post_gname:
                    norm_transpose(ho[hi][:], b_ho[hi], pgT, b_pgT, postT, b_postT, r0)
        sc.barrier()
        sc.run()
        pes.close()

    b_x = Buf('x_d')
    ffn_phase(x_d, b_x, h1_d, b_h1d, 'ffn1_norm', 'ffn1_w_gate', 'ffn1_w_up', 'ffn1_w_down',
              post_gname='mix_norm', postT=uT, b_postT=b_uT)

    if stage == 'ffn1':
        es.close()
        print('ninst', sc.ninst, 'max sem', max(sc.cnt.values()), max(sc.dsem.values()), 'nsem', len(sc.semobj))
        return nc, dbg

    def phase_a2():
        pes = ExitStack()

        def sbf(name, shape, dt):
            return pes.enter_context(nc.sbuf_tensor(un(name), list(shape), dt))
        NG = 3280
        win_sb = sbf('win_sb', [128, 8, NG], BF16)
        b_win = Buf('win_sb')
        wv = wb_d['w_in']

        def ldw(dst0, c0, c1):
            sc.dma('sp', win_sb[:, :, dst0:dst0 + (c1 - c0)],
                   wv[:, c0:c1].rearrange('(kc p) f -> p kc f', p=128),
                   reads=[wb_buf['w_in']], writes=[b_win], sem='win')
        GOFF = {k: v[0] for k, v in SPLITS.items()}
        for q4 in range(4):
            ldw(q4 * 820, q4 * 820, (q4 + 1) * 820)

        gq2 = sbf('gq2', [128, 1], F32)
        gk2 = sbf('gk2', [128, 1], F32)
        gmq = sbf('gmq', [128, 1], F32)
        b_g2 = Buf('g2')
        for hh in range(2):
            sc.dma('sp', gq2[hh * 64:(hh + 1) * 64, :],
                   v_d['fox_q_norm'].rearrange('(p o) -> p o', o=1), writes=[b_g2], sem='const')
            sc.dma('sp', gk2[hh * 64:(hh + 1) * 64, :],
                   v_d['fox_k_norm'].rearrange('(p o) -> p o', o=1), writes=[b_g2], sem='const')
        sc.dma('sp', gmq[:], v_d['mem_q_norm'].rearrange('(p o) -> p o', o=1), writes=[b_g2],
               sem='const')
        gdq, b_gdq = load_bc(sbf, 'gdq', v_d['dsa_q_norm'].ap(), 64)
        gdk, b_gdk = load_bc(sbf, 'gdk', v_d['dsa_k_norm'].ap(), 64)
        bfor, b_bfor = load_bc(sbf, 'bfor', v_d['b_forget'].ap(), 8)
        invf, b_invf = load_bc(sbf, 'invf_sb', invf_d.ap(), 32)
        posi = sbf('posi', [128, NT], I32)
        posf = sbf('posf', [128, NT], F32)
        b_pos = Buf('pos')
        sc.dma('sp', posi[:], pos_d.rearrange('(t p) -> p t', p=128), writes=[b_pos], sem='const',
               allow_slow_non_contiguous=True)
        sc.op('dve', lambda e: e.tensor_copy(out=posf[:], in_=posi[:]), reads=[b_pos],
              writes=[b_pos])
        ang = sbf('ang', [128, NT, 64], F32)
        kk = sbf('kk', [128, NT, 64], F32)
        kki = sbf('kki', [128, NT, 64], I32)
        cs = sbf('cs', [128, NT, 64], F32)
        b_ang = Buf('ang')
        b_cs = Buf('cs')
        TWO_PI = float(2 * np.pi)
        for t in range(NT):
            ts('dve', ang[:, t, 32:64], invf[:], posf[:, t:t + 1], None, ALU.mult,
               reads=[b_invf, b_pos], writes=[b_ang])
        ts('dve', ang[:, :, 0:32], ang[:, :, 32:64], float(np.pi / 2), None, ALU.add,
           reads=[b_ang], writes=[b_ang])
        ts('dve', kk[:], ang[:], 1.0 / TWO_PI, 0.5, ALU.mult, ALU.add, reads=[b_ang], writes=[b_ang])
        sc.op('dve', lambda e: e.tensor_copy(out=kki[:], in_=kk[:]), reads=[b_ang], writes=[b_ang])
        sc.op('dve', lambda e: e.tensor_copy(out=kk[:], in_=kki[:]), reads=[b_ang], writes=[b_ang])
        sc.op('dve', lambda e: e.scalar_tensor_tensor(out=ang[:], in0=kk[:], scalar=-TWO_PI,
                                                      in1=ang[:], op0=ALU.mult, op1=ALU.add),
              reads=[b_ang], writes=[b_ang])
        ts('dve', kk[:], ang[:], float(-np.pi), TWO_PI, ALU.is_lt, ALU.mult, reads=[b_ang],
           writes=[b_ang])
        tt('dve', ang[:], ang[:], kk[:], ALU.add, reads=[b_ang], writes=[b_ang])
        ts('dve', kk[:], ang[:], float(np.pi), -TWO_PI, ALU.is_gt, ALU.mult, reads=[b_ang],
           writes=[b_ang])
        tt('dve', ang[:], ang[:], kk[:], ALU.add, reads=[b_ang], writes=[b_ang])
        ts('dve', ang[:], ang[:], float(np.pi), float(-np.pi), ALU.min, ALU.max, reads=[b_ang],
           writes=[b_ang])
        act(cs[:], ang[:], AF.Sin, reads=[b_ang], writes=[b_cs])
        abq = sbf('abq', [128, NT, 4, 32], F32)
        abk = sbf('abk', [128, NT, 4, 32], F32)
        b_abq = Buf('abq')
        b_abk = Buf('abk')
        for (ab, bab, g, bg) in [(abq, b_abq, gdq, b_gdq), (abk, b_abk, gdk, b_gdk)]:
            g1 = g[:, 0:32].unsqueeze(1).broadcast_to([128, NT, 32])
            g2 = g[:, 32:64].unsqueeze(1).broadcast_to([128, NT, 32])
            tt('dve', ab[:, :, 0, :], cs[:, :, 0:32], g1, ALU.mult, reads=[b_cs, bg], writes=[bab])
            tt('dve', ab[:, :, 1, :], cs[:, :, 32:64], g2, ALU.mult, reads=[b_cs, bg], writes=[bab])
            tt('dve', ab[:, :, 2, :], cs[:, :, 0:32], g2, ALU.mult, reads=[b_cs, bg], writes=[bab])
            tt('dve', ab[:, :, 3, :], cs[:, :, 32:64], g1, ALU.mult, reads=[b_cs, bg], writes=[bab])

        sqf = sbf('sqf', [128, 512], F32)
        b_sqf = Buf('sqf')
        st8 = sbf('st8', [128, 32], F32)
        b_st8 = Buf('st8')
        r1 = sbf('r1', [128, 8, 32], F32)
        r2 = sbf('r2', [128, 8, 32], F32)
        ro = sbf('ro', [128, 8, 64], F32)
        b_r1, b_r2, b_ro = Buf('r1'), Buf('r2'), Buf('ro')
        tok_bf = [sbf('tok_bf%d' % i, [128, 512], BF16) for i in range(2)]
        b_tok_bf = [Buf('tok_bf%d' % i) for i in range(2)]
        tT = [sbf('tT%d' % i, [128, 4, 128], BF16) for i in range(2)]
        b_tT = [Buf('tT%d' % i) for i in range(2)]
        pack6 = sbf('pack6', [128, 128], BF16)
        b_pack6 = Buf('pack6')
        p6T = sbf('p6T', [128, 128], BF16)
        b_p6T = Buf('p6T')
        dvb = sbf('dvb', [128, 65], BF16)
        b_dvb = Buf('dvb')
        fvst = sbf('fvst', [128, 8, 65], BF16)
        b_fvst = Buf('fvst')
        sc.op('dve', lambda e: e.memset(dvb[:], 1.0), writes=[b_dvb])
        sc.op('dve', lambda e: e.memset(fvst[:], 1.0), writes=[b_fvst])
        lf = sbf('lf', [128, 3, 8], F32)
        b_lf = Buf('lf')
        absw = sbf('absw', [128, 8], F32)
        b_absw = Buf('absw')
        stt = {'tb': 0, 'tT': 0}
        lfall = sbf('lfall', [128, NT, 8], F32)
        b_lfall = Buf('lfall')
        if stage == 'b':
            lfdbg = sbf('lfdbg', [128, NT, 3, 8], F32)
            b_lfdbg = Buf('lfdbg')

        def nxt(key, n=2):
            i = stt[key]
            stt[key] = (i + 1) % n
            return i

        def rope(src_ps, bsrc, nh, tabs, btabs, t, dst, bdst, plain):
            sv = src_ps.rearrange('p (h two d) -> p h two d', two=2, d=32)
            x1 = sv[:, :, 0, :]
            x2 = sv[:, :, 1, :]

            def tb(j):
                if plain:
                    a = cs[:, t, 0:32] if j in (0, 2) else cs[:, t, 32:64]
                else:
                    a = tabs[:, t, j, :]
                return a.unsqueeze(1).broadcast_to([128, nh, 32])
            A_, B_, C_, D_ = tb(0), tb(1), tb(2), tb(3)
            tt('dve', r1[:, 0:nh, :], x1, A_, ALU.mult, reads=[bsrc, btabs], writes=[b_r1])
            tt('dve', r2[:, 0:nh, :], x2, B_, ALU.mult, reads=[bsrc, btabs], writes=[b_r2])
            tt('dve', dst[:, 0:nh, 0:32], r1[:, 0:nh, :], r2[:, 0:nh, :], ALU.subtract,
               reads=[b_r1, b_r2], writes=[bdst])
            tt('dve', r1[:, 0:nh, :], x2, C_, ALU.mult, reads=[bsrc, btabs], writes=[b_r1])
            tt('dve', r2[:, 0:nh, :], x1, D_, ALU.mult, reads=[bsrc, btabs], writes=[b_r2])
            tt('dve', dst[:, 0:nh, 32:64], r1[:, 0:nh, :], r2[:, 0:nh, :], ALU.add,
               reads=[b_r1, b_r2], writes=[bdst])

        def head_ss(ps, bps, nh, hd):
            act(sqf[:, 0:nh * hd], ps[:, 0:nh * hd], AF.Square, reads=[bps], writes=[b_sqf])
            sc.op('dve', lambda e: e.tensor_reduce(
                out=st8[:, 0:nh], in_=sqf[:, 0:nh * hd].rearrange('p (h d) -> p h d', d=hd),
                axis=AX.X, op=ALU.add), reads=[b_sqf], writes=[b_st8])
            return rstd_from_ss(st8, b_st8, nh, hd)

        def transpose_out(src_bf, bsrc, gcol, bg, dst_dram, bdst, col0):
            ps, bps = next_ps()
            psv = ps[:].bitcast(BF16)
            for k4 in range(4):
                tr(psv[:, k4 * 128:(k4 + 1) * 128], src_bf[:, k4 * 128:(k4 + 1) * 128], ident_b[:],
                   reads=[bsrc, b_ident_b], writes=[bps])
            i = nxt('tT')
            if gcol is None:
                act(tT[i][:].rearrange('p a b -> p (a b)'), psv[:, 0:512], AF.Copy, reads=[bps],
                    writes=[b_tT[i]])
            else:
                ts('dve', tT[i][:].rearrange('p a b -> p (a b)'), psv[:, 0:512], gcol, None,
                   ALU.mult, reads=[bps, bg], writes=[b_tT[i]])
            if dst_dram is None:
                for pr in range(2):
                    sc.dma('sp', fkT_in[pr].rearrange('(hp p) t -> p hp t', p=128)[
                        :, :, col0:col0 + 128], tT[i][:, 2 * pr:2 * pr + 2, :], reads=[b_tT[i]],
                        writes=[bdst], sem='tTo%d' % i, nowaw=True)
            else:
                sc.dma('sp', dst_dram[:, :, col0:col0 + 128], tT[i][:], reads=[b_tT[i]],
                       writes=[bdst], sem='tTo%d' % i, nowaw=True)

        def proj(t, goff, n):
            ps, bps = next_ps()
            for kc in range(8):
                mm(ps[:, 0:n], uT[:, kc, t * 128:(t + 1) * 128], win_sb[:, kc, goff:goff + n],
                   kc == 0, kc == 7, reads=[b_uT, b_win], writes=[bps])
            return ps, bps

        for t in range(NT):
            c0 = t * 128
            pff, bpff = proj(t, GOFF['ff'], 8)
            pdd, bpdd = proj(t, GOFF['dk'], 128)
            pii, bpii = proj(t, GOFF['ik'], 72)
            act(lf[:, 1, :], pff[:, 0:8], AF.Copy, reads=[bpff], writes=[b_lf])
            tt('dve', lf[:, 0, :], lf[:, 1, :], bfor[:], ALU.add, reads=[b_lf, b_bfor],
               writes=[b_lf])
            act(lf[:, 1, :], lf[:, 0, :], AF.Exp, reads=[b_lf], writes=[b_lf], scale=-1.0)
            ts('dve', lf[:, 1, :], lf[:, 1, :], 1.0, None, ALU.add, reads=[b_lf], writes=[b_lf])
            act(lf[:, 2, :], lf[:, 1, :], AF.Ln, reads=[b_lf], writes=[b_lf])
            ts('dve', lfall[:, t, :], lf[:, 2, :], -1.0, None, ALU.mult, reads=[b_lf],
               writes=[b_lfall], nowaw=True)
            if stage == 'b':
                sc.op('dve', lambda e, t=t: e.tensor_copy(out=lfdbg[:, t, :, :], in_=lf[:]),
                      reads=[b_lf], writes=[b_lfdbg], nowaw=True)
            act(absw[:], pii[:, 64:72], AF.Abs, reads=[bpii], writes=[b_absw])
            act(isign[:, t, :], pii[:, 64:72], AF.Sign, reads=[bpii], writes=[b_isign])
            act(sqf[:, 0:64], pdd[:, 0:64], AF.Square, reads=[bpdd], writes=[b_sqf, b_st8],
                accum_out=st8[:, 0:1])
            rsk = rstd_from_ss(st8, b_st8, 1, 64)
            rope(pdd[:, 0:64], bpdd, 1, abk, b_abk, t, ro, b_ro, False)
            ts('dve', pack6[:, 0:64], ro[:, 0, :], rsk, None, ALU.mult, reads=[b_ro, b_st8],
               writes=[b_pack6])
            rope(pii[:, 0:64], bpii, 1, cs, b_cs, t, ro, b_ro, True)
            sc.op('dve', lambda e: e.tensor_copy(out=pack6[:, 64:128], in_=ro[:, 0, :]),
                  reads=[b_ro], writes=[b_pack6])
            act(dvb[:, 0:64], pdd[:, 64:128], AF.Copy, reads=[bpdd], writes=[b_dvb])
            sc.dma('sp', kin_dvA[c0:c0 + 128, :], dvb[:], reads=[b_dvb], writes=[b_kind], sem='dvo',
                   nowaw=True)
            ps, bps = next_ps()
            psv = ps[:].bitcast(BF16)
            tr(psv[:, 0:128], pack6[:], ident_b[:], reads=[b_pack6, b_ident_b], writes=[bps])
            act(p6T[:], psv[:, 0:128], AF.Copy, reads=[bps], writes=[b_p6T])
            sc.dma('sp', kin_dikT[:, c0:c0 + 128], p6T[:], reads=[b_p6T], writes=[b_kind],
                   sem='p6o', nowaw=True)
            for nm, gcol, dst, bdst in [('fq', gq2, fqT_d, b_fqTd), ('fk', gk2, None, b_kind)]:
                ps, bps = proj(t, GOFF[nm], 512)
                rs = head_ss(ps, bps, 8, 64)
                i = nxt('tb')
                tt('dve', tok_bf[i][:].rearrange('p (h d) -> p h d', d=64),
                   ps[:].rearrange('p (h d) -> p h d', d=64),
                   rs.unsqueeze(2).broadcast_to([128, 8, 64]), ALU.mult,
                   reads=[bps, b_st8], writes=[b_tok_bf[i]])
                transpose_out(tok_bf[i], b_tok_bf[i], gcol[:, 0:1], b_g2, dst, bdst, c0)
            ps, bps = proj(t, GOFF['fv'], 512)
            act(fvst[:, :, 0:64], ps[:].rearrange('p (h d) -> p h d', d=64), AF.Copy, reads=[bps],
                writes=[b_fvst])
            for hp_ in range(4):
                sc.dma('sp', fvA_in[hp_][c0:c0 + 128, :],
                       fvst[:, 2 * hp_:2 * hp_ + 2, :].rearrange('p e c -> p (e c)'),
                       reads=[b_fvst], writes=[b_kind], sem='fvo', nowaw=True)
            ps, bps = proj(t, GOFF['dq'], 512)
            rs = head_ss(ps, bps, 8, 64)
            rope(ps[:], bps, 8, abq, b_abq, t, ro, b_ro, False)
            i = nxt('tb')
            tt('dve', tok_bf[i][:].rearrange('p (h d) -> p h d', d=64), ro[:],
               rs.unsqueeze(2).broadcast_to([128, 8, 64]), ALU.mult,
               reads=[b_ro, b_st8], writes=[b_tok_bf[i]])
            transpose_out(tok_bf[i], b_tok_bf[i], None, None, dqT_d, b_dqTd, c0)
            ps, bps = proj(t, GOFF['iq'], 512)
            rope(ps[:], bps, 8, cs, b_cs, t, ro, b_ro, True)
            i = nxt('tb')
            tt('dve', tok_bf[i][:].rearrange('p (h d) -> p h d', d=64), ro[:],
               absw[:].unsqueeze(2).broadcast_to([128, 8, 64]), ALU.mult,
               reads=[b_ro, b_absw], writes=[b_tok_bf[i]])
            transpose_out(tok_bf[i], b_tok_bf[i], None, None, iqT_d, b_iqTd, c0)
            ps, bps = proj(t, GOFF['mq'], 512)
            rs = head_ss(ps, bps, 4, 128)
            i = nxt('tb')
            tt('dve', tok_bf[i][:].rearrange('p (h d) -> p h d', d=128),
               ps[:].rearrange('p (h d) -> p h d', d=128),
               rs.unsqueeze(2).broadcast_to([128, 4, 128]), ALU.mult,
               reads=[bps, b_st8], writes=[b_tok_bf[i]])
            transpose_out(tok_bf[i], b_tok_bf[i], gmq[:, 0:1], b_g2, mqT_d, b_mqTd, c0)
        sc.dma('sp', lin_d.rearrange('(t p) h -> p t h', p=128), lfall[:], reads=[b_lfall],
               writes=[b_lind], sem='lfo')
        if stage == 'b':
            d1 = dscr('lf_dbg', [128, NT, 3, 8], F32, out=True)
            sc.dma('sp', d1[:, :, :, :], lfdbg[:], reads=[b_lfdbg], writes=[Buf('d1')], sem='dbg0')
            d2 = dscr('bfor_dbg', [128, 8], F32, out=True)
            sc.dma('sp', d2[:, :], bfor[:], reads=[b_bfor], writes=[Buf('d2')], sem='dbg0')
            d3 = dscr('wff_dbg', [128, 8, 8], BF16, out=True)
            sc.dma('sp', d3[:, :, :], win_sb[:, :, 3072:3080], reads=[b_win], writes=[Buf('d3')],
                   sem='dbg0')
        sc.barrier()
        sc.run()
        pes.close()

    phase_a2()
    es_a.close()
    o_sb = {n: sbg('o_' + n, [128, NT, 512], BF16) for n in 'abc'}
    b_o = {n: Buf('o_' + n) for n in 'abc'}
    if stage == 'a2':
        es.close()
        print('ninst', sc.ninst, 'max sem', max(sc.cnt.values()), max(sc.dsem.values()), 'nsem', len(sc.semobj))
        return nc, dbg

    RG = [[0, 1, 2, 3], [4, 5, 6, 7]]
    b_kall, b_lall = Buf('kall'), Buf('lall')

    def gather(name, src, rows, cols, dt, b_src, b_dst):
        g = g_d[name]
        sc.custom('pool', lambda e: e.collective_compute(
            'AllGather', ALU.bypass, replica_groups=RG, ins=[src.ap().opt()],
            outs=[g.ap().opt()]), reads=[b_src], writes=[b_dst], sem='cc')
        return g
    if stage == 'b':
        lo0 = dscr('lin_dbg0', [TOK, 8], F32, out=True)
        b_lo0 = Buf('lo0')
        sc.dma('sp', lo0[:, :], lin_d[:, :], reads=[b_lind], writes=[b_lo0], sem='dbg0')
    bar_in = dscr('bar_in', [1, 64], F32)
    bar_out = dscr('bar_out', [1, 64], F32)
    b_bar = Buf('bar')
    sc.dma('sp', bar_in[:, :], ident_d[0:1, 0:64], reads=[b_kind, b_lind], writes=[b_bar],
           sem='bar')
    sc.custom('pool', lambda e: e.collective_compute(
        'AllReduce', ALU.add, replica_groups=RG, ins=[bar_in.ap().opt()],
        outs=[bar_out.ap().opt()]), reads=[b_bar, b_kind, b_lind], writes=[b_bar, b_kind, b_lind],
        sem='cc')
    lall_d = gather('lall', lin_d, TOK, 8, F32, b_lind, b_lall)
    fkT_g = [gather('fkT_g%d' % i, fkT_in[i], 256, TOK, BF16, b_kind, b_kall) for i in range(2)]
    fvA_g = [gather('fvA_g%d' % i, fvA_in[i], TOK, 130, BF16, b_kind, b_kall) for i in range(4)]
    dikT_g = gather('dikT_g', dikT_in, 128, TOK, BF16, b_kind, b_kall)
    dvA_g = gather('dvA_g', dvA_in, TOK, 65, BF16, b_kind, b_kall)
    if stage == 'b':
        ld = dscr('lall_dbg', [4 * TOK, 8], F32, out=True)
        sc.dma('sp', ld[:, :], lall_d[:, :], reads=[b_lall], writes=[Buf('ldbg')], sem='dbg')
        lo_ = dscr('lin_dbg', [TOK, 8], F32, out=True)
        sc.dma('sp', lo_[:, :], lin_d[:, :], reads=[b_lind, b_lall], writes=[Buf('lodbg')],
               sem='dbg')
        kd = dscr('kall_dbg', [64, 2048], BF16, out=True)
        sc.dma('sp', kd[:, :], fkT_g[0][3 * 256:3 * 256 + 64, :], reads=[b_kall],
               writes=[Buf('kdbg')], sem='dbg')
        sc.barrier()
        sc.run()
        es.close()
        return nc, dbg

    def rank_of(jj):
        return jj if jj < 4 else 7 - jj

    NKI = [8 * (i // 2) + 4 if i % 2 == 0 else 8 * (i // 2) + 8 for i in range(NT)]
    BOFF = [0]
    for i in range(NT):
        BOFF.append(BOFF[-1] + NKI[i])

    def load_seq_T(q, dst, b_dst, p0, p1, gt, rows_per_rank, row0, sem):
        dv = dst[p0:p1, :].rearrange('p (m j c) -> p m j c', m=8, j=8)
        n = p1 - p0
        for r in range(4):
            src = gt[r * rows_per_rank + row0:r * rows_per_rank + row0 + n, :].rearrange(
                'p (m two c) -> p m two c', two=2, c=128)
            sc.dma(q, dv[:, :, r, :], src[:, :, 0, :], reads=[b_kall], writes=[b_dst], sem=sem,
                   nowaw=True)
            sc.dma(q, dv[:, :, 7 - r, :], src[:, :, 1, :], reads=[b_kall], writes=[b_dst], sem=sem,
                   nowaw=True)

    def load_seq_tok(q, dst, b_dst, gt, sem):
        dv = dst.rearrange('p (m j) c -> p m j c', j=8)
        for r in range(4):
            src = gt[r * TOK:(r + 1) * TOK, :].rearrange('(m two p) c -> p m two c', two=2, p=128)
            sc.dma(q, dv[:, :, r, :], src[:, :, 0, :], reads=[b_kall], writes=[b_dst], sem=sem,
                   nowaw=True)
            sc.dma(q, dv[:, :, 7 - r, :], src[:, :, 1, :], reads=[b_kall], writes=[b_dst], sem=sem,
                   nowaw=True)

    def phase_c1():
        pes = ExitStack()

        def sbf(name, shape, dt):
            return pes.enter_context(nc.sbuf_tensor(un(name), list(shape), dt))
        tri = sbf('tri_sb', [128, 128], F32)
        ones_f = sbf('ones_f', [128, 128], F32)
        kposc = sbf('kposc_sb', [128, 64], F32)
        qpos_bc = sbf('qpos_bc', [128, TOK], F32)
        b_cst = Buf('c1const')
        sc.dma('sp', tri[:], tri_d[:, :], writes=[b_cst], sem='const', nowaw=True)
        sc.dma('sp', kposc[:], kposc_d[:, :], writes=[b_cst], sem='const', nowaw=True)
        sc.dma('sp', qpos_bc[:], qposf_d.ap().partition_broadcast(128), writes=[b_cst],
               sem='const', nowaw=True)
        b_ones = Buf('ones_f')
        sc.op('dve', lambda e: e.memset(ones_f[:], 1.0), writes=[b_ones])
        L_sb = sbf('L_sb', [128, 64, 8], F32)
        b_L = Buf('L_sb')
        Lv = L_sb[:].rearrange('p (m j) h -> p m j h', j=8)
        for r in range(4):
            src = lall_d[r * TOK:(r + 1) * TOK, :].rearrange('(m two p) h -> p m two h',
                                                              two=2, p=128)
            sc.dma('sp', Lv[:, :, r, :], src[:, :, 0, :], reads=[b_lall], writes=[b_L],
                   sem='Lld', nowaw=True)
            sc.dma('sp', Lv[:, :, 7 - r, :], src[:, :, 1, :], reads=[b_lall], writes=[b_L],
                   sem='Lld', nowaw=True)
        Lf = L_sb[:].rearrange('p g h -> p (g h)')
        cinc = sbf('cinc', [128, 64, 8], F32)
        tot = sbf('tot', [128, 64, 8], F32)
        scA = sbf('scA', [128, 64, 8], F32)
        scB = sbf('scB', [128, 64, 8], F32)
        c_all = sbf('c_all', [128, 64, 8], F32)
        Tpre = sbf('Tpre', [128, 64, 8], F32)
        b_cinc, b_tot, b_scA, b_scB, b_call, b_Tpre = [Buf(n) for n in
                                                       ['cinc', 'tot', 'scA', 'scB', 'c_all', 'Tpre']]
        p1, bp1 = next_ps()
        mm(p1[:], tri[:], Lf, True, True, reads=[b_cst, b_L], writes=[bp1])
        p2, bp2 = next_ps()
        mm(p2[:], ones_f[:], Lf, True, True, reads=[b_ones, b_L], writes=[bp2])
        act(cinc[:].rearrange('p g h -> p (g h)'), p1[:], AF.Copy, reads=[bp1], writes=[b_cinc])
        act(tot[:].rearrange('p g h -> p (g h)'), p2[:], AF.Copy, reads=[bp2], writes=[b_tot])
        sc.op('dve', lambda e: e.tensor_copy(out=scA[:], in_=tot[:]), reads=[b_tot], writes=[b_scA])
        cur, bcur, oth, both = scA, b_scA, scB, b_scB
        sft = 1
        while sft < 64:
            tt('dve', oth[:, sft:, :], cur[:, sft:, :], cur[:, :64 - sft, :], ALU.add,
               reads=[bcur], writes=[both])
            sc.op('dve', lambda e, oth=oth, cur=cur, sft=sft: e.tensor_copy(
                out=oth[:, :sft, :], in_=cur[:, :sft, :]), reads=[bcur], writes=[both])
            cur, bcur, oth, both = oth, both, cur, bcur
            sft *= 2
        tt('dve', Tpre[:], cur[:], tot[:], ALU.subtract, reads=[bcur, b_tot], writes=[b_Tpre])
        tt('dve', c_all[:], cinc[:], Tpre[:], ALU.add, reads=[b_cinc, b_Tpre], writes=[b_call])
        biasT = sbf('biasT', [128, BOFF[NT], 8], F32)
        b_biasT = Buf('biasT')
        negmT = sbf('negmT', [128, NT, 4, 128], BF16)
        b_negmT = Buf('negmT')
        mcol = sbf('mcol', [128, 4], F32)
        b_mcol = Buf('mcol')
        mrep = [sbf('mrep%d' % i, [128, 128], F32) for i in range(2)] * 2
        b_mrep = [Buf('mrep%d' % i) for i in range(2)] * 2
        cref = sbf('cref', [128, 8], F32)
        b_cref = Buf('cref')
        for i in range(NT):
            nk = NKI[i]
            g0 = nk - 4
            pc, bpc = next_ps()
            qmid = qpos_bc[:, i * 128 + 64:i * 128 + 65]
            for b in range(4):
                tt('dve', mcol[:, b:b + 1], qmid, kposc[:, g0 + b:g0 + b + 1], ALU.is_ge,
                   reads=[b_cst], writes=[b_mcol])
                ts('dve', mrep[b][:], ones_f[:], mcol[:, b:b + 1], None, ALU.mult,
                   reads=[b_ones, b_mcol], writes=[b_mrep[b]])
                mm(pc[:, 0:8], mrep[b][:], L_sb[:, g0 + b, :], b == 0, b == 3,
                   reads=[b_mrep[b], b_L], writes=[bpc])
                ts('dve', negmT[:, i, b, :], qpos_bc[:, i * 128:(i + 1) * 128],
                   kposc[:, g0 + b:g0 + b + 1], -30000.0, ALU.is_lt, ALU.mult,
                   reads=[b_cst], writes=[b_negmT], nowaw=True)
            tt('dve', cref[:], pc[:, 0:8], Tpre[:, g0, :], ALU.add, reads=[bpc, b_Tpre],
               writes=[b_cref])
            tt('dve', biasT[:, BOFF[i]:BOFF[i] + nk, :],
               cref[:].unsqueeze(1).broadcast_to([128, nk, 8]), c_all[:, 0:nk, :], ALU.subtract,
               reads=[b_cref, b_call], writes=[b_biasT], nowaw=True)
            ts('dve', biasT[:, BOFF[i]:BOFF[i] + nk, :], biasT[:, BOFF[i]:BOFF[i] + nk, :], 60.0,
               None, ALU.min, reads=[b_biasT], writes=[b_biasT], nowaw=True)
        if stage == 'c1a':
            cd = dscr('call_dbg', [128, 64, 8], F32, out=True)
            sc.dma('sp', cd[:, :, :], c_all[:], reads=[b_call], writes=[Buf('cad')], sem='dbg')
            bd = dscr('bias_dbg', [128, BOFF[NT], 8], F32, out=True)
            sc.dma('sp', bd[:, :, :], biasT[:], reads=[b_biasT], writes=[Buf('bad')], sem='dbg')
            sc.barrier()
            sc.run()
            pes.close()
            return
        fqT_sb = sbf('fqT_sb', [128, 4, TOK], BF16)
        b_fqT = Buf('fqT_sb')
        sc.dma('sp', fqT_sb[:], fqT_d[:, :, :], reads=[b_fqTd], writes=[b_fqT], sem='fqld')
        KT = [sbf('KT%d' % i, [128, S], BF16) for i in range(2)]
        VA = sbf('VA', [128, 64, 130], BF16)
        b_KT = [Buf('KT%d' % i) for i in range(2)]
        b_VA = Buf('VA')
        Vp = [sbf('Vp%d' % i, [128, 64, 130], BF16) for i in range(2)]
        b_Vp = [Buf('Vp%d' % i) for i in range(2)]
        wexp = [sbf('wexp%d' % i, [128, 64, 2], F32) for i in range(2)]
        b_wexp = [Buf('wexp%d' % i) for i in range(2)]
        PT = [sbf('PT%d' % i, [128, 512], BF16) for i in range(2)]
        b_PT = [Buf('PT%d' % i) for i in range(2)]
        rcp = sbf('rcp', [128, 2], F32)
        b_rcp = Buf('rcp')
        cnt = {'s': 0, 'o': 0, 'p': 0, 'v': 0}
        for hp in range(4):
            kb = hp % 2
            load_seq_T('sp', KT[kb], b_KT[kb], 0, 128, g_d['fkT_g%d' % (hp // 2)], 256,
                       (hp % 2) * 128, 'KT%d' % kb)
            load_seq_tok('sp', VA[:], b_VA, g_d['fvA_g%d' % hp], 'VA')
            for i in range(NT):
                nk = NKI[i]
                vi = cnt['v'] % 2
                cnt['v'] += 1
                act(wexp[vi][:, 0:nk, :], biasT[:, BOFF[i]:BOFF[i] + nk, 2 * hp:2 * hp + 2], AF.Exp,
                    reads=[b_biasT], writes=[b_wexp[vi]])
                tt('dve', Vp[vi][:, 0:nk, :].rearrange('p k (e c) -> p k e c', e=2),
                   VA[:, 0:nk, :].rearrange('p k (e c) -> p k e c', e=2),
                   wexp[vi][:, 0:nk, :].unsqueeze(3).broadcast_to([128, nk, 2, 65]), ALU.mult,
                   reads=[b_VA, b_wexp[vi]], writes=[b_Vp[vi]])
                ob = 4 + 2 * (cnt['o'] % 2)
                cnt['o'] += 1
                for e in range(2):
                    h = 2 * hp + e
                    pO, bpO = psum[ob + e], psb[ob + e]
                    for g4 in range(nk // 4):
                        sbk = cnt['s'] % 4
                        cnt['s'] += 1
                        pS, bpS = psum[sbk], psb[sbk]
                        last = (g4 == nk // 4 - 1)
                        if last:
                            mm(pS[:], ident_b[:], negmT[:, i, :, :].rearrange('p a b -> p (a b)'),
                               True, False, reads=[b_ident_b, b_negmT], writes=[bpS])
                        for kk in range(4):
                            kt = g4 * 4 + kk
                            mm(pS[:, kk * 128:(kk + 1) * 128],
                               KT[kb][e * 64:(e + 1) * 64, kt * 128:(kt + 1) * 128],
                               fqT_sb[e * 64:(e + 1) * 64, hp, i * 128:(i + 1) * 128],
                               not last, (not last) or kk == 3, reads=[b_KT[kb], b_fqT],
                               writes=[bpS], skip_group_check=True)
                        pi = cnt['p'] % 2
                        cnt['p'] += 1
                        act(PT[pi][:], pS[:], AF.Exp, reads=[bpS], writes=[b_PT[pi]], scale=0.125)
                        for kk in range(4):
                            kt = g4 * 4 + kk
                            mm(pO[:, 0:65], PT[pi][:, kk * 128:(kk + 1) * 128],
                               Vp[vi][:, kt, e * 65:(e + 1) * 65], kt == 0, kt == nk - 1,
                               reads=[b_PT[pi], b_Vp[vi]], writes=[bpO])
                    sc.op('dve', lambda e_, pO=pO, e=e: e_.reciprocal(out=rcp[:, e:e + 1],
                                                                      in_=pO[:, 64:65]),
                          reads=[bpO], writes=[b_rcp])
                    ts('dve', o_sb['a'][:, i, h * 64:(h + 1) * 64], pO[:, 0:64], rcp[:, e:e + 1],
                       None, ALU.mult, reads=[bpO, b_rcp], writes=[b_o['a']], nowaw=True)
        if stage == 'c1':
            od = dscr('oa_dbg', [128, NT, 512], BF16, out=True)
            sc.dma('sp', od[:, :, :], o_sb['a'][:], reads=[b_o['a']], writes=[Buf('oad')],
                   sem='dbg')
            cd = dscr('call_dbg', [128, 64, 8], F32, out=True)
            sc.dma('sp', cd[:, :, :], c_all[:], reads=[b_call], writes=[Buf('cad')], sem='dbg')
        sc.barrier()
        sc.run()
        pes.close()

    def phase_c3():
        pes = ExitStack()

        def sbf(name, shape, dt):
            return pes.enter_context(nc.sbuf_tensor(un(name), list(shape), dt))
        norm_transpose = make_norm_transpose(sbf)
        gTm, b_gTm = load_gain_T(sbf, 'gT_mem', v_d['mem_norm'], 8)
        gmk = sbf('gmk', [128, 1], F32)
        b_gmk = Buf('gmk')
        sc.dma('sp', gmk[:], v_d['mem_k_norm'].rearrange('(p o) -> p o', o=1), writes=[b_gmk],
               sem='const')
        wkv = sbf('wkv', [128, 8, D], BF16)
        b_wkv = Buf('wkv')
        sc.dma('sp', wkv[:], wb_d['w_mem_kv'].rearrange('(kc p) f -> p kc f', p=128),
               reads=[wb_buf['w_mem_kv']], writes=[b_wkv], sem='wkv')
        memT = sbf('memT', [128, 8, 256], BF16)
        b_memT = Buf('memT')
        mt_sb = [sbf('memt%d' % i, [128, D], F32) for i in range(2)]
        b_mt = [Buf('memt%d' % i) for i in range(2)]
        for m in range(2):
            sc.dma('sp', mt_sb[m][:], mem_d[m * 128:(m + 1) * 128, :], writes=[b_mt[m]],
                   sem='memld')
            norm_transpose(mt_sb[m][:], b_mt[m], gTm, b_gTm, memT, b_memT, m * 128)
        kmT = sbf('kmT', [128, 4, 256], BF16)
        b_kmT = Buf('kmT')
        vma = sbf('vma', [128, 2, 4, 129], BF16)
        b_vma = Buf('vma')
        sc.op('dve', lambda e: e.memset(vma[:], 1.0), writes=[b_vma])
        sqf = sbf('sqf3', [128, 512], F32)
        b_sqf = Buf('sqf3')
        st8 = sbf('st83', [128, 16], F32)
        b_st8 = Buf('st83')
        kmb = sbf('kmb', [128, 512], BF16)
        b_kmb = Buf('kmb')
        for m in range(2):
            ps, bps = next_ps()
            for kc in range(8):
                mm(ps[:], memT[:, kc, m * 128:(m + 1) * 128], wkv[:, kc, 0:512], kc == 0, kc == 7,
                   reads=[b_memT, b_wkv], writes=[bps])
            act(sqf[:], ps[:], AF.Square, reads=[bps], writes=[b_sqf])
            sc.op('dve', lambda e: e.tensor_reduce(
                out=st8[:, 0:4], in_=sqf[:].rearrange('p (h d) -> p h d', d=128), axis=AX.X,
                op=ALU.add), reads=[b_sqf], writes=[b_st8])
            rs = rstd_from_ss(st8, b_st8, 4, 128)
            tt('dve', kmb[:].rearrange('p (h d) -> p h d', d=128),
               ps[:].rearrange('p (h d) -> p h d', d=128),
               rs.unsqueeze(2).broadcast_to([128, 4, 128]), ALU.mult, reads=[bps, b_st8],
               writes=[b_kmb])
            pt_, bpt = next_ps()
            ptv = pt_[:].bitcast(BF16)
            for h in range(4):
                tr(ptv[:, h * 128:(h + 1) * 128], kmb[:, h * 128:(h + 1) * 128], ident_b[:],
                   reads=[b_kmb, b_ident_b], writes=[bpt])
            ts('dve', kmT[:, :, m * 128:(m + 1) * 128],
               ptv[:, 0:512].rearrange('p (h k) -> p h k', k=128), gmk[:, 0:1], None, ALU.mult,
               reads=[bpt, b_gmk], writes=[b_kmT], nowaw=True)
            ps2, bps2 = next_ps()
            for kc in range(8):
                mm(ps2[:], memT[:, kc, m * 128:(m + 1) * 128], wkv[:, kc, 512:1024], kc == 0,
                   kc == 7, reads=[b_memT, b_wkv], writes=[bps2])
            act(vma[:, m, :, 0:128], ps2[:].rearrange('p (h d) -> p h d', d=128), AF.Copy,
                reads=[bps2], writes=[b_vma], nowaw=True)
        mqT_sb = sbf('mqT_sb', [128, 4, TOK], BF16)
        b_mqT = Buf('mqT_sb')
        sc.dma('sp', mqT_sb[:], mqT_d[:, :, :], reads=[b_mqTd], writes=[b_mqT], sem='mqld')
        zer = sbf('zer3', [128, 264], BF16)
        b_zer = Buf('zer3')
        sc.op('dve', lambda e: e.memset(zer[:], 0.0), writes=[b_zer])
        PTm = [sbf('PTm%d' % i, [128, 512], BF16) for i in range(2)]
        b_PTm = [Buf('PTm%d' % i) for i in range(2)]
        rc4 = sbf('rc43', [128, 4], F32)
        b_rc4 = Buf('rc43')
        cnt = {'o': 0, 'p': 0}
        for i in range(NT):
            ob = 4 + 2 * (cnt['o'] % 2)
            cnt['o'] += 1
            for half in range(2):
                pO, bpO = psum[ob + half], psb[ob + half]
                mm(pO[:, 0:264], zer[:, 0:128], zer[:, 0:264], True, False, reads=[b_zer],
                   writes=[bpO])
                pS, bpS = psum[half], psb[half]
                for hh in range(2):
                    h = 2 * half + hh
                    for m in range(2):
                        mm(pS[:, (hh * 2 + m) * 128:(hh * 2 + m + 1) * 128],
                           kmT[:, h, m * 128:(m + 1) * 128], mqT_sb[:, h, i * 128:(i + 1) * 128],
                           True, True, reads=[b_kmT, b_mqT], writes=[bpS], skip_group_check=True)
                pi = cnt['p'] % 2
                cnt['p'] += 1
                act(PTm[pi][:], pS[:], AF.Exp, reads=[bpS], writes=[b_PTm[pi]],
                    scale=float(128 ** -0.5))
                for hh in range(2):
                    h = 2 * half + hh
                    for m in range(2):
                        mm(pO[:, hh * 132:hh * 132 + 129],
                           PTm[pi][:, (hh * 2 + m) * 128:(hh * 2 + m + 1) * 128], vma[:, m, h, :],
                           False, m == 1, reads=[b_PTm[pi], b_vma], writes=[bpO],
                           skip_group_check=True)
                pv = pO[:, 0:264].rearrange('p (h c) -> p h c', c=132)
                sc.op('dve', lambda e, pv=pv, half=half: e.reciprocal(
                    out=rc4[:, half * 2:(half + 1) * 2], in_=pv[:, :, 128]), reads=[bpO],
                    writes=[b_rc4])
                tt('dve', o_sb['c'][:, i, half * 256:(half + 1) * 256].rearrange(
                    'p (h d) -> p h d', d=128), pv[:, :, 0:128],
                   rc4[:, half * 2:(half + 1) * 2].unsqueeze(2).broadcast_to([128, 2, 128]),
                   ALU.mult, reads=[bpO, b_rc4], writes=[b_o['c']], nowaw=True)
        sc.barrier()
        sc.run()
        pes.close()

    phase_c3()
    if stage == 'c3':
        od = dscr('oc_dbg', [128, NT, 512], BF16, out=True)
        sc.dma('sp', od[:, :, :], o_sb['c'][:], reads=[b_o['c']], writes=[Buf('ocd')], sem='dbg')
        od3 = dscr('ob_dbg', [128, NT, 512], BF16, out=True)
        sc.dma('sp', od3[:, :, :], o_sb['b'][:], reads=[b_o['b']], writes=[Buf('obd')], sem='dbg')
        od2 = dscr('oa_dbg', [128, NT, 512], BF16, out=True)
        sc.dma('sp', od2[:, :, :], o_sb['a'][:], reads=[b_o['a']], writes=[Buf('oad')], sem='dbg')
        sc.barrier()
        sc.run()
        es.close()
        return nc, dbg

    phase_c1()
    if stage in ('c1', 'c1a'):
        es.close()
        print('ninst', sc.ninst, 'max sem', max(sc.cnt.values()), max(sc.dsem.values()), 'nsem', len(sc.semobj))
        return nc, dbg

    NBIS = 25

    def phase_c2():
        pes = ExitStack()

        def sbf(name, shape, dt):
            return pes.enter_context(nc.sbuf_tensor(un(name), list(shape), dt))
        b_cst = Buf('c2const')
        iota = sbf('iota_sb', [128, 512], F32)
        sc.dma('sp', iota[:], iota512_d.ap().partition_broadcast(128), writes=[b_cst],
               sem='const', nowaw=True)
        pow2 = sbf('pow2_sb', [128, 32], F32)
        sc.dma('sp', pow2[:], pow2_d.ap().partition_broadcast(128), writes=[b_cst], sem='const',
               nowaw=True)
        qcol = sbf('qcol', [128, NT], F32)
        sc.dma('sp', qcol[:], qposf_d.rearrange('(t p) -> p t', p=128), writes=[b_cst],
               sem='const', nowaw=True, allow_slow_non_contiguous=True)
        ident4 = sbf('ident4', [128, 4, 128], BF16)
        b_id4 = Buf('ident4')
        for k4 in range(4):
            sc.op('dve', lambda e, k4=k4: e.tensor_copy(out=ident4[:, k4, :], in_=ident_b[:]),
                  reads=[b_ident_b], writes=[b_id4], nowaw=True)
        zer = sbf('zer', [128, 272], BF16)
        b_zer = Buf('zer')
        sc.op('dve', lambda e: e.memset(zer[:], 0.0), writes=[b_zer])
        dkT2 = sbf('dkT2', [128, S], BF16)
        ikT2 = sbf('ikT2', [128, S], BF16)
        dva = sbf('dva', [128, 64, 65], BF16)
        b_dk, b_ik, b_dva = Buf('dkT2'), Buf('ikT2'), Buf('dva')
        for hh in range(2):
            load_seq_T('sp', dkT2, b_dk, hh * 64, hh * 64 + 64, g_d['dikT_g'], 128, 0, 'dkld')
            load_seq_T('sp', ikT2, b_ik, hh * 64, hh * 64 + 64, g_d['dikT_g'], 128, 64, 'ikld')
        load_seq_tok('sp', dva[:], b_dva, g_d['dvA_g'], 'dvld')
        dqT_sb = sbf('dqT_sb', [128, 4, TOK], BF16)
        iqT_sb = sbf('iqT_sb', [128, 4, TOK], BF16)
        b_dqT, b_iqT = Buf('dqT_sb'), Buf('iqT_sb')
        sc.dma('sp', dqT_sb[:], dqT_d[:, :, :], reads=[b_dqTd], writes=[b_dqT], sem='dqld')
        sc.dma('sp', iqT_sb[:], iqT_d[:, :, :], reads=[b_iqTd], writes=[b_iqT], sem='iqld')
        score = sbf('score', [128, S], F32)
        negm2 = [sbf('negm%d' % i_, [128, S], BF16) for i_ in range(2)]
        b_negm2 = [Buf('negm%d' % i_) for i_ in range(2)]
        b_score = Buf('score')
        PTd = [sbf('PTd%d' % i, [128, 512], BF16) for i in range(3)]
        b_PTd = [Buf('PTd%d' % i) for i in range(3)]
        sm = sbf('sm', [128, 8], F32)
        b_sm = Buf('sm')
        steps = sbf('steps', [128, 32], F32)
        b_steps = Buf('steps')
        cneg = sbf('cneg', [128, 512], F32)
        b_cneg = Buf('cneg')
        rc4 = sbf('rc4', [128, 8], F32)
        b_rc4 = Buf('rc4')
        cnt = {'s': 0, 'p': 0, 'o': 0}

        def sbank():
            i = cnt['s'] % 4
            cnt['s'] += 1
            return psum[i], psb[i]
        def stage1(i):
            nk = NKI[i]
            n = nk * 128
            negm, b_negm = negm2[i % 2], b_negm2[i % 2]
            for c4 in range(nk // 4):
                for h in range(8):
                    e_, hp = h % 2, h // 2
                    ps, bps = sbank()
                    mm(ps[:], iqT_sb[e_ * 64:(e_ + 1) * 64, hp, i * 128:(i + 1) * 128],
                       ikT2[e_ * 64:(e_ + 1) * 64, c4 * 512:(c4 + 1) * 512], True, True,
                       reads=[b_iqT, b_ik], writes=[bps])
                    act(ps[:], ps[:], AF.Relu, reads=[bps], writes=[bps])
                    if h == 0:
                        ts('dve', score[:, c4 * 512:(c4 + 1) * 512], ps[:], isign[:, i, 0:1], None,
                           ALU.mult, reads=[bps, b_isign], writes=[b_score])
                    else:
                        sc.op('dve', lambda e, ps=ps, c4=c4, h=h, i=i: e.scalar_tensor_tensor(
                            out=score[:, c4 * 512:(c4 + 1) * 512], in0=ps[:],
                            scalar=isign[:, i, h:h + 1], in1=score[:, c4 * 512:(c4 + 1) * 512],
                            op0=ALU.mult, op1=ALU.add), reads=[bps, b_isign, b_score],
                            writes=[b_score])
            sc.op('dve', lambda e, n=n: e.tensor_reduce(out=sm[:, 0:1], in_=score[:, 0:n], axis=AX.X,
                                                       op=ALU.max, apply_absolute_value=True),
                  reads=[b_score], writes=[b_sm])
            ts('dve', sm[:, 1:2], sm[:, 0:1], -1.0, -1e-3, ALU.mult, ALU.add, reads=[b_sm],
               writes=[b_sm])
            ts('dve', sm[:, 2:3], sm[:, 0:1], 2.0, 2e-3, ALU.mult, ALU.add, reads=[b_sm],
               writes=[b_sm])
            ts('dve', steps[:], pow2[:], sm[:, 2:3], None, ALU.mult, reads=[b_sm, b_cst],
               writes=[b_steps])
            ts('dve', sm[:, 6:7], qcol[:, i:i + 1], float(-(n - 512)), None, ALU.add,
               reads=[b_cst], writes=[b_sm])
            ts('dve', cneg[:], iota[:], sm[:, 6:7], -1e9, ALU.is_gt, ALU.mult, reads=[b_cst, b_sm],
               writes=[b_cneg])
            tt('dve', score[:, n - 512:n], score[:, n - 512:n], cneg[:], ALU.add,
               reads=[b_score, b_cneg], writes=[b_score])
            for k in range(NBIS):
                tt('dve', sm[:, 3:4], sm[:, 1:2], steps[:, k:k + 1], ALU.add, reads=[b_sm, b_steps],
                   writes=[b_sm])
                sc.op('dve', lambda e, n=n: e.tensor_scalar(
                    out=negm[:, 0:n], in0=score[:, 0:n], scalar1=sm[:, 3:4], scalar2=None,
                    op0=ALU.is_ge, op1=ALU.add, accum_out=sm[:, 4:5]),
                    reads=[b_score, b_sm], writes=[b_negm, b_sm])
                ts('dve', sm[:, 5:6], sm[:, 4:5], 255.5, steps[:, k:k + 1], ALU.is_ge, ALU.mult,
                   reads=[b_sm, b_steps], writes=[b_sm])
                tt('dve', sm[:, 1:2], sm[:, 1:2], sm[:, 5:6], ALU.add, reads=[b_sm], writes=[b_sm])
            ts('dve', negm[:, 0:n], score[:, 0:n], sm[:, 1:2], -30000.0, ALU.is_lt, ALU.mult,
               reads=[b_score, b_sm], writes=[b_negm])
        def stage2(i):
            nk = NKI[i]
            negm, b_negm = negm2[i % 2], b_negm2[i % 2]
            ob = 4 + 2 * (cnt['o'] % 2)
            cnt['o'] += 1
            for half in range(2):
                mm(psum[ob + half][:, 0:272], zer[:, 0:128], zer[:, 0:272], True, False,
                   reads=[b_zer], writes=[psb[ob + half]])
            for kt in range(nk):
                for half in range(2):
                    pS, bpS = sbank()
                    pO, bpO = psum[ob + half], psb[ob + half]
                    mm(pS[:], negm[:, kt * 128:(kt + 1) * 128],
                       ident4[:].rearrange('p a b -> p (a b)'), True, False,
                       reads=[b_negm, b_id4], writes=[bpS])
                    for hh in range(4):
                        h = 2 * hh + half
                        e_, hp = h % 2, h // 2
                        mm(pS[:, hh * 128:(hh + 1) * 128],
                           dkT2[e_ * 64:(e_ + 1) * 64, kt * 128:(kt + 1) * 128],
                           dqT_sb[e_ * 64:(e_ + 1) * 64, hp, i * 128:(i + 1) * 128], False, hh == 3,
                           reads=[b_dk, b_dqT], writes=[bpS])
                    pi = cnt['p'] % 3
                    cnt['p'] += 1
                    act(PTd[pi][:], pS[:], AF.Exp, reads=[bpS], writes=[b_PTd[pi]], scale=0.125)
                    for hh in range(4):
                        mm(pO[:, hh * 68:hh * 68 + 65], PTd[pi][:, hh * 128:(hh + 1) * 128],
                           dva[:, kt, :], False, kt == nk - 1, reads=[b_PTd[pi], b_dva],
                           writes=[bpO], skip_group_check=True)
            for half in range(2):
                pO, bpO = psum[ob + half], psb[ob + half]
                pv = pO[:, 0:272].rearrange('p (h c) -> p h c', c=68)
                sc.op('dve', lambda e, pv=pv, half=half: e.reciprocal(
                    out=rc4[:, half * 4:(half + 1) * 4], in_=pv[:, :, 64]), reads=[bpO],
                    writes=[b_rc4])
                tt('dve', o_sb['b'][:, i, :].rearrange(
                    'p (hh par d) -> p hh par d', par=2, d=64)[:, :, half, :], pv[:, :, 0:64],
                   rc4[:, half * 4:(half + 1) * 4].unsqueeze(2).broadcast_to([128, 4, 64]),
                   ALU.mult, reads=[bpO, b_rc4], writes=[b_o['b']], nowaw=True)
        for i in range(NT):
            stage1(i)
            if i >= 1:
                stage2(i - 1)
        stage2(NT - 1)
        sc.barrier()
        sc.run()
        pes.close()

    phase_c2()
    if stage == 'c2':
        od = dscr('ob_dbg', [128, NT, 512], BF16, out=True)
        sc.dma('sp', od[:, :, :], o_sb['b'][:], reads=[b_o['b']], writes=[Buf('obd')], sem='dbg')
        od2 = dscr('oa_dbg', [128, NT, 512], BF16, out=True)
        sc.dma('sp', od2[:, :, :], o_sb['a'][:], reads=[b_o['a']], writes=[Buf('oad')], sem='dbg')
        sc.barrier()
        sc.run()
        es.close()
        return nc, dbg

    def phase_d():
        pes = ExitStack()

        def sbf(name, shape, dt):
            return pes.enter_context(nc.sbuf_tensor(un(name), list(shape), dt))
        norm_transpose = make_norm_transpose(sbf)
        gTx, b_gTx = load_gain_T(sbf, 'gT_mix2', v_d['mix_norm'], 8)
        wing = sbf('wing', [128, 8, 3072], BF16)
        wbr = sbf('wbr', [128, 12, D], BF16)
        wout = sbf('wout', [128, 8, D], BF16)
        b_wing, b_wbr, b_wout = Buf('wing'), Buf('wbr'), Buf('wout')
        for n3 in range(3):
            sc.dma('sp', wing[:, :, n3 * 1024:(n3 + 1) * 1024],
                   wb_d['w_in'][:, 3280 + n3 * 1024:3280 + (n3 + 1) * 1024].rearrange(
                       '(kc p) f -> p kc f', p=128), reads=[wb_buf['w_in']], writes=[b_wing],
                   sem='wing', nowaw=True)
        sc.dma('sp', wbr[:], wb_d['w_branch'].rearrange('(c p) d -> p c d', p=128),
               reads=[wb_buf['w_branch']], writes=[b_wbr], sem='wbr')
        sc.dma('sp', wout[:], wb_d['w_out'].rearrange('(c p) d -> p c d', p=128),
               reads=[wb_buf['w_out']], writes=[b_wout], sem='wout')
        uTb = sbf('uTb', [128, 8, 512], BF16)
        b_uTb = Buf('uTb')
        oT = {n_: sbf('oT_' + n_, [128, 4, 512], BF16) for n_ in 'abc'}
        b_oT = {n_: Buf('oT_' + n_) for n_ in 'abc'}
        mergedT = sbf('mergedT', [128, 8, 512], BF16)
        b_mg = Buf('mergedT')
        h1t = [sbf('h1t%d' % i, [128, D], F32) for i in range(4)]
        b_h1t = [Buf('h1t%d' % i) for i in range(4)]
        h2t = [sbf('h2t%d' % i, [128, D], F32) for i in range(2)]
        b_h2t = [Buf('h2t%d' % i) for i in range(2)]
        gs = [sbf('gs%d' % i, [128, 512], BF16) for i in range(2)]
        b_gs = [Buf('gs%d' % i) for i in range(2)]
        acc = sbf('acc', [128, 512], F32)
        tmpm = sbf('tmpm', [128, 512], F32)
        b_acc, b_tmpm = Buf('acc'), Buf('tmpm')
        cnt = {'g': 0, 'h2': 0}
        for blk in range(4):
            for tl in range(4):
                t = blk * 4 + tl
                sc.dma('sp', h1t[tl][:], h1_d[t * 128:(t + 1) * 128, :], reads=[b_h1d],
                       writes=[b_h1t[tl]], sem='h1t%d' % tl)
                norm_transpose(h1t[tl][:], b_h1t[tl], gTx, b_gTx, uTb, b_uTb, tl * 128)
                for n_ in 'abc':
                    ps, bps = next_ps()
                    psv = ps[:].bitcast(BF16)
                    for wc in range(4):
                        tr(psv[:, wc * 128:(wc + 1) * 128], o_sb[n_][:, t, wc * 128:(wc + 1) * 128],
                           ident_b[:], reads=[b_o[n_], b_ident_b], writes=[bps])
                    act(oT[n_][:, :, tl * 128:(tl + 1) * 128],
                        psv[:, 0:512].rearrange('p (w k) -> p w k', k=128), AF.Copy, reads=[bps],
                        writes=[b_oT[n_]], nowaw=True)
            for dc in range(8):
                for n3, n_ in enumerate('abc'):
                    pg, bpg = next_ps()
                    for kc in range(8):
                        mm(pg[:], wing[:, kc, n3 * 1024 + dc * 128:n3 * 1024 + (dc + 1) * 128],
                           uTb[:, kc, :], kc == 0, kc == 7, reads=[b_wing, b_uTb], writes=[bpg])
                    gi = cnt['g'] % 2
                    cnt['g'] += 1
                    act(gs[gi][:], pg[:], AF.Sigmoid, reads=[bpg], writes=[b_gs[gi]])
                    pp, bpp = next_ps()
                    for wc in range(4):
                        mm(pp[:], wbr[:, n3 * 4 + wc, dc * 128:(dc + 1) * 128], oT[n_][:, wc, :],
                           wc == 0, wc == 3, reads=[b_wbr, b_oT[n_]], writes=[bpp])
                    if n3 == 0:
                        tt('dve', acc[:], pp[:], gs[gi][:], ALU.mult, reads=[bpp, b_gs[gi]],
                           writes=[b_acc])
                    else:
                        tt('dve', tmpm[:], pp[:], gs[gi][:], ALU.mult, reads=[bpp, b_gs[gi]],
                           writes=[b_tmpm])
                        if n3 == 1:
                            tt('dve', acc[:], acc[:], tmpm[:], ALU.add, reads=[b_acc, b_tmpm],
                               writes=[b_acc])
                        else:
                            tt('dve', mergedT[:, dc, :], acc[:], tmpm[:], ALU.add,
                               reads=[b_acc, b_tmpm], writes=[b_mg], nowaw=True)
            for tl in range(4):
                t = blk * 4 + tl
                hi = cnt['h2'] % 2
                cnt['h2'] += 1
                for dh in range(2):
                    po, bpo = next_ps()
                    for dc in range(8):
                        mm(po[:], mergedT[:, dc, tl * 128:(tl + 1) * 128],
                           wout[:, dc, dh * 512:(dh + 1) * 512], dc == 0, dc == 7,
                           reads=[b_mg, b_wout], writes=[bpo])
                    tt('dve', h2t[hi][:, dh * 512:(dh + 1) * 512], po[:],
                       h1t[tl][:, dh * 512:(dh + 1) * 512], ALU.add, reads=[bpo, b_h1t[tl]],
                       writes=[b_h2t[hi]])
                sc.dma('sp', h2_d[t * 128:(t + 1) * 128, :], h2t[hi][:], reads=[b_h2t[hi]],
                       writes=[b_h2d], sem='h2o%d' % hi, nowaw=True)
        sc.barrier()
        sc.run()
        pes.close()

    phase_d()
    ffn_phase(h2_d, b_h2d, y_d, b_yd, 'ffn2_norm', 'ffn2_w_gate', 'ffn2_w_up', 'ffn2_w_down')
    es.close()
    print('ninst', sc.ninst, 'max sem', max(sc.cnt.values()), max(sc.dsem.values()), 'nsem', len(sc.semobj))
    return nc, dbg


_NC_CACHE = {}


def _core_rows(cid):
    b, j = divmod(cid, 4)
    tiles = zig_tiles(j)
    rows = np.concatenate([np.arange(g * 128, (g + 1) * 128) for g in tiles])
    return b, rows


def kernel(**inputs):
    stage = inputs.pop('_stage', 'full')
    if stage not in _NC_CACHE:
        _NC_CACHE[stage] = build(stage)
    nc, dbg = _NC_CACHE[stage]
    f = lambda k: np.asarray(inputs[k], dtype=np.float32)
    x = f('x')
    mem = f('mem')
    pos = np.asarray(inputs['positions']).astype(np.int32)
    ident = np.eye(128, dtype=np.float32)
    invf = (10000.0 ** (-np.arange(0, 64, 2, dtype=np.float32) / 64)).astype(np.float32)
    tri = np.triu(np.ones((128, 128), np.float32))
    kposc = (np.arange(64, dtype=np.float32)[None, :] * 128 + np.arange(128, dtype=np.float32)[:, None])
    pow2 = (0.5 ** np.arange(1, 33)).astype(np.float32)
    shared = {'ident': ident, 'invf': invf, 'tri': tri, 'kposc': np.ascontiguousarray(kposc),
              'pow2': pow2, 'iota512': np.arange(512, dtype=np.float32)}
    for k, shp in WNAMES.items():
        shared[k] = np.ascontiguousarray(f(k)[0].reshape(shp))
    for k in VNAMES:
        shared[k] = np.ascontiguousarray(f(k)[0])
    in_maps = []
    for cid in range(NCORES):
        b, rows = _core_rows(cid)
        m = dict(shared)
        m['x'] = np.ascontiguousarray(x[b][rows])
        m['pos'] = np.ascontiguousarray(pos[b][rows])
        m['qposf'] = rows.astype(np.float32)
        m['mem'] = np.ascontiguousarray(mem[b])
        in_maps.append(m)
    res = run_bass_kernel_spmd(nc, in_maps, core_ids=list(range(NCORES)))
    if stage != 'full':
        return res.results
    out = np.zeros((2, S, D), np.float32)
    for cid in range(NCORES):
        b, rows = _core_rows(cid)
        out[b][rows] = res.results[cid]['y']
    return out
```

```python
import numpy as np
from contextlib import ExitStack
import concourse.bass as bass
import concourse.mybir as mybir
from concourse.bass_utils import run_bass_kernel_spmd

F32 = mybir.dt.float32
BF16 = mybir.dt.bfloat16
I32 = mybir.dt.int32
AF = mybir.ActivationFunctionType
ALU = mybir.AluOpType
AX = mybir.AxisListType

NCORES = 8
D = 1024
S = 8192
TOK = 2048
NT = 16
DFF = 2816
NFC = 22
DIN = 6352
EPS = 1e-6
ENG = ['pe', 'act', 'dve', 'pool', 'sp']
SAME_ENG_SYNC = True
MAXQ = {'sp': 6, 'pool': 4, 'act': 6}


class Buf:
    __slots__ = ('name', 'w', 'r')

    def __init__(self, name):
        self.name = name
        self.w = {}
        self.r = {}


class Sched:
    def __init__(self, nc, es):
        self.nc = nc
        self.es = es
        self.semobj = {}
        self.cnt = {}
        self.prog = {e: [] for e in ENG}
        self.waited = {e: {} for e in ENG}
        self.dsem = {}
        self.epoch = 0
        self.ekey = {}
        for e in ENG:
            k = e + '@0'
            self.ekey[e] = k
            self.semobj[k] = es.enter_context(nc.semaphore('sem_' + e + '_0'))
            self.cnt[k] = 0
        self.ninst = {e: 0 for e in ENG}
        self.outq = {}

    def _deps(self, reads, writes, nowaw=False):
        deps = {}

        def add(k, v):
            if deps.get(k, 0) < v:
                deps[k] = v
        for b in reads:
            for k, v in b.w.items():
                add(k, v)
        for b in writes:
            if not nowaw:
                for k, v in b.w.items():
                    add(k, v)
            for k, v in b.r.items():
                add(k, v)
        return deps

    def _mark(self, key, v, reads, writes, nowaw):
        for b in reads:
            if b.r.get(key, 0) < v:
                b.r[key] = v
        for b in writes:
            if nowaw:
                if b.w.get(key, 0) < v:
                    b.w[key] = v
            else:
                b.w = {key: v}
                b.r = {}

    def _emit_waits(self, eng, deps):
        for k, v in deps.items():
            if k.split('@')[0] == eng and (eng == 'pe' or not SAME_ENG_SYNC):
                continue
            if k in self.dsem:
                v = self.dsem[k]
            if self.waited[eng].get(k, 0) >= v:
                continue
            self.waited[eng][k] = v
            sem = self.semobj[k]
            self.prog[eng].append(lambda e, sem=sem, v=v: e.wait_ge(sem, v))
            self.ninst[eng] += 1

    def op(self, eng, fn, reads=(), writes=(), nowaw=False):
        deps = self._deps(reads, writes, nowaw)
        self._emit_waits(eng, deps)
        key = self.ekey[eng]
        self.cnt[key] += 1
        n = self.cnt[key]
        sem = self.semobj[key]
        self.prog[eng].append(lambda e, fn=fn, sem=sem: fn(e).then_inc(sem, 1))
        self.ninst[eng] += 1
        self._mark(key, n, reads, writes, nowaw)

    def dma(self, q, out, in_, reads=(), writes=(), sem='dma', nowaw=False, **kw):
        deps = self._deps(reads, writes, nowaw)
        self._emit_waits(q, deps)
        if sem not in self.dsem:
            self.semobj[sem] = self.es.enter_context(self.nc.semaphore('dsem_' + sem))
            self.dsem[sem] = 0
        fifo = self.outq.setdefault(q, [])
        while len(fifo) >= MAXQ[q]:
            osem = fifo[0][0]
            ov = self.dsem[osem]
            fifo[:] = [x for x in fifo if x[0] != osem]
            if self.waited[q].get(osem, 0) < ov:
                self.waited[q][osem] = ov
                so = self.semobj[osem]
                self.prog[q].append(lambda e, so=so, ov=ov: e.wait_ge(so, ov))
                self.ninst[q] += 1
        self.dsem[sem] += 16
        v = self.dsem[sem]
        fifo.append((sem, v))
        s = self.semobj[sem]
        self.prog[q].append(lambda e, out=out, in_=in_, kw=kw, s=s:
                            e.dma_start(out=out, in_=in_, **kw).then_inc(s, 16))
        self.ninst[q] += 1
        self._mark(sem, v, reads, writes, nowaw)

    def custom(self, q, fn, reads=(), writes=(), sem='cc', inc=1):
        deps = self._deps(reads, writes)
        self._emit_waits(q, deps)
        if sem not in self.dsem:
            self.semobj[sem] = self.es.enter_context(self.nc.semaphore('dsem_' + sem))
            self.dsem[sem] = 0
        self.dsem[sem] += inc
        v = self.dsem[sem]
        s = self.semobj[sem]
        self.prog[q].append(lambda e, fn=fn, s=s, inc=inc: fn(e).then_inc(s, inc))
        self._mark(sem, v, reads, writes, False)

    def barrier(self):
        for e in ENG:
            deps = {self.ekey[k]: self.cnt[self.ekey[k]] for k in ENG
                    if self.cnt[self.ekey[k]] > 0 and k != e}
            for k, v in self.dsem.items():
                if v > 0:
                    deps[k] = v
            self._emit_waits(e, deps)

    def new_epoch(self):
        self.epoch += 1
        for e in ENG:
            k = '%s@%d' % (e, self.epoch)
            self.ekey[e] = k
            self.semobj[k] = self.es.enter_context(self.nc.semaphore('sem_%s_%d' % (e, self.epoch)))
            self.cnt[k] = 0

    def run(self):
        nc = self.nc
        with nc.Block() as block:
            @block.tensor
            def _(e):
                for f in self.prog['pe']:
                    f(e)

            @block.scalar
            def _(e):
                for f in self.prog['act']:
                    f(e)

            @block.vector
            def _(e):
                for f in self.prog['dve']:
                    f(e)

            @block.gpsimd
            def _(e):
                for f in self.prog['pool']:
                    f(e)

            @block.sync
            def _(e):
                for f in self.prog['sp']:
                    f(e)
        self.prog = {e: [] for e in ENG}
        self.new_epoch()


class Ctx:
    pass


def zig_tiles(j):
    out = []
    for m in range(8):
        out.append(8 * m + j)
        out.append(8 * m + 7 - j)
    return out


SPLITS = dict(fq=(0, 512), fk=(512, 1024), fv=(1024, 1536), ff=(1536, 1544), dq=(1544, 2056),
              dk=(2056, 2120), dv=(2120, 2184), iq=(2184, 2696), ik=(2696, 2760),
              iw=(2760, 2768), mq=(2768, 3280), g=(3280, 6352))
KROWS = 1225
WNAMES = {
    'ffn1_w_gate': (D, DFF), 'ffn1_w_up': (D, DFF), 'ffn1_w_down': (DFF, D),
    'w_in': (D, DIN), 'w_mem_kv': (D, D), 'w_branch': (3 * 512, D), 'w_out': (D, D),
    'ffn2_w_gate': (D, DFF), 'ffn2_w_up': (D, DFF), 'ffn2_w_down': (DFF, D),
}
VNAMES = {'ffn1_norm': D, 'mix_norm': D, 'mem_norm': D, 'ffn2_norm': D, 'b_forget': 8,
          'fox_q_norm': 64, 'fox_k_norm': 64, 'dsa_q_norm': 64, 'dsa_k_norm': 64,
          'mem_q_norm': 128, 'mem_k_norm': 128}
STAGES = ['ffn1', 'a2', 'full']


def build(stage='full'):
    nc = bass.Bass("TRN2", target_bir_lowering=False)
    es = ExitStack()
    sc = Sched(nc, es)
    c = Ctx()
    dbg = {}
    uq = [0]

    def un(name):
        uq[0] += 1
        return '%s_%d' % (name, uq[0])

    def din(name, shape, dt=F32):
        return nc.dram_tensor(name, list(shape), dt, kind="ExternalInput")

    def dscr(name, shape, dt, out=False):
        if out:
            dbg[name] = True
        return nc.dram_tensor(name, list(shape), dt, kind="ExternalOutput" if out else "Internal")

    x_d = din('x', [TOK, D])
    pos_d = din('pos', [TOK], I32)
    qposf_d = din('qposf', [TOK])
    ident_d = din('ident', [128, 128])
    invf_d = din('invf', [32])
    mem_d = din('mem', [256, D])
    tri_d = din('tri', [128, 128])
    kposc_d = din('kposc', [128, 64])
    pow2_d = din('pow2', [32])
    iota512_d = din('iota512', [512])
    w_d = {k: din(k, s) for k, s in WNAMES.items()}
    v_d = {k: din(k, [n]) for k, n in VNAMES.items()}
    h1_d = dscr('h1s', [TOK, D], F32, out=(stage == 'ffn1'))
    b_h1d = Buf('h1s')
    A2O = (stage == 'a2')
    fqT_d = dscr('fqT_s', [128, 4, TOK], BF16, out=A2O)
    dqT_d = dscr('dqT_s', [128, 4, TOK], BF16, out=A2O)
    iqT_d = dscr('iqT_s', [128, 4, TOK], BF16, out=A2O)
    mqT_d = dscr('mqT_s', [128, 4, TOK], BF16, out=A2O)
    fkT_in = [dscr('fkT_in%d' % i, [256, TOK], BF16, out=A2O) for i in range(2)]
    fvA_in = [dscr('fvA_in%d' % i, [TOK, 130], BF16, out=A2O) for i in range(4)]
    dikT_in = dscr('dikT_in', [128, TOK], BF16, out=A2O)
    dvA_in = dscr('dvA_in', [TOK, 65], BF16, out=A2O)
    lin_d = dscr('lin', [TOK, 8], F32, out=A2O)
    b_fqTd, b_dqTd, b_iqTd, b_mqTd, b_kind, b_lind = [Buf(n) for n in
                                                    ['fqTd', 'dqTd', 'iqTd', 'mqTd', 'kind', 'lind']]
    o = 0
    kin_dikT = dikT_in
    kin_dvA = dvA_in
    GSH = {'lall': (TOK, 8, F32), 'fkT_g0': (256, TOK, BF16), 'fkT_g1': (256, TOK, BF16),
           'fvA_g0': (TOK, 130, BF16), 'fvA_g1': (TOK, 130, BF16), 'fvA_g2': (TOK, 130, BF16),
           'fvA_g3': (TOK, 130, BF16), 'dikT_g': (128, TOK, BF16), 'dvA_g': (TOK, 65, BF16)}
    g_d = {k: dscr(k, [4 * r, c_], dt) for k, (r, c_, dt) in GSH.items()}
    y_d = nc.dram_tensor('y', [TOK, D], F32, kind="ExternalOutput")
    b_yd = Buf('y')
    h2_d = dscr('h2s', [TOK, D], F32)
    b_h2d = Buf('h2s')
    wb_d = {k: dscr(k + '_bf', s_, BF16) for k, s_ in WNAMES.items()}
    wb_buf = {k: Buf(k + '_bf') for k in WNAMES}

    def sbg(name, shape, dt):
        return es.enter_context(nc.sbuf_tensor(un(name), list(shape), dt))

    ident_f = sbg('ident_f', [128, 128], F32)
    ident_b = sbg('ident_b', [128, 128], BF16)
    b_ident_f = Buf('ident_f')
    b_ident_b = Buf('ident_b')
    sc.dma('sp', ident_f[:], ident_d[:, :], writes=[b_ident_f], sem='ident')
    sc.op('dve', lambda e: e.tensor_copy(out=ident_b[:], in_=ident_f[:]),
          reads=[b_ident_f], writes=[b_ident_b])
    isign = sbg('isign', [128, NT, 8], F32)
    b_isign = Buf('isign')
    es_a = ExitStack()
    uT = es_a.enter_context(nc.sbuf_tensor('uT', [128, 8, TOK], BF16))
    b_uT = Buf('uT')

    def cast_weight(name):
        src = w_d[name].reshape([-1, 1024])
        dst = wb_d[name].reshape([-1, 1024])
        rows = src.shape[0]
        r0 = 0
        while r0 < rows:
            r1 = min(rows, r0 + 2048)
            sc.dma('pool', dst[r0:r1, :], src[r0:r1, :], writes=[wb_buf[name]], sem='wcast')
            r0 = r1

    for k in WNAMES:
        cast_weight(k)

    psum = [es.enter_context(nc.psum_tensor('ps%d' % i, [128, 512], F32)) for i in range(8)]
    psb = [Buf('ps%d' % i) for i in range(8)]
    c.ps_i = 0

    def next_ps():
        i = c.ps_i
        c.ps_i = (i + 1) % 8
        return psum[i], psb[i]

    def ts(eng, out, in0, s1, s2, op0, op1=None, reads=(), writes=(), accum=None, nowaw=False):
        if op1 is None:
            sc.op(eng, lambda e: e.tensor_scalar(out=out, in0=in0, scalar1=s1, scalar2=None,
                                                 op0=op0, accum_out=accum), reads, writes, nowaw)
        else:
            sc.op(eng, lambda e: e.tensor_scalar(out=out, in0=in0, scalar1=s1, scalar2=s2,
                                                 op0=op0, op1=op1, accum_out=accum), reads, writes,
                  nowaw)

    def tt(eng, out, in0, in1, op, reads=(), writes=(), nowaw=False):
        sc.op(eng, lambda e: e.tensor_tensor(out=out, in0=in0, in1=in1, op=op), reads, writes,
              nowaw)

    def act(out, in_, func, reads=(), writes=(), nowaw=False, **kw):
        sc.op('act', lambda e: e.activation(out=out, in_=in_, func=func, **kw), reads, writes,
              nowaw)

    def mm(out, lhsT, rhs, start, stop, reads=(), writes=(), **kw):
        sc.op('pe', lambda e: e.matmul(out=out, lhsT=lhsT, rhs=rhs, start=start, stop=stop, **kw),
              reads, writes)

    def tr(out, in_, ident, reads=(), writes=()):
        sc.op('pe', lambda e: e.transpose(out=out, in_=in_, identity=ident), reads, writes)

    def load_gain_T(sbf, name, dram, ncol):
        t = sbf(name, [128, ncol], F32)
        b = Buf(name)
        sc.dma('sp', t[:], dram.rearrange('(kc p) -> p kc', p=128), writes=[b], sem='const',
               allow_slow_non_contiguous=True)
        return t, b

    def load_bc(sbf, name, dram_ap, n):
        t = sbf(name, [128, n], F32)
        b = Buf(name)
        sc.dma('sp', t[:], dram_ap.partition_broadcast(128), writes=[b], sem='const')
        return t, b

    def rstd_from_ss(st, bst, n, dim):
        ts('dve', st[:, n:2 * n], st[:, 0:n], 1.0 / dim, EPS, ALU.mult, ALU.add,
           reads=[bst], writes=[bst])
        act(st[:, 2 * n:3 * n], st[:, n:2 * n], AF.Sqrt, reads=[bst], writes=[bst])
        sc.op('dve', lambda e: e.reciprocal(out=st[:, 3 * n:4 * n], in_=st[:, 2 * n:3 * n]),
              reads=[bst], writes=[bst])
        return st[:, 3 * n:4 * n]

    def make_norm_transpose(sbf):
        sq_junk = sbf('sq_junk', [128, D], BF16)
        b_sq_junk = Buf('sq_junk')
        xn_bf = [sbf('xn_bf%d' % i, [128, D], BF16) for i in range(2)]
        b_xn_bf = [Buf('xn_bf%d' % i) for i in range(2)]
        stat = [sbf('nstat%d' % i, [128, 4], F32) for i in range(2)]
        b_stat = [Buf('nstat%d' % i) for i in range(2)]
        state = {'i': 0}

        def norm_transpose(src_tile, b_src, gT, b_gT, dstT, b_dst, col0):
            i = state['i']
            state['i'] ^= 1
            st, bst = stat[i], b_stat[i]
            xb, bxb = xn_bf[i], b_xn_bf[i]
            act(sq_junk[:], src_tile, AF.Square, reads=[b_src], writes=[b_sq_junk, bst],
                accum_out=st[:, 0:1])
            rs = rstd_from_ss(st, bst, 1, D)
            act(xb[:], src_tile, AF.Copy, reads=[b_src, bst], writes=[bxb], scale=rs)
            ps, bps = next_ps()
            psv = ps[:].bitcast(BF16)
            for kc in range(8):
                tr(psv[:, kc * 128:(kc + 1) * 128], xb[:, kc * 128:(kc + 1) * 128], ident_b[:],
                   reads=[bxb, b_ident_b], writes=[bps])
            for kc in range(8):
                ts('dve', dstT[:, kc, col0:col0 + 128], psv[:, kc * 128:(kc + 1) * 128],
                   gT[:, kc:kc + 1], None, ALU.mult, reads=[bps, b_gT], writes=[b_dst],
                   nowaw=True)
        return norm_transpose

    HALF = 1024

    def ffn_phase(src_d, b_srcd, dst_d, b_dstd, gname, wgn, wun, wdn, post_gname=None,
                  postT=None, b_postT=None):
        pes = ExitStack()

        def sbf(name, shape, dt):
            return pes.enter_context(nc.sbuf_tensor(un(name), list(shape), dt))
        norm_transpose = make_norm_transpose(sbf)
        gT, b_gT = load_gain_T(sbf, 'gT_' + gname, v_d[gname], 8)
        if post_gname:
            pgT, b_pgT = load_gain_T(sbf, 'gT_' + post_gname, v_d[post_gname], 8)
        xt = [sbf('xt%d' % i, [128, D], F32) for i in range(3)]
        b_xt = [Buf('xt%d' % i) for i in range(3)]
        xnT = sbf('xnT', [128, 8, HALF], BF16)
        b_xnT = Buf('xnT')
        actT = sbf('actT', [128, NFC, HALF], BF16)
        b_actT = [Buf('actT%d' % i) for i in range(NFC)]
        wd_sb = sbf('wd_sb', [128, NFC, D], BF16)
        b_wd = Buf('wd_sb')
        WC = 256
        wg_sb = [sbf('wg_sb%d' % i, [128, 8, WC], BF16) for i in range(2)]
        wu_sb = [sbf('wu_sb%d' % i, [128, 8, WC], BF16) for i in range(2)]
        b_wg = [Buf('wg%d' % i) for i in range(2)]
        b_wu = [Buf('wu%d' % i) for i in range(2)]
        sg_sb = [sbf('sg_sb%d' % i, [128, 512], BF16) for i in range(2)]
        b_sg = [Buf('sg%d' % i) for i in range(2)]
        ho = [sbf('ho%d' % i, [128, D], F32) for i in range(2)]
        b_ho = [Buf('ho%d' % i) for i in range(2)]
        st = {'xt': 0, 'w': 0, 'sg': 0, 'ho': 0}
        wg_d, wu_d, wdd = wb_d[wgn], wb_d[wun], wb_d[wdn]
        sc.dma('sp', wd_sb[:], wdd.rearrange('(fc p) d -> p fc d', p=128),
               reads=[wb_buf[wdn]], writes=[b_wd], sem='wd')
        for half in range(TOK // HALF):
            for t in range(HALF // 128):
                i = st['xt']
                st['xt'] = (i + 1) % 3
                r0 = half * HALF + t * 128
                sc.dma('sp', xt[i][:], src_d[r0:r0 + 128, :], reads=[b_srcd], writes=[b_xt[i]],
                       sem='xt%d' % i)
                norm_transpose(xt[i][:], b_xt[i], gT, b_gT, xnT, b_xnT, t * 128)
            for wc in range(DFF // WC):
                wi = st['w']
                st['w'] ^= 1
                sc.dma('sp', wg_sb[wi][:],
                       wg_d[:, wc * WC:(wc + 1) * WC].rearrange('(kc p) f -> p kc f', p=128),
                       reads=[wb_buf[wgn]], writes=[b_wg[wi]], sem='wg%d' % wi)
                sc.dma('sp', wu_sb[wi][:],
                       wu_d[:, wc * WC:(wc + 1) * WC].rearrange('(kc p) f -> p kc f', p=128),
                       reads=[wb_buf[wun]], writes=[b_wu[wi]], sem='wu%d' % wi)
                for fl in range(WC // 128):
                    fc = wc * (WC // 128) + fl
                    for tb in range(HALF // 512):
                        pg, bpg = next_ps()
                        pu, bpu = next_ps()
                        for kc in range(8):
                            mm(pg[:], wg_sb[wi][:, kc, fl * 128:(fl + 1) * 128],
                               xnT[:, kc, tb * 512:(tb + 1) * 512], kc == 0, kc == 7,
                               reads=[b_wg[wi], b_xnT], writes=[bpg])
                        for kc in range(8):
                            mm(pu[:], wu_sb[wi][:, kc, fl * 128:(fl + 1) * 128],
                               xnT[:, kc, tb * 512:(tb + 1) * 512], kc == 0, kc == 7,
                               reads=[b_wu[wi], b_xnT], writes=[bpu])
                        si = st['sg']
                        st['sg'] ^= 1
                        act(sg_sb[si][:], pg[:], AF.Silu, reads=[bpg], writes=[b_sg[si]])
                        tt('dve', actT[:, fc, tb * 512:(tb + 1) * 512], pu[:], sg_sb[si][:],
                           ALU.mult, reads=[bpu, b_sg[si]], writes=[b_actT[fc]])
            for t in range(HALF // 128):
                r0 = half * HALF + t * 128
                i = st['xt']
                st['xt'] = (i + 1) % 3
                sc.dma('sp', xt[i][:], src_d[r0:r0 + 128, :], reads=[b_srcd], writes=[b_xt[i]],
                       sem='xt%d' % i)
                hi = st['ho']
                st['ho'] ^= 1
                for dh in range(2):
                    pd, bpd = next_ps()
                    for fc in range(NFC):
                        mm(pd[:], actT[:, fc, t * 128:(t + 1) * 128],
                           wd_sb[:, fc, dh * 512:(dh + 1) * 512], fc == 0, fc == NFC - 1,
                           reads=[b_actT[fc], b_wd], writes=[bpd])
                    sc.op('dve', lambda e, pd=pd, hi=hi, i=i, dh=dh: e.scalar_tensor_tensor(
                        out=ho[hi][:, dh * 512:(dh + 1) * 512], in0=pd[:], scalar=0.5,
                        in1=xt[i][:, dh * 512:(dh + 1) * 512], op0=ALU.mult, op1=ALU.add),
                        reads=[bpd, b_xt[i]], writes=[b_ho[hi]])
                sc.dma('sp', dst_d[r0:r0 + 128, :], ho[hi][:], reads=[b_ho[hi]], writes=[b_dstd],
                       sem='ho%d' % hi, nowaw=True)
                if post_gname:
                    norm_transpose(ho[hi][:], b_ho[hi], pgT, b_pgT, postT, b_postT, r0)
        sc.barrier()
        sc.run()
        pes.close()

    b_x = Buf('x_d')
    ffn_phase(x_d, b_x, h1_d, b_h1d, 'ffn1_norm', 'ffn1_w_gate', 'ffn1_w_up', 'ffn1_w_down',
              post_gname='mix_norm', postT=uT, b_postT=b_uT)

    if stage == 'ffn1':
        es.close()
        print('ninst', sc.ninst, 'max sem', max(sc.cnt.values()), max(sc.dsem.values()), 'nsem', len(sc.semobj))
        return nc, dbg

    def phase_a2():
        pes = ExitStack()

        def sbf(name, shape, dt):
            return pes.enter_context(nc.sbuf_tensor(un(name), list(shape), dt))
        NG = 3280
        win_sb = sbf('win_sb', [128, 8, NG], BF16)
        b_win = Buf('win_sb')
        wv = wb_d['w_in']

        def ldw(dst0, c0, c1):
            sc.dma('sp', win_sb[:, :, dst0:dst0 + (c1 - c0)],
                   wv[:, c0:c1].rearrange('(kc p) f -> p kc f', p=128),
                   reads=[wb_buf['w_in']], writes=[b_win], sem='win')
        GOFF = {k: v[0] for k, v in SPLITS.items()}
        for q4 in range(4):
            ldw(q4 * 820, q4 * 820, (q4 + 1) * 820)

        gq2 = sbf('gq2', [128, 1], F32)
        gk2 = sbf('gk2', [128, 1], F32)
        gmq = sbf('gmq', [128, 1], F32)
        b_g2 = Buf('g2')
        for hh in range(2):
            sc.dma('sp', gq2[hh * 64:(hh + 1) * 64, :],
                   v_d['fox_q_norm'].rearrange('(p o) -> p o', o=1), writes=[b_g2], sem='const')
            sc.dma('sp', gk2[hh * 64:(hh + 1) * 64, :],
                   v_d['fox_k_norm'].rearrange('(p o) -> p o', o=1), writes=[b_g2], sem='const')
        sc.dma('sp', gmq[:], v_d['mem_q_norm'].rearrange('(p o) -> p o', o=1), writes=[b_g2],
               sem='const')
        gdq, b_gdq = load_bc(sbf, 'gdq', v_d['dsa_q_norm'].ap(), 64)
        gdk, b_gdk = load_bc(sbf, 'gdk', v_d['dsa_k_norm'].ap(), 64)
        bfor, b_bfor = load_bc(sbf, 'bfor', v_d['b_forget'].ap(), 8)
        invf, b_invf = load_bc(sbf, 'invf_sb', invf_d.ap(), 32)
        posi = sbf('posi', [128, NT], I32)
        posf = sbf('posf', [128, NT], F32)
        b_pos = Buf('pos')
        sc.dma('sp', posi[:], pos_d.rearrange('(t p) -> p t', p=128), writes=[b_pos], sem='const',
               allow_slow_non_contiguous=True)
        sc.op('dve', lambda e: e.tensor_copy(out=posf[:], in_=posi[:]), reads=[b_pos],
              writes=[b_pos])
        ang = sbf('ang', [128, NT, 64], F32)
        kk = sbf('kk', [128, NT, 64], F32)
        kki = sbf('kki', [128, NT, 64], I32)
        cs = sbf('cs', [128, NT, 64], F32)
        b_ang = Buf('ang')
        b_cs = Buf('cs')
        TWO_PI = float(2 * np.pi)
        for t in range(NT):
            ts('dve', ang[:, t, 32:64], invf[:], posf[:, t:t + 1], None, ALU.mult,
               reads=[b_invf, b_pos], writes=[b_ang])
        ts('dve', ang[:, :, 0:32], ang[:, :, 32:64], float(np.pi / 2), None, ALU.add,
           reads=[b_ang], writes=[b_ang])
        ts('dve', kk[:], ang[:], 1.0 / TWO_PI, 0.5, ALU.mult, ALU.add, reads=[b_ang], writes=[b_ang])
        sc.op('dve', lambda e: e.tensor_copy(out=kki[:], in_=kk[:]), reads=[b_ang], writes=[b_ang])
        sc.op('dve', lambda e: e.tensor_copy(out=kk[:], in_=kki[:]), reads=[b_ang], writes=[b_ang])
        sc.op('dve', lambda e: e.scalar_tensor_tensor(out=ang[:], in0=kk[:], scalar=-TWO_PI,
                                                      in1=ang[:], op0=ALU.mult, op1=ALU.add),
              reads=[b_ang], writes=[b_ang])
        ts('dve', kk[:], ang[:], float(-np.pi), TWO_PI, ALU.is_lt, ALU.mult, reads=[b_ang],
           writes=[b_ang])
        tt('dve', ang[:], ang[:], kk[:], ALU.add, reads=[b_ang], writes=[b_ang])
        ts('dve', kk[:], ang[:], float(np.pi), -TWO_PI, ALU.is_gt, ALU.mult, reads=[b_ang],
           writes=[b_ang])
        tt('dve', ang[:], ang[:], kk[:], ALU.add, reads=[b_ang], writes=[b_ang])
        ts('dve', ang[:], ang[:], float(np.pi), float(-np.pi), ALU.min, ALU.max, reads=[b_ang],
           writes=[b_ang])
        act(cs[:], ang[:], AF.Sin, reads=[b_ang], writes=[b_cs])
        abq = sbf('abq', [128, NT, 4, 32], F32)
        abk = sbf('abk', [128, NT, 4, 32], F32)
        b_abq = Buf('abq')
        b_abk = Buf('abk')
        for (ab, bab, g, bg) in [(abq, b_abq, gdq, b_gdq), (abk, b_abk, gdk, b_gdk)]:
            g1 = g[:, 0:32].unsqueeze(1).broadcast_to([128, NT, 32])
            g2 = g[:, 32:64].unsqueeze(1).broadcast_to([128, NT, 32])
            tt('dve', ab[:, :, 0, :], cs[:, :, 0:32], g1, ALU.mult, reads=[b_cs, bg], writes=[bab])
            tt('dve', ab[:, :, 1, :], cs[:, :, 32:64], g2, ALU.mult, reads=[b_cs, bg], writes=[bab])
            tt('dve', ab[:, :, 2, :], cs[:, :, 0:32], g2, ALU.mult, reads=[b_cs, bg], writes=[bab])
            tt('dve', ab[:, :, 3, :], cs[:, :, 32:64], g1, ALU.mult, reads=[b_cs, bg], writes=[bab])

        sqf = sbf('sqf', [128, 512], F32)
        b_sqf = Buf('sqf')
        st8 = sbf('st8', [128, 32], F32)
        b_st8 = Buf('st8')
        r1 = sbf('r1', [128, 8, 32], F32)
        r2 = sbf('r2', [128, 8, 32], F32)
        ro = sbf('ro', [128, 8, 64], F32)
        b_r1, b_r2, b_ro = Buf('r1'), Buf('r2'), Buf('ro')
        tok_bf = [sbf('tok_bf%d' % i, [128, 512], BF16) for i in range(2)]
        b_tok_bf = [Buf('tok_bf%d' % i) for i in range(2)]
        tT = [sbf('tT%d' % i, [128, 4, 128], BF16) for i in range(2)]
        b_tT = [Buf('tT%d' % i) for i in range(2)]
        pack6 = sbf('pack6', [128, 128], BF16)
        b_pack6 = Buf('pack6')
        p6T = sbf('p6T', [128, 128], BF16)
        b_p6T = Buf('p6T')
        dvb = sbf('dvb', [128, 65], BF16)
        b_dvb = Buf('dvb')
        fvst = sbf('fvst', [128, 8, 65], BF16)
        b_fvst = Buf('fvst')
        sc.op('dve', lambda e: e.memset(dvb[:], 1.0), writes=[b_dvb])
        sc.op('dve', lambda e: e.memset(fvst[:], 1.0), writes=[b_fvst])
        lf = sbf('lf', [128, 3, 8], F32)
        b_lf = Buf('lf')
        absw = sbf('absw', [128, 8], F32)
        b_absw = Buf('absw')
        stt = {'tb': 0, 'tT': 0}
        lfall = sbf('lfall', [128, NT, 8], F32)
        b_lfall = Buf('lfall')
        if stage == 'b':
            lfdbg = sbf('lfdbg', [128, NT, 3, 8], F32)
            b_lfdbg = Buf('lfdbg')

        def nxt(key, n=2):
            i = stt[key]
            stt[key] = (i + 1) % n
            return i

        def rope(src_ps, bsrc, nh, tabs, btabs, t, dst, bdst, plain):
            sv = src_ps.rearrange('p (h two d) -> p h two d', two=2, d=32)
            x1 = sv[:, :, 0, :]
            x2 = sv[:, :, 1, :]

            def tb(j):
                if plain:
                    a = cs[:, t, 0:32] if j in (0, 2) else cs[:, t, 32:64]
                else:
                    a = tabs[:, t, j, :]
                return a.unsqueeze(1).broadcast_to([128, nh, 32])
            A_, B_, C_, D_ = tb(0), tb(1), tb(2), tb(3)
            tt('dve', r1[:, 0:nh, :], x1, A_, ALU.mult, reads=[bsrc, btabs], writes=[b_r1])
            tt('dve', r2[:, 0:nh, :], x2, B_, ALU.mult, reads=[bsrc, btabs], writes=[b_r2])
            tt('dve', dst[:, 0:nh, 0:32], r1[:, 0:nh, :], r2[:, 0:nh, :], ALU.subtract,
               reads=[b_r1, b_r2], writes=[bdst])
            tt('dve', r1[:, 0:nh, :], x2, C_, ALU.mult, reads=[bsrc, btabs], writes=[b_r1])
            tt('dve', r2[:, 0:nh, :], x1, D_, ALU.mult, reads=[bsrc, btabs], writes=[b_r2])
            tt('dve', dst[:, 0:nh, 32:64], r1[:, 0:nh, :], r2[:, 0:nh, :], ALU.add,
               reads=[b_r1, b_r2], writes=[bdst])

        def head_ss(ps, bps, nh, hd):
            act(sqf[:, 0:nh * hd], ps[:, 0:nh * hd], AF.Square, reads=[bps], writes=[b_sqf])
            sc.op('dve', lambda e: e.tensor_reduce(
                out=st8[:, 0:nh], in_=sqf[:, 0:nh * hd].rearrange('p (h d) -> p h d', d=hd),
                axis=AX.X, op=ALU.add), reads=[b_sqf], writes=[b_st8])
            return rstd_from_ss(st8, b_st8, nh, hd)

        def transpose_out(src_bf, bsrc, gcol, bg, dst_dram, bdst, col0):
            ps, bps = next_ps()
            psv = ps[:].bitcast(BF16)
            for k4 in range(4):
                tr(psv[:, k4 * 128:(k4 + 1) * 128], src_bf[:, k4 * 128:(k4 + 1) * 128], ident_b[:],
                   reads=[bsrc, b_ident_b], writes=[bps])
            i = nxt('tT')
            if gcol is None:
                act(tT[i][:].rearrange('p a b -> p (a b)'), psv[:, 0:512], AF.Copy, reads=[bps],
                    writes=[b_tT[i]])
            else:
                ts('dve', tT[i][:].rearrange('p a b -> p (a b)'), psv[:, 0:512], gcol, None,
                   ALU.mult, reads=[bps, bg], writes=[b_tT[i]])
            if dst_dram is None:
                for pr in range(2):
                    sc.dma('sp', fkT_in[pr].rearrange('(hp p) t -> p hp t', p=128)[
                        :, :, col0:col0 + 128], tT[i][:, 2 * pr:2 * pr + 2, :], reads=[b_tT[i]],
                        writes=[bdst], sem='tTo%d' % i, nowaw=True)
            else:
                sc.dma('sp', dst_dram[:, :, col0:col0 + 128], tT[i][:], reads=[b_tT[i]],
                       writes=[bdst], sem='tTo%d' % i, nowaw=True)

        def proj(t, goff, n):
            ps, bps = next_ps()
            for kc in range(8):
                mm(ps[:, 0:n], uT[:, kc, t * 128:(t + 1) * 128], win_sb[:, kc, goff:goff + n],
                   kc == 0, kc == 7, reads=[b_uT, b_win], writes=[bps])
            return ps, bps

        for t in range(NT):
            c0 = t * 128
            pff, bpff = proj(t, GOFF['ff'], 8)
            pdd, bpdd = proj(t, GOFF['dk'], 128)
            pii, bpii = proj(t, GOFF['ik'], 72)
            act(lf[:, 1, :], pff[:, 0:8], AF.Copy, reads=[bpff], writes=[b_lf])
            tt('dve', lf[:, 0, :], lf[:, 1, :], bfor[:], ALU.add, reads=[b_lf, b_bfor],
               writes=[b_lf])
            act(lf[:, 1, :], lf[:, 0, :], AF.Exp, reads=[b_lf], writes=[b_lf], scale=-1.0)
            ts('dve', lf[:, 1, :], lf[:, 1, :], 1.0, None, ALU.add, reads=[b_lf], writes=[b_lf])
            act(lf[:, 2, :], lf[:, 1, :], AF.Ln, reads=[b_lf], writes=[b_lf])
            ts('dve', lfall[:, t, :], lf[:, 2, :], -1.0, None, ALU.mult, reads=[b_lf],
               writes=[b_lfall], nowaw=True)
            if stage == 'b':
                sc.op('dve', lambda e, t=t: e.tensor_copy(out=lfdbg[:, t, :, :], in_=lf[:]),
                      reads=[b_lf], writes=[b_lfdbg], nowaw=True)
            act(absw[:], pii[:, 64:72], AF.Abs, reads=[bpii], writes=[b_absw])
            act(isign[:, t, :], pii[:, 64:72], AF.Sign, reads=[bpii], writes=[b_isign])
            act(sqf[:, 0:64], pdd[:, 0:64], AF.Square, reads=[bpdd], writes=[b_sqf, b_st8],
                accum_out=st8[:, 0:1])
            rsk = rstd_from_ss(st8, b_st8, 1, 64)
            rope(pdd[:, 0:64], bpdd, 1, abk, b_abk, t, ro, b_ro, False)
            ts('dve', pack6[:, 0:64], ro[:, 0, :], rsk, None, ALU.mult, reads=[b_ro, b_st8],
               writes=[b_pack6])
            rope(pii[:, 0:64], bpii, 1, cs, b_cs, t, ro, b_ro, True)
            sc.op('dve', lambda e: e.tensor_copy(out=pack6[:, 64:128], in_=ro[:, 0, :]),
                  reads=[b_ro], writes=[b_pack6])
            act(dvb[:, 0:64], pdd[:, 64:128], AF.Copy, reads=[bpdd], writes=[b_dvb])
            sc.dma('sp', kin_dvA[c0:c0 + 128, :], dvb[:], reads=[b_dvb], writes=[b_kind], sem='dvo',
                   nowaw=True)
            ps, bps = next_ps()
            psv = ps[:].bitcast(BF16)
            tr(psv[:, 0:128], pack6[:], ident_b[:], reads=[b_pack6, b_ident_b], writes=[bps])
            act(p6T[:], psv[:, 0:128], AF.Copy, reads=[bps], writes=[b_p6T])
            sc.dma('sp', kin_dikT[:, c0:c0 + 128], p6T[:], reads=[b_p6T], writes=[b_kind],
                   sem='p6o', nowaw=True)
            for nm, gcol, dst, bdst in [('fq', gq2, fqT_d, b_fqTd), ('fk', gk2, None, b_kind)]:
                ps, bps = proj(t, GOFF[nm], 512)
                rs = head_ss(ps, bps, 8, 64)
                i = nxt('tb')
                tt('dve', tok_bf[i][:].rearrange('p (h d) -> p h d', d=64),
                   ps[:].rearrange('p (h d) -> p h d', d=64),
                   rs.unsqueeze(2).broadcast_to([128, 8, 64]), ALU.mult,
                   reads=[bps, b_st8], writes=[b_tok_bf[i]])
                transpose_out(tok_bf[i], b_tok_bf[i], gcol[:, 0:1], b_g2, dst, bdst, c0)
            ps, bps = proj(t, GOFF['fv'], 512)
            act(fvst[:, :, 0:64], ps[:].rearrange('p (h d) -> p h d', d=64), AF.Copy, reads=[bps],
                writes=[b_fvst])
            for hp_ in range(4):
                sc.dma('sp', fvA_in[hp_][c0:c0 + 128, :],
                       fvst[:, 2 * hp_:2 * hp_ + 2, :].rearrange('p e c -> p (e c)'),
                       reads=[b_fvst], writes=[b_kind], sem='fvo', nowaw=True)
            ps, bps = proj(t, GOFF['dq'], 512)
            rs = head_ss(ps, bps, 8, 64)
            rope(ps[:], bps, 8, abq, b_abq, t, ro, b_ro, False)
            i = nxt('tb')
            tt('dve', tok_bf[i][:].rearrange('p (h d) -> p h d', d=64), ro[:],
               rs.unsqueeze(2).broadcast_to([128, 8, 64]), ALU.mult,
               reads=[b_ro, b_st8], writes=[b_tok_bf[i]])
            transpose_out(tok_bf[i], b_tok_bf[i], None, None, dqT_d, b_dqTd, c0)
            ps, bps = proj(t, GOFF['iq'], 512)
            rope(ps[:], bps, 8, cs, b_cs, t, ro, b_ro, True)
            i = nxt('tb')
            tt('dve', tok_bf[i][:].rearrange('p (h d) -> p h d', d=64), ro[:],
               absw[:].unsqueeze(2).broadcast_to([128, 8, 64]), ALU.mult,
               reads=[b_ro, b_absw], writes=[b_tok_bf[i]])
            transpose_out(tok_bf[i], b_tok_bf[i], None, None, iqT_d, b_iqTd, c0)
            ps, bps = proj(t, GOFF['mq'], 512)
            rs = head_ss(ps, bps, 4, 128)
            i = nxt('tb')
            tt('dve', tok_bf[i][:].rearrange('p (h d) -> p h d', d=128),
               ps[:].rearrange('p (h d) -> p h d', d=128),
               rs.unsqueeze(2).broadcast_to([128, 4, 128]), ALU.mult,
               reads=[bps, b_st8], writes=[b_tok_bf[i]])
            transpose_out(tok_bf[i], b_tok_bf[i], gmq[:, 0:1], b_g2, mqT_d, b_mqTd, c0)
        sc.dma('sp', lin_d.rearrange('(t p) h -> p t h', p=128), lfall[:], reads=[b_lfall],
               writes=[b_lind], sem='lfo')
        if stage == 'b':
            d1 = dscr('lf_dbg', [128, NT, 3, 8], F32, out=True)
            sc.dma('sp', d1[:, :, :, :], lfdbg[:], reads=[b_lfdbg], writes=[Buf('d1')], sem='dbg0')
            d2 = dscr('bfor_dbg', [128, 8], F32, out=True)
            sc.dma('sp', d2[:, :], bfor[:], reads=[b_bfor], writes=[Buf('d2')], sem='dbg0')
            d3 = dscr('wff_dbg', [128, 8, 8], BF16, out=True)
            sc.dma('sp', d3[:, :, :], win_sb[:, :, 3072:3080], reads=[b_win], writes=[Buf('d3')],
                   sem='dbg0')
        sc.barrier()
        sc.run()
        pes.close()

    phase_a2()
    es_a.close()
    o_sb = {n: sbg('o_' + n, [128, NT, 512], BF16) for n in 'abc'}
    b_o = {n: Buf('o_' + n) for n in 'abc'}
    if stage == 'a2':
        es.close()
        print('ninst', sc.ninst, 'max sem', max(sc.cnt.values()), max(sc.dsem.values()), 'nsem', len(sc.semobj))
        return nc, dbg

    RG = [[0, 1, 2, 3], [4, 5, 6, 7]]
    b_kall, b_lall = Buf('kall'), Buf('lall')

    def gather(name, src, rows, cols, dt, b_src, b_dst):
        g = g_d[name]
        sc.custom('pool', lambda e: e.collective_compute(
            'AllGather', ALU.bypass, replica_groups=RG, ins=[src.ap().opt()],
            outs=[g.ap().opt()]), reads=[b_src], writes=[b_dst], sem='cc')
        return g
    if stage == 'b':
        lo0 = dscr('lin_dbg0', [TOK, 8], F32, out=True)
        b_lo0 = Buf('lo0')
        sc.dma('sp', lo0[:, :], lin_d[:, :], reads=[b_lind], writes=[b_lo0], sem='dbg0')
    bar_in = dscr('bar_in', [1, 64], F32)
    bar_out = dscr('bar_out', [1, 64], F32)
    b_bar = Buf('bar')
    sc.dma('sp', bar_in[:, :], ident_d[0:1, 0:64], reads=[b_kind, b_lind], writes=[b_bar],
           sem='bar')
    sc.custom('pool', lambda e: e.collective_compute(
        'AllReduce', ALU.add, replica_groups=RG, ins=[bar_in.ap().opt()],
        outs=[bar_out.ap().opt()]), reads=[b_bar, b_kind, b_lind], writes=[b_bar, b_kind, b_lind],
        sem='cc')
    lall_d = gather('lall', lin_d, TOK, 8, F32, b_lind, b_lall)
    fkT_g = [gather('fkT_g%d' % i, fkT_in[i], 256, TOK, BF16, b_kind, b_kall) for i in range(2)]
    fvA_g = [gather('fvA_g%d' % i, fvA_in[i], TOK, 130, BF16, b_kind, b_kall) for i in range(4)]
    dikT_g = gather('dikT_g', dikT_in, 128, TOK, BF16, b_kind, b_kall)
    dvA_g = gather('dvA_g', dvA_in, TOK, 65, BF16, b_kind, b_kall)
    if stage == 'b':
        ld = dscr('lall_dbg', [4 * TOK, 8], F32, out=True)
        sc.dma('sp', ld[:, :], lall_d[:, :], reads=[b_lall], writes=[Buf('ldbg')], sem='dbg')
        lo_ = dscr('lin_dbg', [TOK, 8], F32, out=True)
        sc.dma('sp', lo_[:, :], lin_d[:, :], reads=[b_lind, b_lall], writes=[Buf('lodbg')],
               sem='dbg')
        kd = dscr('kall_dbg', [64, 2048], BF16, out=True)
        sc.dma('sp', kd[:, :], fkT_g[0][3 * 256:3 * 256 + 64, :], reads=[b_kall],
               writes=[Buf('kdbg')], sem='dbg')
        sc.barrier()
        sc.run()
        es.close()
        return nc, dbg

    def rank_of(jj):
        return jj if jj < 4 else 7 - jj

    NKI = [8 * (i // 2) + 4 if i % 2 == 0 else 8 * (i // 2) + 8 for i in range(NT)]
    BOFF = [0]
    for i in range(NT):
        BOFF.append(BOFF[-1] + NKI[i])

    def load_seq_T(q, dst, b_dst, p0, p1, gt, rows_per_rank, row0, sem):
        dv = dst[p0:p1, :].rearrange('p (m j c) -> p m j c', m=8, j=8)
        n = p1 - p0
        for r in range(4):
            src = gt[r * rows_per_rank + row0:r * rows_per_rank + row0 + n, :].rearrange(
                'p (m two c) -> p m two c', two=2, c=128)
            sc.dma(q, dv[:, :, r, :], src[:, :, 0, :], reads=[b_kall], writes=[b_dst], sem=sem,
                   nowaw=True)
            sc.dma(q, dv[:, :, 7 - r, :], src[:, :, 1, :], reads=[b_kall], writes=[b_dst], sem=sem,
                   nowaw=True)

    def load_seq_tok(q, dst, b_dst, gt, sem):
        dv = dst.rearrange('p (m j) c -> p m j c', j=8)
        for r in range(4):
            src = gt[r * TOK:(r + 1) * TOK, :].rearrange('(m two p) c -> p m two c', two=2, p=128)
            sc.dma(q, dv[:, :, r, :], src[:, :, 0, :], reads=[b_kall], writes=[b_dst], sem=sem,
                   nowaw=True)
            sc.dma(q, dv[:, :, 7 - r, :], src[:, :, 1, :], reads=[b_kall], writes=[b_dst], sem=sem,
                   nowaw=True)

    def phase_c1():
        pes = ExitStack()

        def sbf(name, shape, dt):
            return pes.enter_context(nc.sbuf_tensor(un(name), list(shape), dt))
        tri = sbf('tri_sb', [128, 128], F32)
        ones_f = sbf('ones_f', [128, 128], F32)
        kposc = sbf('kposc_sb', [128, 64], F32)
        qpos_bc = sbf('qpos_bc', [128, TOK], F32)
        b_cst = Buf('c1const')
        sc.dma('sp', tri[:], tri_d[:, :], writes=[b_cst], sem='const', nowaw=True)
        sc.dma('sp', kposc[:], kposc_d[:, :], writes=[b_cst], sem='const', nowaw=True)
        sc.dma('sp', qpos_bc[:], qposf_d.ap().partition_broadcast(128), writes=[b_cst],
               sem='const', nowaw=True)
        b_ones = Buf('ones_f')
        sc.op('dve', lambda e: e.memset(ones_f[:], 1.0), writes=[b_ones])
        L_sb = sbf('L_sb', [128, 64, 8], F32)
        b_L = Buf('L_sb')
        Lv = L_sb[:].rearrange('p (m j) h -> p m j h', j=8)
        for r in range(4):
            src = lall_d[r * TOK:(r + 1) * TOK, :].rearrange('(m two p) h -> p m two h',
                                                              two=2, p=128)
            sc.dma('sp', Lv[:, :, r, :], src[:, :, 0, :], reads=[b_lall], writes=[b_L],
                   sem='Lld', nowaw=True)
            sc.dma('sp', Lv[:, :, 7 - r, :], src[:, :, 1, :], reads=[b_lall], writes=[b_L],
                   sem='Lld', nowaw=True)
        Lf = L_sb[:].rearrange('p g h -> p (g h)')
        cinc = sbf('cinc', [128, 64, 8], F32)
        tot = sbf('tot', [128, 64, 8], F32)
        scA = sbf('scA', [128, 64, 8], F32)
        scB = sbf('scB', [128, 64, 8], F32)
        c_all = sbf('c_all', [128, 64, 8], F32)
        Tpre = sbf('Tpre', [128, 64, 8], F32)
        b_cinc, b_tot, b_scA, b_scB, b_call, b_Tpre = [Buf(n) for n in
                                                       ['cinc', 'tot', 'scA', 'scB', 'c_all', 'Tpre']]
        p1, bp1 = next_ps()
        mm(p1[:], tri[:], Lf, True, True, reads=[b_cst, b_L], writes=[bp1])
        p2, bp2 = next_ps()
        mm(p2[:], ones_f[:], Lf, True, True, reads=[b_ones, b_L], writes=[bp2])
        act(cinc[:].rearrange('p g h -> p (g h)'), p1[:], AF.Copy, reads=[bp1], writes=[b_cinc])
        act(tot[:].rearrange('p g h -> p (g h)'), p2[:], AF.Copy, reads=[bp2], writes=[b_tot])
        sc.op('dve', lambda e: e.tensor_copy(out=scA[:], in_=tot[:]), reads=[b_tot], writes=[b_scA])
        cur, bcur, oth, both = scA, b_scA, scB, b_scB
        sft = 1
        while sft < 64:
            tt('dve', oth[:, sft:, :], cur[:, sft:, :], cur[:, :64 - sft, :], ALU.add,
               reads=[bcur], writes=[both])
            sc.op('dve', lambda e, oth=oth, cur=cur, sft=sft: e.tensor_copy(
                out=oth[:, :sft, :], in_=cur[:, :sft, :]), reads=[bcur], writes=[both])
            cur, bcur, oth, both = oth, both, cur, bcur
            sft *= 2
        tt('dve', Tpre[:], cur[:], tot[:], ALU.subtract, reads=[bcur, b_tot], writes=[b_Tpre])
        tt('dve', c_all[:], cinc[:], Tpre[:], ALU.add, reads=[b_cinc, b_Tpre], writes=[b_call])
        biasT = sbf('biasT', [128, BOFF[NT], 8], F32)
        b_biasT = Buf('biasT')
        negmT = sbf('negmT', [128, NT, 4, 128], BF16)
        b_negmT = Buf('negmT')
        mcol = sbf('mcol', [128, 4], F32)
        b_mcol = Buf('mcol')
        mrep = [sbf('mrep%d' % i, [128, 128], F32) for i in range(2)] * 2
        b_mrep = [Buf('mrep%d' % i) for i in range(2)] * 2
        cref = sbf('cref', [128, 8], F32)
        b_cref = Buf('cref')
        for i in range(NT):
            nk = NKI[i]
            g0 = nk - 4
            pc, bpc = next_ps()
            qmid = qpos_bc[:, i * 128 + 64:i * 128 + 65]
            for b in range(4):
                tt('dve', mcol[:, b:b + 1], qmid, kposc[:, g0 + b:g0 + b + 1], ALU.is_ge,
                   reads=[b_cst], writes=[b_mcol])
                ts('dve', mrep[b][:], ones_f[:], mcol[:, b:b + 1], None, ALU.mult,
                   reads=[b_ones, b_mcol], writes=[b_mrep[b]])
                mm(pc[:, 0:8], mrep[b][:], L_sb[:, g0 + b, :], b == 0, b == 3,
                   reads=[b_mrep[b], b_L], writes=[bpc])
                ts('dve', negmT[:, i, b, :], qpos_bc[:, i * 128:(i + 1) * 128],
                   kposc[:, g0 + b:g0 + b + 1], -30000.0, ALU.is_lt, ALU.mult,
                   reads=[b_cst], writes=[b_negmT], nowaw=True)
            tt('dve', cref[:], pc[:, 0:8], Tpre[:, g0, :], ALU.add, reads=[bpc, b_Tpre],
               writes=[b_cref])
            tt('dve', biasT[:, BOFF[i]:BOFF[i] + nk, :],
               cref[:].unsqueeze(1).broadcast_to([128, nk, 8]), c_all[:, 0:nk, :], ALU.subtract,
               reads=[b_cref, b_call], writes=[b_biasT], nowaw=True)
            ts('dve', biasT[:, BOFF[i]:BOFF[i] + nk, :], biasT[:, BOFF[i]:BOFF[i] + nk, :], 60.0,
               None, ALU.min, reads=[b_biasT], writes=[b_biasT], nowaw=True)
        if stage == 'c1a':
            cd = dscr('call_dbg', [128, 64, 8], F32, out=True)
            sc.dma('sp', cd[:, :, :], c_all[:], reads=[b_call], writes=[Buf('cad')], sem='dbg')
            bd = dscr('bias_dbg', [128, BOFF[NT], 8], F32, out=True)
            sc.dma('sp', bd[:, :, :], biasT[:], reads=[b_biasT], writes=[Buf('bad')], sem='dbg')
            sc.barrier()
            sc.run()
            pes.close()
            return
        fqT_sb = sbf('fqT_sb', [128, 4, TOK], BF16)
        b_fqT = Buf('fqT_sb')
        sc.dma('sp', fqT_sb[:], fqT_d[:, :, :], reads=[b_fqTd], writes=[b_fqT], sem='fqld')
        KT = [sbf('KT%d' % i, [128, S], BF16) for i in range(2)]
        VA = sbf('VA', [128, 64, 130], BF16)
        b_KT = [Buf('KT%d' % i) for i in range(2)]
        b_VA = Buf('VA')
        Vp = [sbf('Vp%d' % i, [128, 64, 130], BF16) for i in range(2)]
        b_Vp = [Buf('Vp%d' % i) for i in range(2)]
        wexp = [sbf('wexp%d' % i, [128, 64, 2], F32) for i in range(2)]
        b_wexp = [Buf('wexp%d' % i) for i in range(2)]
        PT = [sbf('PT%d' % i, [128, 512], BF16) for i in range(2)]
        b_PT = [Buf('PT%d' % i) for i in range(2)]
        rcp = sbf('rcp', [128, 2], F32)
        b_rcp = Buf('rcp')
        cnt = {'s': 0, 'o': 0, 'p': 0, 'v': 0}
        for hp in range(4):
            kb = hp % 2
            load_seq_T('sp', KT[kb], b_KT[kb], 0, 128, g_d['fkT_g%d' % (hp // 2)], 256,
                       (hp % 2) * 128, 'KT%d' % kb)
            load_seq_tok('sp', VA[:], b_VA, g_d['fvA_g%d' % hp], 'VA')
            for i in range(NT):
                nk = NKI[i]
                vi = cnt['v'] % 2
                cnt['v'] += 1
                act(wexp[vi][:, 0:nk, :], biasT[:, BOFF[i]:BOFF[i] + nk, 2 * hp:2 * hp + 2], AF.Exp,
                    reads=[b_biasT], writes=[b_wexp[vi]])
                tt('dve', Vp[vi][:, 0:nk, :].rearrange('p k (e c) -> p k e c', e=2),
                   VA[:, 0:nk, :].rearrange('p k (e c) -> p k e c', e=2),
                   wexp[vi][:, 0:nk, :].unsqueeze(3).broadcast_to([128, nk, 2, 65]), ALU.mult,
                   reads=[b_VA, b_wexp[vi]], writes=[b_Vp[vi]])
                ob = 4 + 2 * (cnt['o'] % 2)
                cnt['o'] += 1
                for e in range(2):
                    h = 2 * hp + e
                    pO, bpO = psum[ob + e], psb[ob + e]
                    for g4 in range(nk // 4):
                        sbk = cnt['s'] % 4
                        cnt['s'] += 1
                        pS, bpS = psum[sbk], psb[sbk]
                        last = (g4 == nk // 4 - 1)
                        if last:
                            mm(pS[:], ident_b[:], negmT[:, i, :, :].rearrange('p a b -> p (a b)'),
                               True, False, reads=[b_ident_b, b_negmT], writes=[bpS])
                        for kk in range(4):
                            kt = g4 * 4 + kk
                            mm(pS[:, kk * 128:(kk + 1) * 128],
                               KT[kb][e * 64:(e + 1) * 64, kt * 128:(kt + 1) * 128],
                               fqT_sb[e * 64:(e + 1) * 64, hp, i * 128:(i + 1) * 128],
                               not last, (not last) or kk == 3, reads=[b_KT[kb], b_fqT],
                               writes=[bpS], skip_group_check=True)
                        pi = cnt['p'] % 2
                        cnt['p'] += 1
                        act(PT[pi][:], pS[:], AF.Exp, reads=[bpS], writes=[b_PT[pi]], scale=0.125)
                        for kk in range(4):
                            kt = g4 * 4 + kk
                            mm(pO[:, 0:65], PT[pi][:, kk * 128:(kk + 1) * 128],
                               Vp[vi][:, kt, e * 65:(e + 1) * 65], kt == 0, kt == nk - 1,
                               reads=[b_PT[pi], b_Vp[vi]], writes=[bpO])
                    sc.op('dve', lambda e_, pO=pO, e=e: e_.reciprocal(out=rcp[:, e:e + 1],
                                                                      in_=pO[:, 64:65]),
                          reads=[bpO], writes=[b_rcp])
                    ts('dve', o_sb['a'][:, i, h * 64:(h + 1) * 64], pO[:, 0:64], rcp[:, e:e + 1],
                       None, ALU.mult, reads=[bpO, b_rcp], writes=[b_o['a']], nowaw=True)
        if stage == 'c1':
            od = dscr('oa_dbg', [128, NT, 512], BF16, out=True)
            sc.dma('sp', od[:, :, :], o_sb['a'][:], reads=[b_o['a']], writes=[Buf('oad')],
                   sem='dbg')
            cd = dscr('call_dbg', [128, 64, 8], F32, out=True)
            sc.dma('sp', cd[:, :, :], c_all[:], reads=[b_call], writes=[Buf('cad')], sem='dbg')
        sc.barrier()
        sc.run()
        pes.close()

    def phase_c3():
        pes = ExitStack()

        def sbf(name, shape, dt):
            return pes.enter_context(nc.sbuf_tensor(un(name), list(shape), dt))
        norm_transpose = make_norm_transpose(sbf)
        gTm, b_gTm = load_gain_T(sbf, 'gT_mem', v_d['mem_norm'], 8)
        gmk = sbf('gmk', [128, 1], F32)
        b_gmk = Buf('gmk')
        sc.dma('sp', gmk[:], v_d['mem_k_norm'].rearrange('(p o) -> p o', o=1), writes=[b_gmk],
               sem='const')
        wkv = sbf('wkv', [128, 8, D], BF16)
        b_wkv = Buf('wkv')
        sc.dma('sp', wkv[:], wb_d['w_mem_kv'].rearrange('(kc p) f -> p kc f', p=128),
               reads=[wb_buf['w_mem_kv']], writes=[b_wkv], sem='wkv')
        memT = sbf('memT', [128, 8, 256], BF16)
        b_memT = Buf('memT')
        mt_sb = [sbf('memt%d' % i, [128, D], F32) for i in range(2)]
        b_mt = [Buf('memt%d' % i) for i in range(2)]
        for m in range(2):
            sc.dma('sp', mt_sb[m][:], mem_d[m * 128:(m + 1) * 128, :], writes=[b_mt[m]],
                   sem='memld')
            norm_transpose(mt_sb[m][:], b_mt[m], gTm, b_gTm, memT, b_memT, m * 128)
        kmT = sbf('kmT', [128, 4, 256], BF16)
        b_kmT = Buf('kmT')
        vma = sbf('vma', [128, 2, 4, 129], BF16)
        b_vma = Buf('vma')
        sc.op('dve', lambda e: e.memset(vma[:], 1.0), writes=[b_vma])
        sqf = sbf('sqf3', [128, 512], F32)
        b_sqf = Buf('sqf3')
        st8 = sbf('st83', [128, 16], F32)
        b_st8 = Buf('st83')
        kmb = sbf('kmb', [128, 512], BF16)
        b_kmb = Buf('kmb')
        for m in range(2):
            ps, bps = next_ps()
            for kc in range(8):
                mm(ps[:], memT[:, kc, m * 128:(m + 1) * 128], wkv[:, kc, 0:512], kc == 0, kc == 7,
                   reads=[b_memT, b_wkv], writes=[bps])
            act(sqf[:], ps[:], AF.Square, reads=[bps], writes=[b_sqf])
            sc.op('dve', lambda e: e.tensor_reduce(
                out=st8[:, 0:4], in_=sqf[:].rearrange('p (h d) -> p h d', d=128), axis=AX.X,
                op=ALU.add), reads=[b_sqf], writes=[b_st8])
            rs = rstd_from_ss(st8, b_st8, 4, 128)
            tt('dve', kmb[:].rearrange('p (h d) -> p h d', d=128),
               ps[:].rearrange('p (h d) -> p h d', d=128),
               rs.unsqueeze(2).broadcast_to([128, 4, 128]), ALU.mult, reads=[bps, b_st8],
               writes=[b_kmb])
            pt_, bpt = next_ps()
            ptv = pt_[:].bitcast(BF16)
            for h in range(4):
                tr(ptv[:, h * 128:(h + 1) * 128], kmb[:, h * 128:(h + 1) * 128], ident_b[:],
                   reads=[b_kmb, b_ident_b], writes=[bpt])
            ts('dve', kmT[:, :, m * 128:(m + 1) * 128],
               ptv[:, 0:512].rearrange('p (h k) -> p h k', k=128), gmk[:, 0:1], None, ALU.mult,
               reads=[bpt, b_gmk], writes=[b_kmT], nowaw=True)
            ps2, bps2 = next_ps()
            for kc in range(8):
                mm(ps2[:], memT[:, kc, m * 128:(m + 1) * 128], wkv[:, kc, 512:1024], kc == 0,
                   kc == 7, reads=[b_memT, b_wkv], writes=[bps2])
            act(vma[:, m, :, 0:128], ps2[:].rearrange('p (h d) -> p h d', d=128), AF.Copy,
                reads=[bps2], writes=[b_vma], nowaw=True)
        mqT_sb = sbf('mqT_sb', [128, 4, TOK], BF16)
        b_mqT = Buf('mqT_sb')
        sc.dma('sp', mqT_sb[:], mqT_d[:, :, :], reads=[b_mqTd], writes=[b_mqT], sem='mqld')
        zer = sbf('zer3', [128, 264], BF16)
        b_zer = Buf('zer3')
        sc.op('dve', lambda e: e.memset(zer[:], 0.0), writes=[b_zer])
        PTm = [sbf('PTm%d' % i, [128, 512], BF16) for i in range(2)]
        b_PTm = [Buf('PTm%d' % i) for i in range(2)]
        rc4 = sbf('rc43', [128, 4], F32)
        b_rc4 = Buf('rc43')
        cnt = {'o': 0, 'p': 0}
        for i in range(NT):
            ob = 4 + 2 * (cnt['o'] % 2)
            cnt['o'] += 1
            for half in range(2):
                pO, bpO = psum[ob + half], psb[ob + half]
                mm(pO[:, 0:264], zer[:, 0:128], zer[:, 0:264], True, False, reads=[b_zer],
                   writes=[bpO])
                pS, bpS = psum[half], psb[half]
                for hh in range(2):
                    h = 2 * half + hh
                    for m in range(2):
                        mm(pS[:, (hh * 2 + m) * 128:(hh * 2 + m + 1) * 128],
                           kmT[:, h, m * 128:(m + 1) * 128], mqT_sb[:, h, i * 128:(i + 1) * 128],
                           True, True, reads=[b_kmT, b_mqT], writes=[bpS], skip_group_check=True)
                pi = cnt['p'] % 2
                cnt['p'] += 1
                act(PTm[pi][:], pS[:], AF.Exp, reads=[bpS], writes=[b_PTm[pi]],
                    scale=float(128 ** -0.5))
                for hh in range(2):
                    h = 2 * half + hh
                    for m in range(2):
                        mm(pO[:, hh * 132:hh * 132 + 129],
                           PTm[pi][:, (hh * 2 + m) * 128:(hh * 2 + m + 1) * 128], vma[:, m, h, :],
                           False, m == 1, reads=[b_PTm[pi], b_vma], writes=[bpO],
                           skip_group_check=True)
                pv = pO[:, 0:264].rearrange('p (h c) -> p h c', c=132)
                sc.op('dve', lambda e, pv=pv, half=half: e.reciprocal(
                    out=rc4[:, half * 2:(half + 1) * 2], in_=pv[:, :, 128]), reads=[bpO],
                    writes=[b_rc4])
                tt('dve', o_sb['c'][:, i, half * 256:(half + 1) * 256].rearrange(
                    'p (h d) -> p h d', d=128), pv[:, :, 0:128],
                   rc4[:, half * 2:(half + 1) * 2].unsqueeze(2).broadcast_to([128, 2, 128]),
                   ALU.mult, reads=[bpO, b_rc4], writes=[b_o['c']], nowaw=True)
        sc.barrier()
        sc.run()
        pes.close()

    phase_c3()
    if stage == 'c3':
        od = dscr('oc_dbg', [128, NT, 512], BF16, out=True)
        sc.dma('sp', od[:, :, :], o_sb['c'][:], reads=[b_o['c']], writes=[Buf('ocd')], sem='dbg')
        od3 = dscr('ob_dbg', [128, NT, 512], BF16, out=True)
        sc.dma('sp', od3[:, :, :], o_sb['b'][:], reads=[b_o['b']], writes=[Buf('obd')], sem='dbg')
        od2 = dscr('oa_dbg', [128, NT, 512], BF16, out=True)
        sc.dma('sp', od2[:, :, :], o_sb['a'][:], reads=[b_o['a']], writes=[Buf('oad')], sem='dbg')
        sc.barrier()
        sc.run()
        es.close()
        return nc, dbg

    phase_c1()
    if stage in ('c1', 'c1a'):
        es.close()
        print('ninst', sc.ninst, 'max sem', max(sc.cnt.values()), max(sc.dsem.values()), 'nsem', len(sc.semobj))
        return nc, dbg

    NBIS = 25

    def phase_c2():
        pes = ExitStack()

        def sbf(name, shape, dt):
            return pes.enter_context(nc.sbuf_tensor(un(name), list(shape), dt))
        b_cst = Buf('c2const')
        iota = sbf('iota_sb', [128, 512], F32)
        sc.dma('sp', iota[:], iota512_d.ap().partition_broadcast(128), writes=[b_cst],
               sem='const', nowaw=True)
        pow2 = sbf('pow2_sb', [128, 32], F32)
        sc.dma('sp', pow2[:], pow2_d.ap().partition_broadcast(128), writes=[b_cst], sem='const',
               nowaw=True)
        qcol = sbf('qcol', [128, NT], F32)
        sc.dma('sp', qcol[:], qposf_d.rearrange('(t p) -> p t', p=128), writes=[b_cst],
               sem='const', nowaw=True, allow_slow_non_contiguous=True)
        ident4 = sbf('ident4', [128, 4, 128], BF16)
        b_id4 = Buf('ident4')
        for k4 in range(4):
            sc.op('dve', lambda e, k4=k4: e.tensor_copy(out=ident4[:, k4, :], in_=ident_b[:]),
                  reads=[b_ident_b], writes=[b_id4], nowaw=True)
        zer = sbf('zer', [128, 272], BF16)
        b_zer = Buf('zer')
        sc.op('dve', lambda e: e.memset(zer[:], 0.0), writes=[b_zer])
        dkT2 = sbf('dkT2', [128, S], BF16)
        ikT2 = sbf('ikT2', [128, S], BF16)
        dva = sbf('dva', [128, 64, 65], BF16)
        b_dk, b_ik, b_dva = Buf('dkT2'), Buf('ikT2'), Buf('dva')
        for hh in range(2):
            load_seq_T('sp', dkT2, b_dk, hh * 64, hh * 64 + 64, g_d['dikT_g'], 128, 0, 'dkld')
            load_seq_T('sp', ikT2, b_ik, hh * 64, hh * 64 + 64, g_d['dikT_g'], 128, 64, 'ikld')
        load_seq_tok('sp', dva[:], b_dva, g_d['dvA_g'], 'dvld')
        dqT_sb = sbf('dqT_sb', [128, 4, TOK], BF16)
        iqT_sb = sbf('iqT_sb', [128, 4, TOK], BF16)
        b_dqT, b_iqT = Buf('dqT_sb'), Buf('iqT_sb')
        sc.dma('sp', dqT_sb[:], dqT_d[:, :, :], reads=[b_dqTd], writes=[b_dqT], sem='dqld')
        sc.dma('sp', iqT_sb[:], iqT_d[:, :, :], reads=[b_iqTd], writes=[b_iqT], sem='iqld')
        score = sbf('score', [128, S], F32)
        negm2 = [sbf('negm%d' % i_, [128, S], BF16) for i_ in range(2)]
        b_negm2 = [Buf('negm%d' % i_) for i_ in range(2)]
        b_score = Buf('score')
        PTd = [sbf('PTd%d' % i, [128, 512], BF16) for i in range(3)]
        b_PTd = [Buf('PTd%d' % i) for i in range(3)]
        sm = sbf('sm', [128, 8], F32)
        b_sm = Buf('sm')
        steps = sbf('steps', [128, 32], F32)
        b_steps = Buf('steps')
        cneg = sbf('cneg', [128, 512], F32)
        b_cneg = Buf('cneg')
        rc4 = sbf('rc4', [128, 8], F32)
        b_rc4 = Buf('rc4')
        cnt = {'s': 0, 'p': 0, 'o': 0}

        def sbank():
            i = cnt['s'] % 4
            cnt['s'] += 1
            return psum[i], psb[i]
        def stage1(i):
            nk = NKI[i]
            n = nk * 128
            negm, b_negm = negm2[i % 2], b_negm2[i % 2]
            for c4 in range(nk // 4):
                for h in range(8):
                    e_, hp = h % 2, h // 2
                    ps, bps = sbank()
                    mm(ps[:], iqT_sb[e_ * 64:(e_ + 1) * 64, hp, i * 128:(i + 1) * 128],
                       ikT2[e_ * 64:(e_ + 1) * 64, c4 * 512:(c4 + 1) * 512], True, True,
                       reads=[b_iqT, b_ik], writes=[bps])
                    act(ps[:], ps[:], AF.Relu, reads=[bps], writes=[bps])
                    if h == 0:
                        ts('dve', score[:, c4 * 512:(c4 + 1) * 512], ps[:], isign[:, i, 0:1], None,
                           ALU.mult, reads=[bps, b_isign], writes=[b_score])
                    else:
                        sc.op('dve', lambda e, ps=ps, c4=c4, h=h, i=i: e.scalar_tensor_tensor(
                            out=score[:, c4 * 512:(c4 + 1) * 512], in0=ps[:],
                            scalar=isign[:, i, h:h + 1], in1=score[:, c4 * 512:(c4 + 1) * 512],
                            op0=ALU.mult, op1=ALU.add), reads=[bps, b_isign, b_score],
                            writes=[b_score])
            sc.op('dve', lambda e, n=n: e.tensor_reduce(out=sm[:, 0:1], in_=score[:, 0:n], axis=AX.X,
                                                       op=ALU.max, apply_absolute_value=True),
                  reads=[b_score], writes=[b_sm])
            ts('dve', sm[:, 1:2], sm[:, 0:1], -1.0, -1e-3, ALU.mult, ALU.add, reads=[b_sm],
               writes=[b_sm])
            ts('dve', sm[:, 2:3], sm[:, 0:1], 2.0, 2e-3, ALU.mult, ALU.add, reads=[b_sm],
               writes=[b_sm])
            ts('dve', steps[:], pow2[:], sm[:, 2:3], None, ALU.mult, reads=[b_sm, b_cst],
               writes=[b_steps])
            ts('dve', sm[:, 6:7], qcol[:, i:i + 1], float(-(n - 512)), None, ALU.add,
               reads=[b_cst], writes=[b_sm])
            ts('dve', cneg[:], iota[:], sm[:, 6:7], -1e9, ALU.is_gt, ALU.mult, reads=[b_cst, b_sm],
               writes=[b_cneg])
            tt('dve', score[:, n - 512:n], score[:, n - 512:n], cneg[:], ALU.add,
               reads=[b_score, b_cneg], writes=[b_score])
            tt('dve', sm[:, 3:4], sm[:, 1:2], steps[:, 0:1], ALU.add, reads=[b_sm, b_steps],
               writes=[b_sm])
            for k in range(NBIS):
                sc.op('dve', lambda e, n=n: e.tensor_scalar(
                    out=negm[:, 0:n], in0=score[:, 0:n], scalar1=sm[:, 3:4], scalar2=None,
                    op0=ALU.is_ge, op1=ALU.add, accum_out=sm[:, 4:5]),
                    reads=[b_score, b_sm], writes=[b_negm, b_sm])
                ts('dve', sm[:, 5:6], sm[:, 4:5], 255.5, steps[:, k:k + 1], ALU.is_ge, ALU.mult,
                   reads=[b_sm, b_steps], writes=[b_sm])
                sc.op('dve', lambda e, k=k: e.scalar_tensor_tensor(
                    out=sm[:, 3:4], in0=sm[:, 5:6], scalar=steps[:, k + 1:k + 2], in1=sm[:, 3:4],
                    op0=ALU.subtract, op1=ALU.add), reads=[b_sm, b_steps], writes=[b_sm])
            tt('dve', sm[:, 1:2], sm[:, 3:4], steps[:, NBIS:NBIS + 1], ALU.subtract,
               reads=[b_sm, b_steps], writes=[b_sm])
            ts('dve', negm[:, 0:n], score[:, 0:n], sm[:, 1:2], -30000.0, ALU.is_lt, ALU.mult,
               reads=[b_score, b_sm], writes=[b_negm])
        def stage2(i):
            nk = NKI[i]
            negm, b_negm = negm2[i % 2], b_negm2[i % 2]
            ob = 4 + 2 * (cnt['o'] % 2)
            cnt['o'] += 1
            for half in range(2):
                mm(psum[ob + half][:, 0:272], zer[:, 0:128], zer[:, 0:272], True, False,
                   reads=[b_zer], writes=[psb[ob + half]])
            for kt in range(nk):
                for half in range(2):
                    pS, bpS = sbank()
                    pO, bpO = psum[ob + half], psb[ob + half]
                    mm(pS[:], negm[:, kt * 128:(kt + 1) * 128],
                       ident4[:].rearrange('p a b -> p (a b)'), True, False,
                       reads=[b_negm, b_id4], writes=[bpS])
                    for hh in range(4):
                        h = 2 * hh + half
                        e_, hp = h % 2, h // 2
                        mm(pS[:, hh * 128:(hh + 1) * 128],
                           dkT2[e_ * 64:(e_ + 1) * 64, kt * 128:(kt + 1) * 128],
                           dqT_sb[e_ * 64:(e_ + 1) * 64, hp, i * 128:(i + 1) * 128], False, hh == 3,
                           reads=[b_dk, b_dqT], writes=[bpS])
                    pi = cnt['p'] % 3
                    cnt['p'] += 1
                    act(PTd[pi][:], pS[:], AF.Exp, reads=[bpS], writes=[b_PTd[pi]], scale=0.125)
                    for hh in range(4):
                        mm(pO[:, hh * 68:hh * 68 + 65], PTd[pi][:, hh * 128:(hh + 1) * 128],
                           dva[:, kt, :], False, kt == nk - 1, reads=[b_PTd[pi], b_dva],
                           writes=[bpO], skip_group_check=True)
            for half in range(2):
                pO, bpO = psum[ob + half], psb[ob + half]
                pv = pO[:, 0:272].rearrange('p (h c) -> p h c', c=68)
                sc.op('dve', lambda e, pv=pv, half=half: e.reciprocal(
                    out=rc4[:, half * 4:(half + 1) * 4], in_=pv[:, :, 64]), reads=[bpO],
                    writes=[b_rc4])
                tt('dve', o_sb['b'][:, i, :].rearrange(
                    'p (hh par d) -> p hh par d', par=2, d=64)[:, :, half, :], pv[:, :, 0:64],
                   rc4[:, half * 4:(half + 1) * 4].unsqueeze(2).broadcast_to([128, 4, 64]),
                   ALU.mult, reads=[bpO, b_rc4], writes=[b_o['b']], nowaw=True)
        for i in range(NT):
            stage1(i)
            if i >= 1:
                stage2(i - 1)
        stage2(NT - 1)
        sc.barrier()
        sc.run()
        pes.close()

    phase_c2()
    if stage == 'c2':
        od = dscr('ob_dbg', [128, NT, 512], BF16, out=True)
        sc.dma('sp', od[:, :, :], o_sb['b'][:], reads=[b_o['b']], writes=[Buf('obd')], sem='dbg')
        od2 = dscr('oa_dbg', [128, NT, 512], BF16, out=True)
        sc.dma('sp', od2[:, :, :], o_sb['a'][:], reads=[b_o['a']], writes=[Buf('oad')], sem='dbg')
        sc.barrier()
        sc.run()
        es.close()
        return nc, dbg

    def phase_d():
        pes = ExitStack()

        def sbf(name, shape, dt):
            return pes.enter_context(nc.sbuf_tensor(un(name), list(shape), dt))
        norm_transpose = make_norm_transpose(sbf)
        gTx, b_gTx = load_gain_T(sbf, 'gT_mix2', v_d['mix_norm'], 8)
        wing = sbf('wing', [128, 8, 3072], BF16)
        wbr = sbf('wbr', [128, 12, D], BF16)
        wout = sbf('wout', [128, 8, D], BF16)
        b_wing, b_wbr, b_wout = Buf('wing'), Buf('wbr'), Buf('wout')
        for n3 in range(3):
            sc.dma('sp', wing[:, :, n3 * 1024:(n3 + 1) * 1024],
                   wb_d['w_in'][:, 3280 + n3 * 1024:3280 + (n3 + 1) * 1024].rearrange(
                       '(kc p) f -> p kc f', p=128), reads=[wb_buf['w_in']], writes=[b_wing],
                   sem='wing', nowaw=True)
        sc.dma('sp', wbr[:], wb_d['w_branch'].rearrange('(c p) d -> p c d', p=128),
               reads=[wb_buf['w_branch']], writes=[b_wbr], sem='wbr')
        sc.dma('sp', wout[:], wb_d['w_out'].rearrange('(c p) d -> p c d', p=128),
               reads=[wb_buf['w_out']], writes=[b_wout], sem='wout')
        uTb = sbf('uTb', [128, 8, 512], BF16)
        b_uTb = Buf('uTb')
        oT = {n_: sbf('oT_' + n_, [128, 4, 512], BF16) for n_ in 'abc'}
        b_oT = {n_: Buf('oT_' + n_) for n_ in 'abc'}
        mergedT = sbf('mergedT', [128, 8, 512], BF16)
        b_mg = Buf('mergedT')
        h1t = [sbf('h1t%d' % i, [128, D], F32) for i in range(4)]
        b_h1t = [Buf('h1t%d' % i) for i in range(4)]
        h2t = [sbf('h2t%d' % i, [128, D], F32) for i in range(2)]
        b_h2t = [Buf('h2t%d' % i) for i in range(2)]
        gs = [sbf('gs%d' % i, [128, 512], BF16) for i in range(2)]
        b_gs = [Buf('gs%d' % i) for i in range(2)]
        acc = sbf('acc', [128, 512], F32)
        tmpm = sbf('tmpm', [128, 512], F32)
        b_acc, b_tmpm = Buf('acc'), Buf('tmpm')
        cnt = {'g': 0, 'h2': 0}
        for blk in range(4):
            for tl in range(4):
                t = blk * 4 + tl
                sc.dma('sp', h1t[tl][:], h1_d[t * 128:(t + 1) * 128, :], reads=[b_h1d],
                       writes=[b_h1t[tl]], sem='h1t%d' % tl)
                norm_transpose(h1t[tl][:], b_h1t[tl], gTx, b_gTx, uTb, b_uTb, tl * 128)
                for n_ in 'abc':
                    ps, bps = next_ps()
                    psv = ps[:].bitcast(BF16)
                    for wc in range(4):
                        tr(psv[:, wc * 128:(wc + 1) * 128], o_sb[n_][:, t, wc * 128:(wc + 1) * 128],
                           ident_b[:], reads=[b_o[n_], b_ident_b], writes=[bps])
                    act(oT[n_][:, :, tl * 128:(tl + 1) * 128],
                        psv[:, 0:512].rearrange('p (w k) -> p w k', k=128), AF.Copy, reads=[bps],
                        writes=[b_oT[n_]], nowaw=True)
            for dc in range(8):
                for n3, n_ in enumerate('abc'):
                    pg, bpg = next_ps()
                    for kc in range(8):
                        mm(pg[:], wing[:, kc, n3 * 1024 + dc * 128:n3 * 1024 + (dc + 1) * 128],
                           uTb[:, kc, :], kc == 0, kc == 7, reads=[b_wing, b_uTb], writes=[bpg])
                    gi = cnt['g'] % 2
                    cnt['g'] += 1
                    act(gs[gi][:], pg[:], AF.Sigmoid, reads=[bpg], writes=[b_gs[gi]])
                    pp, bpp = next_ps()
                    for wc in range(4):
                        mm(pp[:], wbr[:, n3 * 4 + wc, dc * 128:(dc + 1) * 128], oT[n_][:, wc, :],
                           wc == 0, wc == 3, reads=[b_wbr, b_oT[n_]], writes=[bpp])
                    if n3 == 0:
                        tt('dve', acc[:], pp[:], gs[gi][:], ALU.mult, reads=[bpp, b_gs[gi]],
                           writes=[b_acc])
                    else:
                        tt('dve', tmpm[:], pp[:], gs[gi][:], ALU.mult, reads=[bpp, b_gs[gi]],
                           writes=[b_tmpm])
                        if n3 == 1:
                            tt('dve', acc[:], acc[:], tmpm[:], ALU.add, reads=[b_acc, b_tmpm],
                               writes=[b_acc])
                        else:
                            tt('dve', mergedT[:, dc, :], acc[:], tmpm[:], ALU.add,
                               reads=[b_acc, b_tmpm], writes=[b_mg], nowaw=True)
            for tl in range(4):
                t = blk * 4 + tl
                hi = cnt['h2'] % 2
                cnt['h2'] += 1
                for dh in range(2):
                    po, bpo = next_ps()
                    for dc in range(8):
                        mm(po[:], mergedT[:, dc, tl * 128:(tl + 1) * 128],
                           wout[:, dc, dh * 512:(dh + 1) * 512], dc == 0, dc == 7,
                           reads=[b_mg, b_wout], writes=[bpo])
                    tt('dve', h2t[hi][:, dh * 512:(dh + 1) * 512], po[:],
                       h1t[tl][:, dh * 512:(dh + 1) * 512], ALU.add, reads=[bpo, b_h1t[tl]],
                       writes=[b_h2t[hi]])
                sc.dma('sp', h2_d[t * 128:(t + 1) * 128, :], h2t[hi][:], reads=[b_h2t[hi]],
                       writes=[b_h2d], sem='h2o%d' % hi, nowaw=True)
        sc.barrier()
        sc.run()
        pes.close()

    phase_d()
    ffn_phase(h2_d, b_h2d, y_d, b_yd, 'ffn2_norm', 'ffn2_w_gate', 'ffn2_w_up', 'ffn2_w_down')
    es.close()
    print('ninst', sc.ninst, 'max sem', max(sc.cnt.values()), max(sc.dsem.values()), 'nsem', len(sc.semobj))
    return nc, dbg


_NC_CACHE = {}


def _core_rows(cid):
    b, j = divmod(cid, 4)
    tiles = zig_tiles(j)
    rows = np.concatenate([np.arange(g * 128, (g + 1) * 128) for g in tiles])
    return b, rows


def kernel(**inputs):
    stage = inputs.pop('_stage', 'full')
    if stage not in _NC_CACHE:
        _NC_CACHE[stage] = build(stage)
    nc, dbg = _NC_CACHE[stage]
    f = lambda k: np.asarray(inputs[k], dtype=np.float32)
    x = f('x')
    mem = f('mem')
    pos = np.asarray(inputs['positions']).astype(np.int32)
    ident = np.eye(128, dtype=np.float32)
    invf = (10000.0 ** (-np.arange(0, 64, 2, dtype=np.float32) / 64)).astype(np.float32)
    tri = np.triu(np.ones((128, 128), np.float32))
    kposc = (np.arange(64, dtype=np.float32)[None, :] * 128 + np.arange(128, dtype=np.float32)[:, None])
    pow2 = (0.5 ** np.arange(1, 33)).astype(np.float32)
    shared = {'ident': ident, 'invf': invf, 'tri': tri, 'kposc': np.ascontiguousarray(kposc),
              'pow2': pow2, 'iota512': np.arange(512, dtype=np.float32)}
    for k, shp in WNAMES.items():
        shared[k] = np.ascontiguousarray(f(k)[0].reshape(shp))
    for k in VNAMES:
        shared[k] = np.ascontiguousarray(f(k)[0])
    in_maps = []
    for cid in range(NCORES):
        b, rows = _core_rows(cid)
        m = dict(shared)
        m['x'] = np.ascontiguousarray(x[b][rows])
        m['pos'] = np.ascontiguousarray(pos[b][rows])
        m['qposf'] = rows.astype(np.float32)
        m['mem'] = np.ascontiguousarray(mem[b])
        in_maps.append(m)
    res = run_bass_kernel_spmd(nc, in_maps, core_ids=list(range(NCORES)))
    if stage != 'full':
        return res.results
    out = np.zeros((2, S, D), np.float32)
    for cid in range(NCORES):
        b, rows = _core_rows(cid)
        out[b][rows] = res.results[cid]['y']
    return out
```
